# Optimizing a Trainium2 kernel written in Bass

```python
import math
import jax, jax.numpy as jnp
from jax import lax
import numpy as np

D_MODEL = 1024
BATCH = 16
SEQ = 2048
DEPTH = 2

CHUNK = 64
N_META = 16
SSM_WIDTH = D_MODEL // 2
SSM_GROUP = 16
N_SSM_GROUPS = SSM_WIDTH // SSM_GROUP
SSM_STATE = 64
HEAD_DIM = 64
ATTN_HEADS = (D_MODEL // 2) // HEAD_DIM
ATTN_WIDTH = ATTN_HEADS * HEAD_DIM
Q_BLOCK = 128
D_FF = 2816
N_BRANCH = 2
RMS_EPS = 1e-6
DT_MIN, DT_MAX = 1e-3, 1e-1
IN_WIDTH = SSM_WIDTH + 3 * ATTN_WIDTH + ATTN_HEADS + N_BRANCH * D_MODEL

kernel_name = "hybrid_s5_fox_macaron_meta"


def rmsnorm(x, g):
    xf = x.astype(jnp.float32)
    y = xf * lax.rsqrt(jnp.mean(xf * xf, axis=-1, keepdims=True) + RMS_EPS)
    return (y * g.astype(jnp.float32)).astype(x.dtype)


def swiglu(h, w_gate, w_up, w_down):
    return (jax.nn.silu(h @ w_gate) * (h @ w_up)) @ w_down


def _complex_combine(left, right):
    a1r, a1i, b1r, b1i = left
    a2r, a2i, b2r, b2i = right
    return (a2r * a1r - a2i * a1i,
            a2r * a1i + a2i * a1r,
            a2r * b1r - a2i * b1i + b2r,
            a2r * b1i + a2i * b1r + b2i)


def s5_mixer(u, a_re, a_im, log_dt, b_re, b_im, c_re, c_im, d_skip, w_glu):
    f32 = jnp.float32
    bsz, L, _ = u.shape
    ug = u.astype(f32).reshape(bsz, L, N_SSM_GROUPS, SSM_GROUP)
    dt = jnp.exp(log_dt.astype(f32))[:, None]
    lam_re = a_re.astype(f32)
    lam_im = a_im.astype(f32)
    mag = jnp.exp(lam_re * dt)
    ab_re = mag * jnp.cos(lam_im * dt)
    ab_im = mag * jnp.sin(lam_im * dt)
    den = lam_re * lam_re + lam_im * lam_im
    n_re = ab_re - 1.0
    coef_re = (n_re * lam_re + ab_im * lam_im) / den
    coef_im = (ab_im * lam_re - n_re * lam_im) / den
    bu_re = jnp.einsum('blgc,gpc->blgp', ug, b_re.astype(f32))
    bu_im = jnp.einsum('blgc,gpc->blgp', ug, b_im.astype(f32))
    bb_re = coef_re * bu_re - coef_im * bu_im
    bb_im = coef_re * bu_im + coef_im * bu_re
    a_r = jnp.broadcast_to(ab_re, bb_re.shape)
    a_i = jnp.broadcast_to(ab_im, bb_im.shape)
    _, _, h_re, h_im = lax.associative_scan(_complex_combine, (a_r, a_i, bb_re, bb_im), axis=1)
    y = (jnp.einsum('blgp,gcp->blgc', h_re, c_re.astype(f32))
         - jnp.einsum('blgp,gcp->blgc', h_im, c_im.astype(f32))
         + d_skip.astype(f32).reshape(N_SSM_GROUPS, SSM_GROUP) * ug)
    y = jax.nn.gelu(y.reshape(bsz, L, SSM_WIDTH)).astype(u.dtype)
    z = y @ w_glu
    return z[..., :SSM_WIDTH] * jax.nn.sigmoid(z[..., SSM_WIDTH:])


def forgetting_attention(q, k, v, f_logit):
    L = q.shape[1]
    log_f = jax.nn.log_sigmoid(f_logit.astype(jnp.float32))
    cum = jnp.cumsum(log_f, axis=1).transpose(0, 2, 1)
    scale = HEAD_DIM ** -0.5
    outs = []
    for start in range(0, L, Q_BLOCK):
        end = min(start + Q_BLOCK, L)
        qb = q[:, start:end]
        kb = k[:, :end]
        vb = v[:, :end]
        s = jnp.einsum('bqhd,bkhd->bhqk', qb, kb).astype(jnp.float32) * scale
        s = s + cum[:, :, start:end, None] - cum[:, :, None, :end]
        causal = jnp.arange(end)[None, :] <= jnp.arange(start, end)[:, None]
        s = jnp.where(causal, s, -jnp.inf)
        p = jax.nn.softmax(s, axis=-1)
        outs.append(jnp.einsum('bhqk,bkhd->bqhd', p.astype(v.dtype), vb))
    return jnp.concatenate(outs, axis=1)


def setup_inputs(seed: int = 0) -> dict:
    key = jax.random.key(seed)
    ks = iter(jax.random.split(key, 40))
    f32 = jnp.float32

    def nrm(shape, scale):
        return jax.random.normal(next(ks), shape, f32) * scale

    def gain(shape):
        return 1.0 + nrm(shape, 0.02)

    G, P, C = N_SSM_GROUPS, SSM_STATE, SSM_GROUP
    x = jax.random.normal(next(ks), (BATCH, SEQ, D_MODEL), f32)
    meta = nrm((N_META, D_MODEL), 1.0)
    g_ffn1 = gain((DEPTH, D_MODEL))
    w1_gate = nrm((DEPTH, D_MODEL, D_FF), D_MODEL ** -0.5)
    w1_up = nrm((DEPTH, D_MODEL, D_FF), D_MODEL ** -0.5)
    w1_down = nrm((DEPTH, D_FF, D_MODEL), D_FF ** -0.5)
    g_mix = gain((DEPTH, D_MODEL))
    w_in = nrm((DEPTH, D_MODEL, IN_WIDTH), D_MODEL ** -0.5)
    b_gate = nrm((DEPTH, N_BRANCH * D_MODEL), 0.02)
    b_f = 2.0 + nrm((DEPTH, ATTN_HEADS), 0.5)
    n_idx = jnp.arange(P, dtype=f32)
    ssm_a_re = -0.5 + nrm((DEPTH, G, P), 0.01)
    ssm_a_im = math.pi * n_idx + nrm((DEPTH, G, P), 0.01)
    ssm_log_dt = jax.random.uniform(next(ks), (DEPTH, G), f32,
                                    minval=math.log(DT_MIN), maxval=math.log(DT_MAX))
    ssm_b_re = nrm((DEPTH, G, P, C), (2.0 * C) ** -0.5)
    ssm_b_im = nrm((DEPTH, G, P, C), (2.0 * C) ** -0.5)
    ssm_c_re = nrm((DEPTH, G, C, P), (2.0 * P) ** -0.5 * 4.0)
    ssm_c_im = nrm((DEPTH, G, C, P), (2.0 * P) ** -0.5 * 4.0)
    ssm_d = nrm((DEPTH, SSM_WIDTH), 1.0)
    w_glu = nrm((DEPTH, SSM_WIDTH, 2 * SSM_WIDTH), SSM_WIDTH ** -0.5)
    w_br_a = nrm((DEPTH, SSM_WIDTH, D_MODEL), SSM_WIDTH ** -0.5)
    w_br_b = nrm((DEPTH, ATTN_WIDTH, D_MODEL), ATTN_WIDTH ** -0.5)
    w_o = nrm((DEPTH, D_MODEL, D_MODEL), D_MODEL ** -0.5)
    g_ffn2 = gain((DEPTH, D_MODEL))
    w2_gate = nrm((DEPTH, D_MODEL, D_FF), D_MODEL ** -0.5)
    w2_up = nrm((DEPTH, D_MODEL, D_FF), D_MODEL ** -0.5)
    w2_down = nrm((DEPTH, D_FF, D_MODEL), D_FF ** -0.5)
    g_final = gain((D_MODEL,))
    return {"x": x, "meta": meta,
            "g_ffn1": g_ffn1, "w1_gate": w1_gate, "w1_up": w1_up, "w1_down": w1_down,
            "g_mix": g_mix, "w_in": w_in, "b_gate": b_gate, "b_f": b_f,
            "ssm_a_re": ssm_a_re, "ssm_a_im": ssm_a_im, "ssm_log_dt": ssm_log_dt,
            "ssm_b_re": ssm_b_re, "ssm_b_im": ssm_b_im, "ssm_c_re": ssm_c_re, "ssm_c_im": ssm_c_im,
            "ssm_d": ssm_d, "w_glu": w_glu, "w_br_a": w_br_a, "w_br_b": w_br_b, "w_o": w_o,
            "g_ffn2": g_ffn2, "w2_gate": w2_gate, "w2_up": w2_up, "w2_down": w2_down,
            "g_final": g_final}


def reference(x, meta, g_ffn1, w1_gate, w1_up, w1_down, g_mix, w_in, b_gate, b_f,
              ssm_a_re, ssm_a_im, ssm_log_dt, ssm_b_re, ssm_b_im, ssm_c_re, ssm_c_im,
              ssm_d, w_glu, w_br_a, w_br_b, w_o, g_ffn2, w2_gate, w2_up, w2_down, g_final):
    bsz = x.shape[0]
    h = jnp.concatenate([jnp.broadcast_to(meta[None].astype(x.dtype), (bsz, N_META, D_MODEL)), x], axis=1)
    L = h.shape[1]
    o_u = 0
    o_q = o_u + SSM_WIDTH
    o_k = o_q + ATTN_WIDTH
    o_v = o_k + ATTN_WIDTH
    o_f = o_v + ATTN_WIDTH
    o_g = o_f + ATTN_HEADS
    for l in range(DEPTH):
        h = h + 0.5 * swiglu(rmsnorm(h, g_ffn1[l]), w1_gate[l], w1_up[l], w1_down[l])
        n = rmsnorm(h, g_mix[l])
        z = n @ w_in[l]
        u = z[..., o_u:o_q]
        q = z[..., o_q:o_k].reshape(bsz, L, ATTN_HEADS, HEAD_DIM)
        k = z[..., o_k:o_v].reshape(bsz, L, ATTN_HEADS, HEAD_DIM)
        v = z[..., o_v:o_f].reshape(bsz, L, ATTN_HEADS, HEAD_DIM)
        f_logit = z[..., o_f:o_g] + b_f[l]
        gates = jax.nn.sigmoid(z[..., o_g:] + b_gate[l])
        y_a = s5_mixer(u, ssm_a_re[l], ssm_a_im[l], ssm_log_dt[l], ssm_b_re[l], ssm_b_im[l],
                       ssm_c_re[l], ssm_c_im[l], ssm_d[l], w_glu[l])
        y_b = forgetting_attention(q, k, v, f_logit).reshape(bsz, L, ATTN_WIDTH)
        merged = gates[..., :D_MODEL] * (y_a @ w_br_a[l]) + gates[..., D_MODEL:] * (y_b @ w_br_b[l])
        h = h + merged @ w_o[l]
        h = h + 0.5 * swiglu(rmsnorm(h, g_ffn2[l]), w2_gate[l], w2_up[l], w2_down[l])
    return rmsnorm(h[:, N_META:], g_final)
```

```python
import numpy as np
import ml_dtypes
import concourse.bass as bass
import concourse.mybir as mybir
from concourse.bass_utils import run_bass_kernel_spmd

F32 = mybir.dt.float32
BF16 = mybir.dt.bfloat16
AF = mybir.ActivationFunctionType
ALU = mybir.AluOpType


class Buf:
    __slots__ = ("name", "w", "r", "alias")

    def __init__(self, name):
        self.name = name
        self.w = None
        self.r = []
        self.alias = []


class Op:
    __slots__ = ("eng", "fn", "waits", "needed", "done", "dma", "idx", "chain")

    def __init__(self, eng, fn):
        self.eng = eng
        self.fn = fn
        self.waits = []
        self.needed = False
        self.done = None
        self.dma = None
        self.chain = False


class Sched:
    ENGS = ("pe", "act", "dve", "pool", "sp")

    def __init__(self, nc):
        self.nc = nc
        self.ops = {e: [] for e in self.ENGS}
        self.dma_sems = {}
        self.dma_gen = {}
        self.all_ops = []

    def _dep(self, op, prod):
        if prod is None or prod is op:
            return
        if prod.eng == "pe" and op.eng == "pe" and prod.dma is None and op.dma is None:
            return
        if prod.eng == "pool" and op.eng == "pool" and prod.dma is None and op.dma is None and getattr(op, "chain", False) and getattr(prod, "chain", False):
            return
        if prod not in op.waits:
            op.waits.append(prod)
            prod.needed = True

    def add(self, eng, fn, reads=(), writes=(), dma_key=None, chain=False):
        op = Op(eng, fn)
        op.dma = dma_key
        op.chain = chain
        for b in reads:
            for bb in [b] + b.alias:
                self._dep(op, bb.w)
        for b in writes:
            for bb in [b] + b.alias:
                self._dep(op, bb.w)
                for r in bb.r:
                    self._dep(op, r)
        for b in reads:
            if dma_key is None:
                b.r = [r for r in b.r if not (r.eng == eng and r.dma is None)]
            b.r.append(op)
        for b in writes:
            b.w = op
            b.r = []
        if dma_key is not None:
            gen = self.dma_gen.get(dma_key, 0)
            if (dma_key, gen) in self.dma_sems and self.dma_sems[(dma_key, gen)][1] >= 1500:
                gen += 1
                self.dma_gen[dma_key] = gen
            dma_key = (dma_key, gen)
            op.dma = dma_key
            ent = self.dma_sems.setdefault(dma_key, [None, 0, None])
            self._dep(op, ent[2])
            ent[1] += 1
            ent[2] = op
            op.done = (dma_key, 16 * ent[1])
            op.needed = True
        self.ops[eng].append(op)
        self.all_ops.append(op)
        return op

    def emit(self, final_waits=()):
        nc = self.nc
        import contextlib
        with contextlib.ExitStack() as es:
            for k, ent in self.dma_sems.items():
                ent[0] = es.enter_context(nc.semaphore("d_%s_%d" % k))
            nsem = 0
            for e in self.ENGS:
                cnt = 0
                gen = 0
                cur = es.enter_context(nc.semaphore("s_%s_%d" % (e, gen)))
                for op in self.ops[e]:
                    if op.dma is not None:
                        op.done = (self.dma_sems[op.dma][0], op.done[1])
                    elif op.needed:
                        if cnt >= 30000:
                            gen += 1
                            cnt = 0
                            cur = es.enter_context(nc.semaphore("s_%s_%d" % (e, gen)))
                        cnt += 1
                        op.done = (cur, cnt)
            block = es.enter_context(nc.Block())
            handles = {"pe": block.tensor, "act": block.scalar, "dve": block.vector,
                       "pool": block.gpsimd, "sp": block.sync}

            def make(e):
                def body(eng):
                    waited = {}
                    for op in self.ops[e]:
                        for p in op.waits:
                            sem, val = p.done
                            key = id(sem)
                            if waited.get(key, 0) >= val:
                                continue
                            waited[key] = val
                            eng.wait_ge(sem, val)
                        ins = op.fn(eng)
                        if op.dma is not None:
                            ins.then_inc(op.done[0], 16)
                        elif op.needed:
                            ins.then_inc(op.done[0], 1)
                    if e == "sp":
                        for p in final_waits:
                            sem, val = p.done
                            if waited.get(id(sem), 0) < val:
                                eng.wait_ge(sem, val)
                return body

            for e in self.ENGS:
                handles[e](make(e))


D = 1024
KT = D // 128
NMETA = 16
DFF = 2816
NFC = DFF // 128
TN = 344
EPS = 1e-6
IN_W = 4104


def sub128(n):
    out = []
    s = 0
    while s < n:
        out.append((s, min(128, n - s)))
        s += 128
    return out


class Ctx:
    pass


class Kern:
    def __init__(self, n_seq=2, x_len=2048, depth=2):
        import contextlib
        self.n_seq, self.x_len, self.depth = n_seq, x_len, depth
        self.L = NMETA + x_len
        assert self.L % TN == 0
        self.NT = self.L // TN
        self.nc = bass.Bass("TRN2", target_bir_lowering=False)
        self.S = Sched(self.nc)
        self.es = contextlib.ExitStack()
        self.bufs = {}
        self.final_ops = []
        self.cast_rr = 0
        nc = self.nc
        self.ps = [self.es.enter_context(nc.psum_tensor("ps%d" % i, [128, 512], F32)) for i in range(8)]
        self.B_ps = [Buf("ps%d" % i) for i in range(8)]

    def din(self, name, shape, dt=F32):
        return self.nc.dram_tensor(name, list(shape), dt, kind="ExternalInput").ap()

    def dscratch(self, name, shape, dt):
        return self.nc.dram_tensor(name, list(shape), dt).ap()

    def sb(self, name, shape, dt):
        t = self.es.enter_context(self.nc.sbuf_tensor(name, list(shape), dt))
        return t

    def B(self, name):
        if name not in self.bufs:
            self.bufs[name] = Buf(name)
        return self.bufs[name]

    def dma(self, out, in_, reads, writes, key, q="sp", slow=False):
        if slow:
            fn = lambda e: e.dma_start(out=out, in_=in_, allow_slow_non_contiguous=True)
        else:
            fn = lambda e: e.dma_start(out=out, in_=in_)
        return self.S.add(q, fn, reads=reads, writes=writes, dma_key=key)

    def copy(self, eng, out, in_, reads, writes):
        if eng == "act":
            return self.S.add("act", lambda e: e.activation(out=out, in_=in_, func=AF.Copy), reads=reads, writes=writes)
        return self.S.add(eng, lambda e: e.tensor_copy(out=out, in_=in_), reads=reads, writes=writes)

    def mm(self, out, lhsT, rhs, start, stop, reads, writes):
        return self.S.add("pe", lambda e: e.matmul(out, lhsT, rhs, start=start, stop=stop), reads=reads, writes=writes)

    def declare(self):
        n_seq, x_len, depth, L = self.n_seq, self.x_len, self.depth, self.L
        self.x = self.din("x", [n_seq, x_len, D])
        self.meta = self.din("meta", [NMETA, D])
        self.g_final = self.din("g_final", [D])
        self.ident_d = self.din("ident", [128, 128])
        dd = max(depth, 1)
        self.P = {}
        for nm, shp in [("g_ffn1", [dd, D]), ("w1_gate", [dd, D, DFF]), ("w1_up", [dd, D, DFF]), ("w1_down", [dd, DFF, D]),
                        ("g_ffn2", [dd, D]), ("w2_gate", [dd, D, DFF]), ("w2_up", [dd, D, DFF]), ("w2_down", [dd, DFF, D])]:
            self.P[nm] = self.din(nm, shp)
        self.out = self.nc.dram_tensor("out", [n_seq, x_len, D], F32, kind="ExternalOutput").ap()
        self.hs = self.dscratch("hs", [n_seq, 128, KT, L], F32)
        self.wgu = self.dscratch("wgu", [dd, 2, 11, 128, 2, KT, 256], BF16)
        self.wd = self.dscratch("wd", [dd, 2, 8, 128, NFC, 128], BF16)
        S = self.S
        self.ident = self.sb("ident_sb", [128, 128], F32)
        self.ones_bf = self.sb("ones_bf", [128, 128], BF16)
        self.gvec = self.sb("gvec", [128, 1 + 3 * dd, KT], F32)
        self.dma(self.ident[:], self.ident_d[:], [], [self.B("ident")], "const")
        self.dma(self.gvec[:, 0, :], self.g_final.rearrange("(k p) -> p k", p=128), [], [self.B("gvec")], "const", slow=True)
        for l in range(depth):
            self.dma(self.gvec[:, 1 + 3 * l, :], self.P["g_ffn1"][l].rearrange("(k p) -> p k", p=128), [], [self.B("gvec")], "const", slow=True)
            self.dma(self.gvec[:, 3 + 3 * l, :], self.P["g_ffn2"][l].rearrange("(k p) -> p k", p=128), [], [self.B("gvec")], "const", slow=True)
        S.add("dve", lambda e: e.memset(self.ones_bf[:], 1.0), writes=[self.B("ones")])
        self.sq = self.sb("sq", [128, KT, TN], BF16)
        self.rstd = self.sb("rstd", [128, TN], F32)
        self.wslot = [self.sb("wslot%d" % i, [128, 4096], BF16) for i in range(4)]
        self.wslot_i = 0
        for i in range(4):
            self.S.add("pool", lambda e, i=i: e.memset(self.wslot[i][:, :], 0.0), writes=[self.B("wslot%d" % i)])
        self.ARENA_BYTES = 141824
        self.arena = self.sb("arena", [128, self.ARENA_BYTES // 4], F32)
        self.reg_off = {"pre": 0, "ffn": 0}
        self.NSTG = 6
        self.stg32 = [_carve(self, "pre", "stg32", 2816, F32) for i in range(self.NSTG)]
        self.stg16 = [_carve(self, "pre", "stg16", 2816, BF16) for i in range(self.NSTG)]
        self.stg_i = 0
        self.bank_rr = 0
        self.zi = 0

    def next_wslot(self):
        i = self.wslot_i % 4
        self.wslot_i += 1
        return self.wslot[i], self.B("wslot%d" % i), "wslot%d" % i

    def cast_rows(self, src, ncols, stores, wname, ld_view=None):
        i = self.stg_i % self.NSTG
        self.stg_i += 1
        s32, s16 = self.stg32[i], self.stg16[i]
        b32, b16 = self.B("stg32_%d" % i), self.B("stg16_%d" % i)
        if ld_view is None:
            self.dma(s32[:, 0:ncols], src, [], [b32], "stg32_%d" % i)
        else:
            dv = ld_view(s32)
            n1 = dv.shape[1]
            step = 4
            parts = []
            for pi, c0 in enumerate(range(0, n1, step)):
                bp = self.B("stg32_%d_%d" % (i, pi))
                parts.append(bp)
                if pi == 0:
                    self.dma(dv[:, c0:min(c0 + step, n1), :], src[:, c0:min(c0 + step, n1), :], [], [bp, b32], "stg32_%d_%d" % (i, pi))
                else:
                    self.dma(dv[:, c0:min(c0 + step, n1), :], src[:, c0:min(c0 + step, n1), :], [b32], [bp], "stg32_%d_%d" % (i, pi))
            parts.append(b32)
            b32 = None
        eng = ("dve", "act", "dve")[self.cast_rr % 3]
        self.cast_rr += 1
        if b32 is None:
            self.copy(eng, s16[:, 0:ncols], s32[:, 0:ncols], parts, [b16])
        else:
            self.copy(eng, s16[:, 0:ncols], s32[:, 0:ncols], [b32], [b16])
        for dst, view in stores:
            self.dma(dst, view(s16), [b16], [self.B(wname)], "stg16_%d" % i, q="pool")

    def prepass_ffn(self, l, f):
        wg = self.P["w%d_gate" % (f + 1)][l]
        wu = self.P["w%d_up" % (f + 1)][l]
        wdn = self.P["w%d_down" % (f + 1)][l]
        nm = "W_ffn_%d_%d" % (l, f)
        for blk in range(11):
            for gu, w in enumerate((wg, wu)):
                src = w[:, blk * 256:(blk + 1) * 256].rearrange("(k p) c -> p k c", p=128)
                dst = self.wgu[l, f, blk][:, gu, :, :].rearrange("p k c -> p (k c)")
                self.cast_rows(src, 2048, [(dst, lambda s: s[:, 0:2048])], nm,
                               ld_view=lambda s: s[:, 0:2048].rearrange("p (k c) -> p k c", k=KT))
        for o in range(KT):
            src = wdn[:, o * 128:(o + 1) * 128].rearrange("(c p) m -> p c m", p=128)
            dst = self.wd[l, f, o].rearrange("p c m -> p (c m)")
            self.cast_rows(src, NFC * 128, [(dst, lambda s: s[:, 0:NFC * 128])], nm,
                           ld_view=lambda s: s[:, 0:NFC * 128].rearrange("p (c m) -> p c m", c=NFC))

    def stage_in(self):
        S, L = self.S, self.L
        xin, hT = self.xin, self.hT
        ps, B_ps = self.ps, self.B_ps
        ident = self.ident
        it = 0
        for q in range(self.n_seq):
            for (t0, nt) in sub128(L):
                sl = it % 2
                it += 1
                xt, bx = xin[sl], self.B("xin%d" % sl)
                if t0 == 0:
                    self.dma(xt[0:NMETA, :], self.meta[:, :], [], [bx], "xin%d" % sl)
                    self.dma(xt[NMETA:nt, :], self.x[q, 0:nt - NMETA, :], [], [bx], "xin%d" % sl)
                else:
                    self.dma(xt[0:nt, :], self.x[q, t0 - NMETA:t0 - NMETA + nt, :], [], [bx], "xin%d" % sl)
                ht, bh = hT[sl], self.B("hT%d" % sl)
                for half in range(2):
                    pb, bpb = ps[half], B_ps[half]
                    for j in range(4):
                        kt = half * 4 + j
                        S.add("pe", lambda e, pb=pb, xt=xt, kt=kt, j=j, nt=nt: e.transpose(
                            pb[:, j * 128:j * 128 + nt], xt[0:nt, kt * 128:(kt + 1) * 128], ident[0:nt, 0:nt]),
                            reads=[bx, self.B("ident")], writes=[bpb])
                    self.copy("dve" if half == 0 else "act", ht[:, half * 4:half * 4 + 4, 0:nt],
                              pb[:].rearrange("p (j t) -> p j t", j=4)[:, :, 0:nt], [bpb], [bh])
                self.dma(self.hs[q, :, :, t0:t0 + nt], ht[:, :, 0:nt], [bh], [self.B("hs%d" % q)], "hTst%d" % sl)

    def rms_stats(self, h_t, B_h, n, pbi=2):
        S = self.S
        sq, rstd, ones_bf = self.sq, self.rstd, self.ones_bf
        B_sq, B_rstd, B_ones = self.B("sq"), self.B("rstd"), self.B("ones")
        pbank, B_pbank = self.ps[pbi], self.B_ps[pbi]
        S.add("act", lambda e: e.activation(out=sq[:, :, 0:n], in_=h_t[:, :, 0:n], func=AF.Square),
              reads=[B_h], writes=[B_sq])
        for kt in range(KT):
            self.mm(pbank[:, 0:n], ones_bf[:, :], sq[:, kt, 0:n], kt == 0, kt == KT - 1, [B_sq, B_ones], [B_pbank])
        S.add("act", lambda e: e.activation(out=rstd[:, 0:n], in_=pbank[:, 0:n], func=AF.Ln,
                                            scale=1.0 / D, bias=EPS), reads=[B_pbank], writes=[B_rstd])
        S.add("act", lambda e: e.activation(out=rstd[:, 0:n], in_=rstd[:, 0:n], func=AF.Exp, scale=-0.5),
              reads=[B_rstd], writes=[B_rstd])

    def norm_apply(self, h_t, B_h, gi, out_t, B_out, n):
        for kt in range(KT):
            self.S.add("dve", lambda e, kt=kt: e.scalar_tensor_tensor(
                out=out_t[:, kt, 0:n], in0=h_t[:, kt, 0:n], scalar=self.gvec[:, gi, kt:kt + 1], in1=self.rstd[:, 0:n],
                op0=ALU.mult, op1=ALU.mult), reads=[B_h, self.B("rstd"), self.B("gvec")], writes=[B_out])

    def alloc_ffn(self):
        cv = lambda nm, n, dt: _carve(self, "ffn", nm, n, dt)
        self.hF = [cv("hF", KT * TN, F32).rearrange("p (k t) -> p k t", k=KT) for i in range(4)]
        self.nF = [cv("nF", KT * TN, BF16).rearrange("p (k t) -> p k t", k=KT) for i in range(4)]
        self.ffn_set = 0
        self.hid = [cv("hid", NFC * TN, BF16).rearrange("p (k t) -> p k t", k=NFC) for i in range(2)]
        self.sil = [cv("sil", TN, F32) for i in range(3)]
        self.sil_i = 0
        self.yn = cv("yn", KT * TN, F32).rearrange("p (k t) -> p k t", k=KT)
        self.yo = [cv("yo", D, F32) for i in range(2)]
        self.xin = [cv("xin", D, F32) for i in range(2)]
        self.hT = [cv("hT", KT * 128, F32).rearrange("p (k t) -> p k t", k=KT) for i in range(2)]

    def ffn_load_norm(self, q, gi, tiles, st, do_load=True, do_norm=True):
        for j, ti in enumerate(tiles):
            jj = 2 * st + j
            hF, bhF = self.hF[jj], self.B("hF%d" % jj)
            if do_load:
                self.dma(hF[:], self.hs[q, :, :, ti * TN:(ti + 1) * TN], [self.B("hs%d" % q)], [bhF], "hF%d" % jj)
            if do_norm:
                self.rms_stats(hF, bhF, TN)
                self.norm_apply(hF, bhF, gi, self.nF[jj], self.B("nF%d" % jj), TN)

    def stage_ffn(self, q, l, f):
        S, NT = self.S, self.NT
        ps, B_ps = self.ps, self.B_ps
        gi = 1 + 3 * l + (0 if f == 0 else 2)
        wname = "W_ffn_%d_%d" % (l, f)
        tiles_all = list(range(NT))
        groups = [tiles_all[g0:g0 + 2] for g0 in range(0, NT, 2)]
        for gidx, tiles in enumerate(groups):
            st = self.ffn_set % 2
            if gidx == 0:
                self.ffn_load_norm(q, gi, tiles, st)
            self.ffn_set += 1
            mmi = 0
            for blk in range(11):
                wt, bw, wk = self.next_wslot()
                self.dma(wt[:, 0:4096], self.wgu[l, f, blk].rearrange("p g k c -> p (g k c)"),
                         [self.B(wname)], [bw], wk)
                wv = wt[:, 0:4096].rearrange("p (g k c) -> p g k c", g=2, k=KT)
                for j, ti in enumerate(tiles):
                    nF, bn = self.nF[2 * st + j], self.B("nF%d" % (2 * st + j))
                    for cc in range(2):
                        c = blk * 2 + cc
                        gi_, ui_ = (0, 1, 6)[mmi % 3], (2, 3, 7)[mmi % 3]
                        pg, bpg = ps[gi_], B_ps[gi_]
                        pu, bpu = ps[ui_], B_ps[ui_]
                        mmi += 1
                        for kt in range(KT):
                            self.mm(pg[:, 0:TN], wv[:, 0, kt, cc * 128:(cc + 1) * 128], nF[:, kt, :],
                                    kt == 0, kt == KT - 1, [bw, bn], [bpg])
                        for kt in range(KT):
                            self.mm(pu[:, 0:TN], wv[:, 1, kt, cc * 128:(cc + 1) * 128], nF[:, kt, :],
                                    kt == 0, kt == KT - 1, [bw, bn], [bpu])
                        si = self.sil_i % 3
                        self.sil_i += 1
                        sl_t, bsl = self.sil[si], self.B("sil%d" % si)
                        S.add("act", lambda e, sl_t=sl_t, pg=pg: e.activation(out=sl_t[:, :], in_=pg[:, 0:TN], func=AF.Silu),
                              reads=[bpg], writes=[bsl])
                        hid, bhid = self.hid[j], self.B("hid%d" % j)
                        S.add("dve", lambda e, hid=hid, c=c, sl_t=sl_t, pu=pu: e.tensor_tensor(
                            out=hid[:, c, :], in0=sl_t[:, :], in1=pu[:, 0:TN], op=ALU.mult),
                            reads=[bsl, bpu], writes=[bhid])
            if gidx + 1 < len(groups):
                self.ffn_load_norm(q, gi, groups[gidx + 1], 1 - st, do_norm=False)
            for o in range(KT):
                if o == 4 and gidx + 1 < len(groups):
                    self.ffn_load_norm(q, gi, groups[gidx + 1], 1 - st, do_load=False)
                wt, bw, wk = self.next_wslot()
                self.dma(wt[:, 0:NFC * 128], self.wd[l, f, o].rearrange("p c o -> p (c o)"),
                         [self.B(wname)], [bw], wk)
                wv = wt[:, 0:NFC * 128].rearrange("p (c o) -> p c o", c=NFC)
                for j, ti in enumerate(tiles):
                    hF, bhF = self.hF[2 * st + j], self.B("hF%d" % (2 * st + j))
                    hid, bhid = self.hid[j], self.B("hid%d" % j)
                    di_ = (4, 5, 6, 7)[mmi % 4]
                    pd, bpd = ps[di_], B_ps[di_]
                    mmi += 1
                    for c in range(NFC):
                        self.mm(pd[:, 0:TN], wv[:, c, :], hid[:, c, :],
                                c == 0, c == NFC - 1, [bw, bhid], [bpd])
                    S.add("dve", lambda e, hF=hF, o=o, pd=pd: e.scalar_tensor_tensor(
                        out=hF[:, o, :], in0=pd[:, 0:TN], scalar=0.5, in1=hF[:, o, :],
                        op0=ALU.mult, op1=ALU.add), reads=[bpd, bhF], writes=[bhF])
            for j, ti in enumerate(tiles):
                hF, bhF = self.hF[2 * st + j], self.B("hF%d" % (2 * st + j))
                self.dma(self.hs[q, :, :, ti * TN:(ti + 1) * TN], hF[:], [bhF], [self.B("hs%d" % q)], "hFst%d" % (2 * st + j), q="pool")

    def stage_final(self):
        S, NT = self.S, self.NT
        ps, B_ps = self.ps, self.B_ps
        hin = self.hF
        yn = self.yn
        B_yn = self.B("yn")
        yo = self.yo
        it = 0
        oi = 0
        for q in range(self.n_seq):
            for ti in range(NT):
                sl = it % 2
                it += 1
                t0 = ti * TN
                hi_, bhi = hin[sl], self.B("hF%d" % sl)
                self.dma(hi_[:], self.hs[q, :, :, t0:t0 + TN], [self.B("hs%d" % q)], [bhi], "hF%d" % sl)
                self.rms_stats(hi_, bhi, TN)
                self.norm_apply(hi_, bhi, 0, yn, B_yn, TN)
                for (s0, nt) in sub128(TN):
                    lo = max(t0 + s0, NMETA)
                    hi = t0 + s0 + nt
                    if hi <= lo:
                        continue
                    a0 = lo - (t0 + s0)
                    so = oi % 2
                    oi += 1
                    yt, byt = yo[so], self.B("yo%d" % so)
                    for half in range(2):
                        pb, bpb = ps[6 + half], B_ps[6 + half]
                        for j in range(4):
                            kt = half * 4 + j
                            S.add("pe", lambda e, pb=pb, kt=kt, j=j, s0=s0, nt=nt: e.transpose(
                                pb[0:nt, j * 128:(j + 1) * 128], yn[:, kt, s0:s0 + nt], self.ident[:, :]),
                                reads=[B_yn, self.B("ident")], writes=[bpb])
                        self.copy("dve" if half == 0 else "act", yt[0:nt, half * 512:half * 512 + 512], pb[0:nt, :], [bpb], [byt])
                    op = self.dma(self.out[q, lo - NMETA:hi - NMETA, :], yt[a0:nt, :], [byt], [], "yo%d" % so)
                    self.final_ops.append(op)

    def finish(self):
        self.S.emit(final_waits=self.final_ops)
        self.es.close()
        return self.nc


NG = 32
NCH = TN // 8
NB = 3
HEADS = 8
MAGIC = 12582912.0
TWO_PI = 6.283185307179586


def _carve(self, region, name, nelems, dt):
    off = self.reg_off[region]
    nbytes = nelems * (4 if dt == F32 else 2)
    nbytes = (nbytes + 31) // 32 * 32
    self.reg_off[region] = off + nbytes
    assert off + nbytes <= self.ARENA_BYTES, (name, off + nbytes)
    a4 = self.arena[:, off // 4:(off + nbytes) // 4]
    v = a4 if dt == F32 else a4.bitcast(dt)
    return v[:, 0:nelems]


def _barrier(self):
    S = self.S
    lasts = [S.ops[e][-1] for e in S.ENGS if S.ops[e]]
    lasts += [ent[2] for ent in S.dma_sems.values() if ent[2] is not None]
    for e in S.ENGS:
        op = S.add(e, lambda eng: eng.nop())
        for p in lasts:
            S._dep(op, p)
            if p.eng == "pe" and e == "pe":
                pass


def _declare_mix(self):
    dd = max(self.depth, 1)
    for nm, shp in [("g_mix", [dd, D]), ("w_in", [dd, D, IN_W]), ("b_gate", [dd, 2 * D]), ("b_f", [dd, HEADS]),
                    ("ssm_a_re", [dd, NG, 64]), ("ssm_a_im", [dd, NG, 64]), ("ssm_log_dt", [dd, NG]),
                    ("ssm_b_re", [dd, NG, 64, 16]), ("ssm_b_im", [dd, NG, 64, 16]),
                    ("ssm_c_re", [dd, NG, 16, 64]), ("ssm_c_im", [dd, NG, 16, 64]), ("ssm_d", [dd, 512]),
                    ("w_glu", [dd, 512, 1024]), ("w_br_a", [dd, 512, 1024]), ("w_br_b", [dd, 512, 1024]),
                    ("w_o", [dd, D, D])]:
        self.P[nm] = self.din(nm, shp)
    self.c_sel = self.din("c_sel", [128, 64 * 128], BF16)
    self.c_mask8 = self.din("c_mask8", [128, 128])
    self.c_maskb = self.din("c_maskb", [128, NB * TN], BF16)
    self.c_et = self.din("c_et", [8, 3 * 8 * 132], BF16)
    self.c_one = self.din("c_one", [1, 256 + TN], BF16)
    self.c_jtab = self.din("c_jtab", [128, 2 * 17])
    self.c_identbf = self.din("c_identbf", [128, 128], BF16)
    self.wu_s = self.dscratch("wu_s", [dd, 128, KT, 512], BF16)
    self.wq_s = self.dscratch("wq_s", [dd, 128, KT, 1024], BF16)
    self.wk_s = self.dscratch("wk_s", [dd, 128, KT, 1024], BF16)
    self.wv_s = self.dscratch("wv_s", [dd, 128, KT, 512], BF16)
    self.wf_s = self.dscratch("wf_s", [dd, 128, KT, 8], BF16)
    self.wmrg_s = self.dscratch("wmrg_s", [dd, 8, 128, 3072], BF16)
    self.wglu_s = self.dscratch("wglu_s", [dd, 2, 128, 4, 512], BF16)
    self.wo_s = self.dscratch("wo_s", [dd, 2, 128, KT, 512], BF16)
    self.s5c_s = self.dscratch("s5c_s", [dd, 128, (3 * 8192 + 256) // 4], F32)
    self.s5_done = set()
    self.sel = self.sb("sel_sb", [128, 64, 128], BF16)
    self.mask8 = self.sb("mask8_sb", [128, 128], F32)
    self.maskb = self.sb("maskb_sb", [128, NB, TN], BF16)
    self.et = self.sb("et_sb", [8, 3, 8, 132], BF16)
    self.one = self.sb("one_sb", [1, 256 + TN], BF16)
    self.jtab = self.sb("jtab_sb", [128, 2, 17], F32)
    self.identbf = self.sb("identbf_sb", [128, 128], BF16)
    self.bgate = self.sb("bgate_sb", [128, dd, 16], F32)
    self.bfneg = self.sb("bfneg_sb", [8, dd], F32)
    bc = self.B("mconst")
    self.dma(self.sel[:].rearrange("p a b -> p (a b)"), self.c_sel[:], [], [bc], "const")
    self.dma(self.mask8[:], self.c_mask8[:], [], [bc], "const")
    self.dma(self.maskb[:].rearrange("p a b -> p (a b)"), self.c_maskb[:], [], [bc], "const")
    self.dma(self.et[:].rearrange("p a b c -> p (a b c)"), self.c_et[:], [], [bc], "const")
    self.dma(self.one[:], self.c_one[:], [], [bc], "const")
    self.dma(self.jtab[:].rearrange("p a b -> p (a b)"), self.c_jtab[:], [], [bc], "const")
    self.dma(self.identbf[:], self.c_identbf[:], [], [bc], "const")
    for l in range(self.depth):
        self.dma(self.gvec[:, 2 + 3 * l, :], self.P["g_mix"][l].rearrange("(k p) -> p k", p=128), [], [self.B("gvec")], "const", slow=True)
        self.dma(self.bgate[:, l, :], self.P["b_gate"][l].rearrange("(k p) -> p k", p=128), [], [bc], "const", slow=True)
        self.dma(self.bfneg[:, l:l + 1], self.P["b_f"][l].rearrange("(h o) -> h o", o=1), [], [bc], "const", slow=True)
    self.S.add("dve", lambda e: e.tensor_scalar(out=self.bfneg[:], in0=self.bfneg[:], scalar1=-1.0, scalar2=None, op0=ALU.mult),
               reads=[bc], writes=[bc])


def _prepass_mix(self, l):
    nm = "W_mix_%d" % l
    w_in = self.P["w_in"][l]
    for kt in range(KT):
        rows = slice(kt * 128, (kt + 1) * 128)
        i = self.stg_i % self.NSTG
        self.stg_i += 1
        s32, s16 = self.stg32[i], self.stg16[i]
        b32, b16 = self.B("stg32_%d" % i), self.B("stg16_%d" % i)
        self.dma(s32[:, 0:2056], w_in[rows, 0:2056], [], [b32], "stg32_%d" % i)
        j = self.stg_i % self.NSTG
        self.stg_i += 1
        s16b, b16b = self.stg16[j], self.B("stg16_%d" % j)
        self.S.add("dve", lambda e, s16b=s16b: e.memset(s16b[:, 0:2048], 0.0), writes=[b16b])
        self.copy("dve", s16[:, 0:512], s32[:, 0:512], [b32], [b16])
        self.copy("act", s16[:, 512:1032], s32[:, 1536:2056], [b32], [b16])
        self.copy("act", s16b[:, 0:1024].rearrange("p (h c) -> p h c", c=128)[:, :, 0:64],
                  s32[:, 512:1024].rearrange("p (h c) -> p h c", c=64), [b32], [b16b])
        self.copy("dve", s16b[:, 1024:2048].rearrange("p (h c) -> p h c", c=128)[:, :, 0:64],
                  s32[:, 1024:1536].rearrange("p (h c) -> p h c", c=64), [b32], [b16b])
        k_ = "stg16_%d" % i
        self.dma(self.wu_s[l, :, kt, :], s16[:, 0:512], [b16], [self.B(nm)], k_, q="pool")
        self.dma(self.wv_s[l, :, kt, :], s16[:, 512:1024], [b16], [self.B(nm)], k_, q="pool")
        self.dma(self.wf_s[l, :, kt, :], s16[:, 1024:1032], [b16], [self.B(nm)], k_, q="pool")
        self.dma(self.wq_s[l, :, kt, :], s16b[:, 0:1024], [b16b], [self.B(nm)], "stg16_%d" % j, q="pool")
        self.dma(self.wk_s[l, :, kt, :], s16b[:, 1024:2048], [b16b], [self.B(nm)], "stg16_%d" % j, q="pool")
        mrg = self.wmrg_s[l]
        stores = []
        for ab in range(2):
            dst = mrg[:, :, ab * 1024 + kt * 128: ab * 1024 + (kt + 1) * 128].rearrange("m p c -> p m c")
            stores.append((dst, (lambda s, ab=ab: s[:, ab * 1024:(ab + 1) * 1024].rearrange("p (m c) -> p m c", c=128))))
        self.cast_rows(w_in[rows, 2056:4104], 2048, stores, nm)
    for kt in range(4):
        dst = self.wglu_s[l][:, :, kt, :].rearrange("b p c -> p b c")
        self.cast_rows(self.P["w_glu"][l][kt * 128:(kt + 1) * 128, :], 1024,
                       [(dst, lambda s: s[:, 0:1024].rearrange("p (b c) -> p b c", c=512))], nm)
    for bi, src in enumerate((self.P["w_br_a"][l], self.P["w_br_b"][l])):
        for kt in range(4):
            off = 2048 + bi * 512 + kt * 128
            dst = self.wmrg_s[l][:, :, off:off + 128].rearrange("m p c -> p m c")
            self.cast_rows(src[kt * 128:(kt + 1) * 128, :], 1024,
                           [(dst, lambda s: s[:, 0:1024].rearrange("p (m c) -> p m c", c=128))], nm)
    for kt in range(KT):
        dst = self.wo_s[l][:, :, kt, :].rearrange("b p c -> p b c")
        self.cast_rows(self.P["w_o"][l][kt * 128:(kt + 1) * 128, :], 1024,
                       [(dst, lambda s: s[:, 0:1024].rearrange("p (b c) -> p b c", c=512))], nm)


def _alloc_mix(self):
    L, NT = self.L, self.NT
    c = lambda reg, nm, n, dt: _carve(self, reg, nm, n, dt)
    self.reg_off["mixP"] = 0
    self.Kc = c("mixP", "Kc", HEADS * L, BF16).rearrange("p (h t) -> p h t", h=HEADS)
    self.Vc = c("mixP", "Vc", NT * NB * HEADS * 65, BF16).rearrange("p (b h d) -> p b h d", h=HEADS, d=65)
    self.s5_off = self.reg_off["mixP"]
    self.W1 = c("mixP", "W1", NG * 128, BF16).rearrange("p (g m) -> p g m", g=NG)
    self.W2 = c("mixP", "W2", 16 * 2 * 128, BF16).rearrange("p (g r m) -> p g r m", g=16, r=2)
    self.W3 = c("mixP", "W3", NG * 2 * 64, BF16).rearrange("p (g r m) -> p g r m", g=NG, r=2)
    self.A8 = c("mixP", "A8", 2 * 2 * 16, F32).rearrange("p (a r g) -> p a r g", a=2, r=2)
    self.Z = [c("mixP", "Z%d" % i, 3 * 16, F32).rearrange("p (r g) -> p r g", r=3) for i in range(2)]
    self.Gk = [c("mixP", "Gk%d" % i, TN, F32) for i in range(2)]
    self.wf = c("mixP", "wf", KT * 8, BF16).rearrange("p (k c) -> p k c", k=KT)
    self.ones8 = c("mixP", "ones8", TN, F32)
    self.dsk = c("mixP", "dsk", NG, F32)
    self.reg_off["mixT"] = self.reg_off["mixP"]
    self.reg_off["mixS"] = self.reg_off["mixP"]
    self.hM = c("mixT", "hM", KT * TN, F32).rearrange("p (k t) -> p k t", k=KT)
    self.nM = c("mixT", "nM", KT * TN, BF16).rearrange("p (k t) -> p k t", k=KT)
    self.Qa = c("mixT", "Qa", KT * TN, BF16).rearrange("p (k t) -> p k t", k=KT)
    self.uT = c("mixT", "uT", 4 * TN, BF16).rearrange("p (k t) -> p k t", k=4)
    self.U = c("mixT", "U", NG * NCH, BF16).rearrange("p (g c) -> p g c", g=NG)
    self.Ssb = c("mixT", "Ssb", 2 * 16 * NCH, F32).rearrange("p (g r c) -> p g r c", r=2, g=16)
    self.Ssb_gr = self.Ssb.rearrange("p g r c -> p (g r) c")
    self.yaT = self.U.rearrange("p g c -> p (g c)").rearrange("p (k t) -> p k t", k=4)
    self.Hbf = c("mixT", "Hbf", 2 * 16 * NCH, BF16).rearrange("p (r g c) -> p r g c", r=2, g=16)
    self.Ybf = c("mixT", "Ybf", NG * NCH, BF16).rearrange("p (g c) -> p g c", g=NG)
    self.ybtok = c("mixT", "ybtok", NB * 512, F32).rearrange("p (b f) -> p b f", b=NB)
    self.ybT = c("mixT", "ybT", 4 * TN, BF16).rearrange("p (k t) -> p k t", k=4)
    self.Pt = [c("mixT", "Pt%d" % i, TN, BF16) for i in range(3)]
    self.gat = [c("mixT", "gat%d" % i, TN, F32) for i in range(2)]
    self.t12 = [c("mixT", "t12_%d" % i, TN, F32) for i in range(2)]
    self.fl = c("mixT", "fl", TN, F32)
    self.gsp = c("mixT", "gsp", 3 * TN, BF16).rearrange("p (j t) -> p j t", j=3)
    self.gr = c("mixT", "gr", TN, F32)
    self.rec = c("mixT", "rec", 8, F32)
    self.m12 = [c("mixT", "m12_%d" % i, 2 * 16, F32).rearrange("p (r g) -> p r g", r=2) for i in range(2)]
    self.ytmp = [self.Ssb.rearrange("p g r c -> p (g r c)")[:, i * TN:(i + 1) * TN] for i in range(4)]
    self.s_lr = c("mixS", "lr", 16, F32)
    self.s_li = c("mixS", "li", 16, F32)
    self.s_dt = c("mixS", "dt", 16, F32)
    self.s_lrd = c("mixS", "lrd", 16, F32)
    self.s_lid = c("mixS", "lid", 16, F32)
    self.s_t = [c("mixS", "st%d" % i, 16 * 2 * 17, F32).rearrange("p (g a j) -> p g a j", g=16, a=2) for i in range(4)]
    self.s_Ere = c("mixS", "Ere", 16 * 2 * 17, F32).rearrange("p (g a j) -> p g a j", g=16, a=2)
    self.s_Eim = c("mixS", "Eim", 16 * 2 * 17, F32).rearrange("p (g a j) -> p g a j", g=16, a=2)
    self.s_sm = [c("mixS", "sm%d" % i, 16, F32) for i in range(6)]
    self.s_b = [c("mixS", "b%d" % i, 16 * 16, F32).rearrange("p (g c) -> p g c", g=16) for i in range(2)]
    self.s_Bb = [c("mixS", "Bb%d" % i, 16 * 16, F32).rearrange("p (g c) -> p g c", g=16) for i in range(2)]
    self.s_cn = [c("mixS", "cn%d" % i, 128, F32) for i in range(2)]
    self.s_c = [c("mixS", "c%d" % i, 16 * 16, F32).rearrange("p (g c) -> p g c", g=16) for i in range(2)]
    self.s_q = [c("mixS", "q%d" % i, 4 * 128, F32).rearrange("p (g t c) -> p g t c", g=4, t=8) for i in range(8)]
    self.s_w1t = c("mixS", "w1t", 128, F32)


def _mix_setup(self, l):
    S = self.S
    P = self.P
    bs = self.B("s5setup")
    bt = self.B("s5tab")
    V, Sc, G = "dve", "act", "pool"
    tt = lambda o, a, b, op, eng="dve": S.add(eng, lambda e: e.tensor_tensor(out=o, in0=a, in1=b, op=op), reads=[bs, self.B("mconst")], writes=[bs])
    ts = lambda o, a, s1, s2, op0, op1=None: S.add("dve", (lambda e: e.tensor_scalar(out=o, in0=a, scalar1=s1, scalar2=s2, op0=op0, op1=op1)) if op1 is not None else
                                                   (lambda e: e.tensor_scalar(out=o, in0=a, scalar1=s1, scalar2=None, op0=op0)), reads=[bs], writes=[bs])
    act = lambda o, a, f, **kw: S.add("act", lambda e: e.activation(out=o, in_=a, func=f, **kw), reads=[bs], writes=[bs])
    for hg in range(2):
        pr = slice(64 * hg, 64 * hg + 64)
        gs = slice(16 * hg, 16 * hg + 16)
        self.dma(self.s_lr[pr, :], P["ssm_a_re"][l][gs, :].rearrange("g p -> p g"), [], [bs], "s5ld", slow=True)
        self.dma(self.s_li[pr, :], P["ssm_a_im"][l][gs, :].rearrange("g p -> p g"), [], [bs], "s5ld", slow=True)
        self.dma(self.s_dt[pr, :], P["ssm_log_dt"][l][gs].partition_broadcast(64), [], [bs], "s5ld", slow=True)
        self.dma(self.s_b[0][pr, :, :], P["ssm_b_re"][l][gs].rearrange("g p c -> p g c"), [], [bs], "s5ld")
        self.dma(self.s_b[1][pr, :, :], P["ssm_b_im"][l][gs].rearrange("g p c -> p g c"), [], [bs], "s5ld")
    for t in range(8):
        self.dma(self.dsk[16 * t:16 * t + 16, :], P["ssm_d"][l].rearrange("(g c) -> c g", c=16), [], [bt], "s5ld", slow=True)
    for ri, nm in enumerate(("ssm_c_re", "ssm_c_im")):
        for half8 in range(2):
            cn = self.s_cn[ri]
            for hg in range(2):
                g0 = 16 * hg + 8 * half8
                self.dma(cn[:, 64 * hg:64 * hg + 64], P[nm][l][g0:g0 + 8].rearrange("g c p -> (g c) p"), [], [bs], "s5ld")
            pb, bpb = self.ps[5], self.B_ps[5]
            S.add("pe", lambda e, pb=pb, cn=cn: e.transpose(pb[:, 0:128], cn[:, :], self.ident[:, :]),
                  reads=[bs, self.B("ident")], writes=[bpb])
            self.copy("dve", self.s_c[ri][:, 8 * half8:8 * half8 + 8, :], pb[:, 0:128].rearrange("p (g c) -> p g c", g=8), [bpb], [bs])
    act(self.s_dt[:, :], self.s_dt[:, :], AF.Exp)
    tt(self.s_lrd[:, :], self.s_lr[:, :], self.s_dt[:, :], ALU.mult)
    tt(self.s_lid[:, :], self.s_li[:, :], self.s_dt[:, :], ALU.mult)
    jt = self.jtab[:, :, :].unsqueeze(1).broadcast_to([128, 16, 2, 17])
    bc3 = lambda a: a.unsqueeze(2).unsqueeze(3).broadcast_to([128, 16, 2, 17])
    t0, t1, t2, t3 = self.s_t
    tt(t0[:], bc3(self.s_lrd[:, :]), jt, ALU.mult)
    act(t0[:], t0[:], AF.Exp)
    tt(t1[:], bc3(self.s_lid[:, :]), jt, ALU.mult)
    for (dst, shift) in ((self.s_Eim, 0.0), (self.s_Ere, 1.5707963267948966)):
        if shift != 0.0:
            ts(t2[:], t1[:], shift, None, ALU.add)
            src = t2
        else:
            src = t1
        ts(t3[:], src[:], 1.0 / TWO_PI, MAGIC, ALU.mult, ALU.add)
        ts(t3[:], t3[:], -MAGIC, None, ALU.add)
        S.add("dve", lambda e, src=src: e.scalar_tensor_tensor(out=t3[:], in0=t3[:], scalar=-TWO_PI, in1=src[:], op0=ALU.mult, op1=ALU.add),
              reads=[bs], writes=[bs])
        ts(t3[:], t3[:], 3.14159, -3.14159, ALU.min, ALU.max)
        act(dst[:], t3[:], AF.Sin)
        tt(dst[:], dst[:], t0[:], ALU.mult)
    Ere, Eim = self.s_Ere, self.s_Eim
    sm = self.s_sm
    ts(sm[0][:, :], Ere[:, :, 0, 9], -1.0, None, ALU.add)
    tt(sm[1][:, :], self.s_lr[:, :], self.s_lr[:, :], ALU.mult)
    tt(sm[2][:, :], self.s_li[:, :], self.s_li[:, :], ALU.mult)
    tt(sm[1][:, :], sm[1][:, :], sm[2][:, :], ALU.add)
    S.add("dve", lambda e: e.reciprocal(out=sm[1][:, :], in_=sm[1][:, :]), reads=[bs], writes=[bs])
    tt(sm[2][:, :], sm[0][:, :], self.s_lr[:, :], ALU.mult)
    tt(sm[3][:, :], Eim[:, :, 0, 9], self.s_li[:, :], ALU.mult)
    tt(sm[2][:, :], sm[2][:, :], sm[3][:, :], ALU.add)
    tt(sm[2][:, :], sm[2][:, :], sm[1][:, :], ALU.mult)
    tt(sm[3][:, :], Eim[:, :, 0, 9], self.s_lr[:, :], ALU.mult)
    tt(sm[4][:, :], sm[0][:, :], self.s_li[:, :], ALU.mult)
    tt(sm[3][:, :], sm[3][:, :], sm[4][:, :], ALU.subtract)
    tt(sm[3][:, :], sm[3][:, :], sm[1][:, :], ALU.mult)
    bcc = lambda a: a.unsqueeze(2).broadcast_to([128, 16, 16])
    bre, bim = self.s_b
    Bbr, Bbi = self.s_Bb
    q = self.s_q
    tt(q[0].rearrange("p g t c -> p (g t c)")[:, 0:256].rearrange("p (g c) -> p g c", g=16), bcc(sm[2][:, :]), bre[:], ALU.mult)
    tmpA = q[0].rearrange("p g t c -> p (g t c)")[:, 0:256].rearrange("p (g c) -> p g c", g=16)
    tmpB = q[0].rearrange("p g t c -> p (g t c)")[:, 256:512].rearrange("p (g c) -> p g c", g=16)
    tt(tmpB, bcc(sm[3][:, :]), bim[:], ALU.mult)
    tt(Bbr[:], tmpA, tmpB, ALU.subtract)
    tt(tmpA, bcc(sm[2][:, :]), bim[:], ALU.mult)
    tt(tmpB, bcc(sm[3][:, :]), bre[:], ALU.mult)
    tt(Bbi[:], tmpA, tmpB, ALU.add)
    S.add("dve", lambda e: e.tensor_copy(out=self.A8[:, 0, 0, :], in_=Ere[:, :, 0, 16]), reads=[bs], writes=[bt])
    S.add("dve", lambda e: e.tensor_copy(out=self.A8[:, 0, 1, :], in_=Ere[:, :, 0, 16]), reads=[bs], writes=[bt])
    S.add("dve", lambda e: e.tensor_scalar(out=self.A8[:, 1, 0, :], in0=Eim[:, :, 0, 16], scalar1=-1.0, scalar2=None, op0=ALU.mult), reads=[bs], writes=[bt])
    S.add("dve", lambda e: e.tensor_copy(out=self.A8[:, 1, 1, :], in_=Eim[:, :, 0, 16]), reads=[bs], writes=[bt])
    cre, cim = self.s_c
    for qq in range(4):
        g4 = slice(4 * qq, 4 * qq + 4)
        shp = [128, 4, 8, 16]
        bE = lambda E, a, j0: E[:, g4, a, j0:j0 + 8].unsqueeze(3).broadcast_to(shp)
        bX = lambda X: X[:, g4, :].unsqueeze(2).broadcast_to(shp)
        W3r, W3i, CNr, CNi, CAr, CAi, ta, tb = q
        tt(ta[:], bE(Ere, 1, 1), bX(Bbr), ALU.mult); tt(tb[:], bE(Eim, 1, 1), bX(Bbi), ALU.mult); tt(W3r[:], ta[:], tb[:], ALU.subtract)
        tt(ta[:], bE(Ere, 1, 1), bX(Bbi), ALU.mult); tt(tb[:], bE(Eim, 1, 1), bX(Bbr), ALU.mult); tt(W3i[:], ta[:], tb[:], ALU.add)
        for (Xr, Xi, j0) in ((CNr, CNi, 1), (CAr, CAi, 9)):
            tt(ta[:], bE(Ere, 0, j0), bX(cre), ALU.mult); tt(tb[:], bE(Eim, 0, j0), bX(cim), ALU.mult); tt(Xr[:], ta[:], tb[:], ALU.subtract)
            tt(ta[:], bE(Eim, 0, j0), bX(cre), ALU.mult); tt(tb[:], bE(Ere, 0, j0), bX(cim), ALU.mult)
            S.add("dve", lambda e, Xi=Xi: e.scalar_tensor_tensor(out=Xi[:], in0=ta[:], scalar=-1.0, in1=tb[:], op0=ALU.mult, op1=ALU.subtract),
                  reads=[bs], writes=[bs])
        for hg in range(2):
            pr = slice(64 * hg, 64 * hg + 64)
            for gq in range(4):
                gp = 4 * qq + gq
                g = 16 * hg + gp
                f2 = lambda X: X[pr, gq, :, :].rearrange("p t c -> p (t c)")
                self.copy("act", self.W2[pr, gp, 0, :], f2(CAr), [bs], [bt])
                self.copy("act", self.W2[pr, gp, 1, :], f2(CAi), [bs], [bt])
                pb, bpb = self.ps[5 + g % 2], self.B_ps[5 + g % 2]
                self.mm(pb[:, 0:128], f2(W3r), f2(CNr), True, False, [bs], [bpb])
                self.mm(pb[:, 0:128], f2(W3i), f2(CNi), False, True, [bs], [bpb])
                S.add("dve", lambda e, pb=pb: e.tensor_tensor(out=self.s_w1t[:, :], in0=pb[:, 0:128], in1=self.mask8[:, :], op=ALU.mult),
                      reads=[bpb, self.B("mconst")], writes=[self.B("w1t")])
                S.add("dve", lambda e, g=g: e.scalar_tensor_tensor(out=self.W1[:, g, :], in0=self.ident[:, :], scalar=self.dsk[:, g:g + 1],
                                                                     in1=self.s_w1t[:, :], op0=ALU.mult, op1=ALU.add),
                      reads=[self.B("w1t"), bt, self.B("ident")], writes=[bt])
                pw, bpw = self.ps[7], self.B_ps[7]
                S.add("pe", lambda e, pw=pw, a=f2(W3r), pr=pr: e.transpose(pw[:, 0:64], a, self.ident[pr, pr]), reads=[bs, self.B("ident")], writes=[bpw])
                S.add("pe", lambda e, pw=pw, a=f2(W3i), pr=pr: e.transpose(pw[:, 64:128], a, self.ident[pr, pr]), reads=[bs, self.B("ident")], writes=[bpw])
                self.copy("act", self.W3[:, g, :, :], pw[:, 0:128].rearrange("p (r m) -> p r m", r=2), [bpw], [bt])


Kern.declare_mix = _declare_mix
Kern.prepass_mix = _prepass_mix
Kern.alloc_mix = _alloc_mix
Kern.mix_setup = _mix_setup
Kern.barrier = _barrier


def _load_w(self, src_ap, n, wname, full=False):
    wt, bw, wk = self.next_wslot()
    self.dma(wt[:, 0:n], src_ap, [self.B(wname)], [bw], wk)
    return (wt[:, :] if full else wt[:, 0:n]), bw


def _nextb(self, pool=(5, 6, 7, 0, 1, 3, 4)):
    i = pool[self.bank_rr % len(pool)]
    self.bank_rr += 1
    return self.ps[i], self.B_ps[i]


def _mix_tile(self, q, l, ti):
    S = self.S
    L, NT = self.L, self.NT
    wname = "W_mix_%d" % l
    t0 = ti * TN
    first = (ti == 0)
    bh, bn, bqa, but, bU = self.B("hM"), self.B("nM"), self.B("Qa"), self.B("uT"), self.B("U")
    bKc, bVc, bt = self.B("Kc"), self.B("Vc"), self.B("s5tab")
    bmc = self.B("mconst")
    hM, nM, Qa, uT, U = self.hM, self.nM, self.Qa, self.uT, self.U
    evr = [0]

    def evac(out, in_, reads, writes, scale=None):
        evr[0] += 1
        if evr[0] % 2 == 0:
            if scale is None:
                return S.add("act", lambda e: e.activation(out=out, in_=in_, func=AF.Copy), reads=reads, writes=writes)
            return S.add("act", lambda e: e.activation(out=out, in_=in_, func=AF.Copy, scale=scale), reads=reads, writes=writes)
        if scale is None:
            return S.add("dve", lambda e: e.tensor_copy(out=out, in_=in_), reads=reads, writes=writes)
        return S.add("dve", lambda e: e.tensor_scalar(out=out, in0=in_, scalar1=scale, scalar2=None, op0=ALU.mult), reads=reads, writes=writes)

    pre_q = [_load_w(self, self.wq_s[l][:, :, hh * 256:(hh + 1) * 256], KT * 256, wname, full=True) for hh in range(2)]
    self.dma(hM[:], self.hs[q, :, :, t0:t0 + TN], [self.B("hs%d" % q)], [bh], "hM")
    self.rms_stats(hM, bh, TN)
    self.norm_apply(hM, bh, 2 + 3 * l, nM, bn, TN)

    pf, bpf = _nextb(self)
    for kt in range(KT):
        self.mm(pf[0:8, 0:TN], self.wf[:, kt, :], nM[:, kt, :], kt == 0, kt == KT - 1, [bt, bn], [bpf])
    bfl, bG = self.B("fl"), self.B("Gk")
    fl, gsp, gr = self.fl, self.gsp, self.gr
    S.add("act", lambda e: e.activation(out=fl[0:8, :], in_=pf[0:8, 0:TN], func=AF.Exp, scale=-1.0, bias=self.bfneg[:, l:l + 1]),
          reads=[bpf, bmc], writes=[bfl])
    S.add("act", lambda e: e.activation(out=fl[0:8, :], in_=fl[0:8, :], func=AF.Ln, bias=1.0), reads=[bfl], writes=[bfl])
    Gc, Gp = self.Gk[ti % 2], self.Gk[(ti + 1) % 2]
    if first:
        S.add("dve", lambda e: e.tensor_tensor_scan(out=Gc[0:8, :], data0=self.ones8[0:8, :], data1=fl[0:8, :], initial=0.0,
                                                    op0=ALU.mult, op1=ALU.add), reads=[bfl, bt], writes=[bG])
    else:
        S.add("dve", lambda e: e.tensor_tensor_scan(out=Gc[0:8, :], data0=self.ones8[0:8, :], data1=fl[0:8, :],
                                                    initial=Gp[0:8, TN - 1:TN], op0=ALU.mult, op1=ALU.add), reads=[bfl, bG, bt], writes=[bG])
    bgs = self.B("gsp")
    S.add("dve", lambda e: e.tensor_copy(out=gsp[0:8, 0, :], in_=Gc[0:8, :]), reads=[bG], writes=[bgs])
    S.add("dve", lambda e: e.tensor_tensor(out=gr[0:8, :], in0=Gc[0:8, :], in1=gsp[0:8, 0, :], op=ALU.subtract), reads=[bG, bgs], writes=[bfl])
    S.add("dve", lambda e: e.tensor_copy(out=gsp[0:8, 1, :], in_=gr[0:8, :]), reads=[bfl], writes=[bgs])
    S.add("dve", lambda e: e.tensor_tensor(out=gr[0:8, :], in0=gr[0:8, :], in1=gsp[0:8, 1, :], op=ALU.subtract), reads=[bfl, bgs], writes=[bfl])
    S.add("dve", lambda e: e.tensor_copy(out=gsp[0:8, 2, :], in_=gr[0:8, :]), reads=[bfl], writes=[bgs])

    one = self.one
    for which in range(2):
        wsrc = (self.wq_s if which == 0 else self.wk_s)[l]
        E = self.et[:, :, :, 4:132] if which == 0 else self.et[:, :, :, 0:128]
        onerow = one[0:1, 0:128] if which == 0 else one[0:1, 128:256]
        for hh in range(4):
            if which == 0 and hh < 2:
                wv_, bw = pre_q[hh]
            else:
                wv_, bw = _load_w(self, wsrc[:, :, hh * 256:(hh + 1) * 256], KT * 256, wname, full=True)
            for h4 in range(2):
                h = hh * 2 + h4
                pb, bpb = _nextb(self)
                for kt in range(KT):
                    c0 = kt * 256 + h4 * 128
                    self.mm(pb[:, 0:TN], wv_[:, c0:c0 + 128], nM[:, kt, :], kt == 0, False, [bw, bn], [bpb])
                for j in range(3):
                    self.mm(pb[:, 0:TN], E[0:8, j, h, :], gsp[0:8, j, :], False, False, [bmc, bgs], [bpb])
                self.mm(pb[:, 0:TN], onerow, one[0:1, 256:256 + TN], False, True, [bmc], [bpb])
                if which == 0:
                    evac(Qa[0:71, h, :], pb[0:71, 0:TN], [bpb], [bqa], scale=0.125)
                else:
                    evac(self.Kc[0:71, h, t0:t0 + TN], pb[0:71, 0:TN], [bpb], [bKc])
    wv_, bw = _load_w(self, self.wv_s[l].rearrange("p k c -> p (k c)"), KT * 512, wname)
    wv_ = wv_.rearrange("p (k c) -> p k c", k=KT)
    for jb, (s0, nt) in enumerate(sub128(TN)):
        pb, bpb = _nextb(self)
        for kt in range(KT):
            self.mm(pb[0:nt, 0:512], nM[:, kt, s0:s0 + nt], wv_[:, kt, :], kt == 0, kt == KT - 1, [bw, bn], [bpb])
        evac(self.Vc[0:nt, ti * NB + jb, :, 0:64], pb[0:nt, 0:512].rearrange("p (h d) -> p h d", h=HEADS), [bpb], [bVc])
    wv_, bw = _load_w(self, self.wu_s[l].rearrange("p k c -> p (k c)"), KT * 512, wname)
    wv_ = wv_.rearrange("p (k c) -> p k c", k=KT)
    for j in range(4):
        pb, bpb = _nextb(self)
        for kt in range(KT):
            self.mm(pb[:, 0:TN], wv_[:, kt, j * 128:(j + 1) * 128], nM[:, kt, :], kt == 0, kt == KT - 1, [bw, bn], [bpb])
        evac(uT[:, j, :], pb[:, 0:TN], [bpb], [but])

    grp_banks = [(0, 11), (11, 22), (22, 32)]
    for bi, (ga, gb) in enumerate(grp_banks):
        pb, bpb = self.ps[5 + bi], self.B_ps[5 + bi]
        for g in range(ga, gb):
            j, gl = g // 8, g % 8
            for s in range(8):
                self.mm(pb[:, (g - ga) * NCH:(g - ga + 1) * NCH], self.sel[:, gl * 8 + s, :], uT[:, j, s:TN:8],
                        s == 0, s == 7, [bmc, but], [bpb])
        evac(U[:, ga:gb, :], pb[:, 0:(gb - ga) * NCH].rearrange("p (g c) -> p g c", c=NCH), [bpb], [bU])
    bS, bH = self.B("Ssb"), self.B("Hbf")
    Ssb, Hbf = self.Ssb, self.Hbf
    blk_banks = [(0, 11), (11, 22), (22, 32)]
    for bi, (ba, bb) in enumerate(blk_banks):
        pb, bpb = self.ps[5 + bi], self.B_ps[5 + bi]
        for hg in range(2):
            pr = slice(64 * hg, 64 * hg + 64)
            for blk in range(ba, bb):
                gp, ri = blk // 2, blk % 2
                g = 16 * hg + gp
                self.mm(pb[pr, (blk - ba) * NCH:(blk - ba + 1) * NCH], self.W3[:, g, ri, :], U[:, g, :], True, True, [bt, bU], [bpb])
        evac(self.Ssb_gr[:, ba:bb, :], pb[:, 0:(bb - ba) * NCH].rearrange("p (b c) -> p b c", c=NCH), [bpb], [bS])
    bZ = self.B("Z")
    A8 = self.A8
    if first:
        S.add("pool", lambda e: e.memset(self.Z[0][:], 0.0), writes=[bZ])
    zi = self.zi
    for c in range(NCH):
        Zc, Zn = self.Z[zi % 2], self.Z[(zi + 1) % 2]
        zi += 1
        m1, m2 = self.m12
        bm = self.B("m12")
        S.add("pool", lambda e, Zc=Zc, c=c: e.tensor_copy(out=Hbf[:, :, :, c], in_=Zc[:, 0:2, :]), reads=[bZ], writes=[bH], chain=True)
        S.add("pool", lambda e, Zc=Zc: e.tensor_tensor(out=m1[:], in0=A8[:, 0, :, :], in1=Zc[:, 0:2, :], op=ALU.mult), reads=[bZ, bt], writes=[bm], chain=True)
        S.add("pool", lambda e, Zc=Zc: e.tensor_tensor(out=m2[:], in0=A8[:, 1, :, :], in1=Zc[:, 1:3, :], op=ALU.mult), reads=[bZ, bt], writes=[bm], chain=True)
        S.add("pool", lambda e: e.tensor_tensor(out=m1[:], in0=m1[:], in1=m2[:], op=ALU.add), reads=[bm], writes=[bm], chain=True)
        S.add("pool", lambda e, Zn=Zn, c=c: e.tensor_tensor(out=Zn[:, 0:2, :], in0=m1[:], in1=Ssb[:, :, :, c].rearrange("p g r -> p r g"), op=ALU.add), reads=[bm, bS], writes=[bZ], chain=True)
        S.add("pool", lambda e, Zn=Zn: e.tensor_copy(out=Zn[:, 2, :], in_=Zn[:, 0, :]), reads=[bZ], writes=[bZ], chain=True)
    self.zi = zi
    ybtok, ybT = self.ybtok, self.ybT
    bybt, bybT = self.B("ybtok"), self.B("ybT")
    qsubs = sub128(TN)
    nkb = (ti + 1) * NB
    pti = 0
    for h in range(HEADS):
        ob0 = 2 if h % 2 == 0 else 5
        Ob = [(self.ps[ob0 + i], self.B_ps[ob0 + i]) for i in range(3)]
        for kb in range(nkb):
            kti, kj = kb // NB, kb % NB
            ks, nk = kti * TN + kj * 128, (128 if kj < 2 else TN - 256)
            diag = (kti == ti)
            pS, bpS = self.ps[kb % 2], self.B_ps[kb % 2]
            mk = nk
            self.mm(pS[0:mk, 0:TN], self.Kc[0:71, h, ks:ks + mk], Qa[0:71, h, :], True, not diag, [bKc, bqa], [bpS])
            if diag:
                self.mm(pS[0:mk, 0:TN], self.identbf[0:nk, 0:mk], self.maskb[0:nk, kj, :], False, True, [bmc], [bpS])
            Pt, bPt = self.Pt[pti % 3], self.B("Pt%d" % (pti % 3))
            pti += 1
            S.add("act", lambda e, Pt=Pt, pS=pS, nk=nk: e.activation(out=Pt[0:nk, :], in_=pS[0:nk, 0:TN], func=AF.Exp), reads=[bpS], writes=[bPt])
            for sq_, (qs, nq) in enumerate(qsubs):
                if diag and sq_ < kj:
                    continue
                lastkb = nkb - 1 if True else 0
                is_last = diag and kj == sq_
                pO, bpO = Ob[sq_]
                self.mm(pO[0:nq, 0:65], Pt[0:nk, qs:qs + nq], self.Vc[0:nk, kb, h, :], kb == 0, is_last, [bPt, bVc], [bpO])
        for sq_, (qs, nq) in enumerate(qsubs):
            pO, bpO = Ob[sq_]
            brec = self.B("rec")
            S.add("dve", lambda e, pO=pO, nq=nq: e.reciprocal(out=self.rec[0:nq, 0:1], in_=pO[0:nq, 64:65]), reads=[bpO], writes=[brec])
            S.add("dve", lambda e, pO=pO, nq=nq, sq_=sq_, h=h: e.tensor_scalar(out=ybtok[0:nq, sq_, h * 64:(h + 1) * 64], in0=pO[0:nq, 0:64],
                                                                             scalar1=self.rec[0:nq, 0:1], scalar2=None, op0=ALU.mult),
                  reads=[bpO, brec], writes=[bybt])
    for sq_, (qs, nq) in enumerate(qsubs):
        pb, bpb = _nextb(self)
        for kt in range(4):
            S.add("pe", lambda e, pb=pb, kt=kt, nq=nq, sq_=sq_: e.transpose(pb[:, kt * 128:kt * 128 + nq], ybtok[0:nq, sq_, kt * 128:(kt + 1) * 128],
                                                                      self.ident[0:nq, 0:nq]), reads=[bybt, self.B("ident")], writes=[bpb])
        evac(ybT[:, :, qs:qs + nq], pb[:, 0:512].rearrange("p (k t) -> p k t", k=4)[:, :, 0:nq], [bpb], [bybT])

    bY = self.B("Ybf")
    Ybf = self.Ybf
    for bi, (ga, gb) in enumerate(grp_banks):
        pb, bpb = self.ps[5 + bi], self.B_ps[5 + bi]
        for g in range(ga, gb):
            hg, gp = g // 16, g % 16
            pr = slice(64 * hg, 64 * hg + 64)
            o = pb[:, (g - ga) * NCH:(g - ga + 1) * NCH]
            self.mm(o, self.W1[:, g, :], U[:, g, :], True, False, [bt, bU], [bpb])
            self.mm(o, self.W2[pr, gp, 0, :], Hbf[pr, 0, gp, :], False, False, [bt, bH], [bpb])
            self.mm(o, self.W2[pr, gp, 1, :], Hbf[pr, 1, gp, :], False, True, [bt, bH], [bpb])
        evac(Ybf[:, ga:gb, :], pb[:, 0:(gb - ga) * NCH].rearrange("p (g c) -> p g c", c=NCH), [bpb], [bY])
    y0, y1, y2, y3 = self.ytmp
    byt = self.B("Ssb")
    gT = uT
    for j in range(4):
        pb, bpb = _nextb(self)
        for t in range(8):
            for gl in range(8):
                self.mm(pb[:, t * NCH:(t + 1) * NCH], self.sel[:, t * 8 + gl, :], Ybf[:, 8 * j + gl, :], gl == 0, gl == 7, [bmc, bY], [bpb])
        S.add("dve", lambda e, pb=pb: e.tensor_copy(out=y0.rearrange("p (c t) -> p t c", t=8), in_=pb[:, 0:TN].rearrange("p (t c) -> p t c", t=8)),
              reads=[bpb], writes=[byt])
        S.add("act", lambda e: e.activation(out=y1, in_=y0, func=AF.Square), reads=[byt], writes=[byt])
        S.add("dve", lambda e: e.tensor_scalar(out=y1, in0=y1, scalar1=0.044715, scalar2=1.0, op0=ALU.mult, op1=ALU.add), reads=[byt], writes=[byt])
        S.add("dve", lambda e: e.tensor_tensor(out=y1, in0=y1, in1=y0, op=ALU.mult), reads=[byt], writes=[byt])
        S.add("act", lambda e: e.activation(out=y2, in_=y1, func=AF.Sigmoid, scale=1.5957691216057308), reads=[byt], writes=[byt])
        S.add("dve", lambda e, j=j: e.tensor_tensor(out=gT[:, j, :], in0=y0, in1=y2, op=ALU.mult), reads=[byt], writes=[but])
    yaT = self.yaT
    w1_, bw1 = _load_w(self, self.wglu_s[l, 0].rearrange("p k c -> p (k c)"), 4 * 512, wname)
    w2_, bw2 = _load_w(self, self.wglu_s[l, 1].rearrange("p k c -> p (k c)"), 4 * 512, wname)
    w1_ = w1_.rearrange("p (k c) -> p k c", k=4)
    w2_ = w2_.rearrange("p (k c) -> p k c", k=4)
    for m in range(4):
        pa, bpa = _nextb(self)
        pb2, bpb2 = _nextb(self)
        for kt in range(4):
            self.mm(pa[:, 0:TN], w1_[:, kt, m * 128:(m + 1) * 128], gT[:, kt, :], kt == 0, kt == 3, [bw1, but], [bpa])
        for kt in range(4):
            self.mm(pb2[:, 0:TN], w2_[:, kt, m * 128:(m + 1) * 128], gT[:, kt, :], kt == 0, kt == 3, [bw2, but], [bpb2])
        S.add("act", lambda e, pb2=pb2: e.activation(out=y3, in_=pb2[:, 0:TN], func=AF.Sigmoid), reads=[bpb2], writes=[byt])
        S.add("dve", lambda e, pa=pa, m=m: e.tensor_tensor(out=yaT[:, m, :], in0=pa[:, 0:TN], in1=y3, op=ALU.mult), reads=[bpa, byt], writes=[bU])

    mg = Qa
    pool5 = (5, 6, 7, 0, 1, 2, 3, 4)
    for m in range(8):
        wm_, bwm = _load_w(self, self.wmrg_s[l, m], 3072, wname)
        wga_ = wm_[:, 0:1024].rearrange("p (k c) -> p k c", k=KT)
        wgb_ = wm_[:, 1024:2048].rearrange("p (k c) -> p k c", k=KT)
        wa_ = wm_[:, 2048:2560].rearrange("p (k c) -> p k c", k=4)
        wb_ = wm_[:, 2560:3072].rearrange("p (k c) -> p k c", k=4)
        bwga = bwgb = bwa = bwb = bwm
        if True:
            cs = slice(0, 128)
            pga, bpga = _nextb(self, pool5)
            for kt in range(KT):
                self.mm(pga[:, 0:TN], wga_[:, kt, cs], nM[:, kt, :], kt == 0, kt == KT - 1, [bwga, bn], [bpga])
            pA, bpA = _nextb(self, pool5)
            for kt in range(4):
                self.mm(pA[:, 0:TN], wa_[:, kt, cs], yaT[:, kt, :], kt == 0, kt == 3, [bwa, bU], [bpA])
            pgb, bpgb = _nextb(self, pool5)
            for kt in range(KT):
                self.mm(pgb[:, 0:TN], wgb_[:, kt, cs], nM[:, kt, :], kt == 0, kt == KT - 1, [bwgb, bn], [bpgb])
            pB, bpB = _nextb(self, pool5)
            for kt in range(4):
                self.mm(pB[:, 0:TN], wb_[:, kt, cs], ybT[:, kt, :], kt == 0, kt == 3, [bwb, bybT], [bpB])
            g0, g1 = self.gat
            bg0, bg1 = self.B("gat0"), self.B("gat1")
            t1, t2 = self.t12
            b1, b2 = self.B("t12_0"), self.B("t12_1")
            S.add("act", lambda e, pga=pga, m=m: e.activation(out=g0, in_=pga[:, 0:TN], func=AF.Sigmoid, bias=self.bgate[:, l, m:m + 1]),
                  reads=[bpga, bmc], writes=[bg0])
            S.add("act", lambda e, pgb=pgb, m=m: e.activation(out=g1, in_=pgb[:, 0:TN], func=AF.Sigmoid, bias=self.bgate[:, l, 8 + m:9 + m]),
                  reads=[bpgb, bmc], writes=[bg1])
            S.add("dve", lambda e, pA=pA: e.tensor_tensor(out=t1, in0=g0, in1=pA[:, 0:TN], op=ALU.mult), reads=[bg0, bpA], writes=[b1])
            S.add("dve", lambda e, pB=pB: e.tensor_tensor(out=t2, in0=g1, in1=pB[:, 0:TN], op=ALU.mult), reads=[bg1, bpB], writes=[b2])
            S.add("dve", lambda e, m=m: e.tensor_tensor(out=mg[:, m, :], in0=t1, in1=t2, op=ALU.add), reads=[b1, b2], writes=[bqa])
    for half in range(2):
        wo_, bwo = _load_w(self, self.wo_s[l, half].rearrange("p k c -> p (k c)"), KT * 512, wname)
        wo_ = wo_.rearrange("p (k c) -> p k c", k=KT)
        for mm_ in range(4):
            o = half * 4 + mm_
            po, bpo = _nextb(self, pool5)
            for kt in range(KT):
                self.mm(po[:, 0:TN], wo_[:, kt, mm_ * 128:(mm_ + 1) * 128], mg[:, kt, :], kt == 0, kt == KT - 1, [bwo, bqa], [bpo])
            S.add("dve", lambda e, po=po, o=o: e.tensor_tensor(out=hM[:, o, :], in0=hM[:, o, :], in1=po[:, 0:TN], op=ALU.add),
                  reads=[bpo, bh], writes=[bh])
    self.dma(self.hs[q, :, :, t0:t0 + TN], hM[:], [bh], [self.B("hs%d" % q)], "hMst", q="pool")


def _mix_begin(self, l):
    S = self.S
    bt = self.B("s5tab")
    o4 = self.s5_off // 4
    tabs = self.arena[:, o4:o4 + (3 * 8192 + 256) // 4]
    if l not in self.s5_done:
        self.mix_setup(l)
        self.s5_done.add(l)
        self.dma(self.s5c_s[l], tabs, [bt], [self.B("s5c%d" % l)], "s5c")
    else:
        self.dma(tabs, self.s5c_s[l], [self.B("s5c%d" % l)], [bt], "s5c")
    S.add("pool", lambda e: e.memset(self.ones8[:, :], 1.0), writes=[bt])
    S.add("pool", lambda e: e.memset(self.Vc[:, :, :, 64:65], 1.0), writes=[self.B("Vc")])
    self.dma(self.wf[:].rearrange("p k c -> p (k c)"), self.wf_s[l].rearrange("p k c -> p (k c)"), [self.B("W_mix_%d" % l)], [bt], "wfld")
    self.zi = 0


Kern.mix_begin = _mix_begin
Kern.mix_tile = _mix_tile


def build(n_seq=2, x_len=2048, depth=2, mix=True, ffn=True):
    k = Kern(n_seq, x_len, depth)
    k.declare()
    if mix:
        k.declare_mix()
    k.alloc_ffn()
    if mix:
        k.alloc_mix()
    for l in range(depth):
        if ffn:
            for f in range(2):
                k.prepass_ffn(l, f)
        if mix:
            k.prepass_mix(l)
    k.barrier()
    k.stage_in()
    for q in range(n_seq):
        for l in range(depth):
            if ffn:
                k.stage_ffn(q, l, 0)
            if mix:
                k.barrier()
                k.mix_begin(l)
                k.barrier()
                for ti in range(k.NT):
                    k.mix_tile(q, l, ti)
                k.barrier()
            if ffn:
                k.stage_ffn(q, l, 1)
    k.stage_final()
    return k.finish()


def host_consts():
    bf = ml_dtypes.bfloat16
    c = {}
    c["ident"] = np.eye(128, dtype=np.float32)
    sel = np.zeros((128, 64, 128), np.float32)
    for a in range(8):
        for b in range(8):
            for i in range(16):
                sel[16 * a + i, a * 8 + b, 16 * b + i] = 1.0
    c["c_sel"] = sel.reshape(128, 64 * 128).astype(bf)
    m8 = np.zeros((128, 128), np.float32)
    for s in range(8):
        for t in range(s, 8):
            m8[16 * s:16 * s + 16, 16 * t:16 * t + 16] = 1.0
    c["c_mask8"] = m8
    mb = np.zeros((128, NB, TN), np.float32)
    r = np.arange(128)[:, None]
    ql = np.arange(TN)[None, :]
    for j in range(NB):
        mb[:, j, :] = np.where(ql - r - 128 * j >= 0, 0.0, -30000.0)
    c["c_maskb"] = mb.reshape(128, NB * TN).astype(bf)
    et = np.zeros((8, 3, 8, 132), np.float32)
    for h in range(8):
        for j in range(3):
            et[h, j, h, 68 + j] = 1.0
    c["c_et"] = et.reshape(8, -1).astype(bf)
    one = np.zeros((1, 256 + TN), np.float32)
    one[0, 68:71] = 8.0
    one[0, 128 + 64:128 + 67] = -8.0
    one[0, 256:] = 1.0
    c["c_one"] = one.astype(bf)
    jt = np.zeros((128, 2, 17), np.float32)
    jt[:, 0, :] = np.arange(17) - 8
    jt[:, 1, :] = 8 - np.arange(17)
    c["c_jtab"] = jt.reshape(128, 34)
    c["c_identbf"] = np.eye(128, dtype=np.float32).astype(bf)
    return c


PARAM_NAMES = ["g_ffn1", "w1_gate", "w1_up", "w1_down", "g_mix", "w_in", "b_gate", "b_f",
               "ssm_a_re", "ssm_a_im", "ssm_log_dt", "ssm_b_re", "ssm_b_im", "ssm_c_re", "ssm_c_im",
               "ssm_d", "w_glu", "w_br_a", "w_br_b", "w_o", "g_ffn2", "w2_gate", "w2_up", "w2_down"]

_NC_CACHE = {}


def kernel(**inputs):
    n_cores = 8
    x = np.ascontiguousarray(np.asarray(inputs["x"], dtype=np.float32))
    bsz, x_len, _ = x.shape
    n_seq = bsz // n_cores
    key = (n_seq, x_len)
    if key not in _NC_CACHE:
        _NC_CACHE[key] = build(n_seq=n_seq, x_len=x_len, depth=2)
    nc = _NC_CACHE[key]
    base = host_consts()
    base["meta"] = np.ascontiguousarray(np.asarray(inputs["meta"], np.float32))
    base["g_final"] = np.ascontiguousarray(np.asarray(inputs["g_final"], np.float32))
    for nm in PARAM_NAMES:
        base[nm] = np.ascontiguousarray(np.asarray(inputs[nm], np.float32))
    in_maps = []
    for c in range(n_cores):
        m = dict(base)
        m["x"] = x[c * n_seq:(c + 1) * n_seq]
        in_maps.append(m)
    res = run_bass_kernel_spmd(nc, in_maps, core_ids=list(range(n_cores)))
    return np.concatenate([np.asarray(r["out"], np.float32) for r in res.results], axis=0)
```

```python
import numpy as np
import ml_dtypes
import concourse.bass as bass
import concourse.mybir as mybir
from concourse.bass_utils import run_bass_kernel_spmd

F32 = mybir.dt.float32
BF16 = mybir.dt.bfloat16
AF = mybir.ActivationFunctionType
ALU = mybir.AluOpType


class Buf:
    __slots__ = ("name", "w", "r", "alias")

    def __init__(self, name):
        self.name = name
        self.w = None
        self.r = []
        self.alias = []


class Op:
    __slots__ = ("eng", "fn", "waits", "needed", "done", "dma", "idx", "chain")

    def __init__(self, eng, fn):
        self.eng = eng
        self.fn = fn
        self.waits = []
        self.needed = False
        self.done = None
        self.dma = None
        self.chain = False


class Sched:
    ENGS = ("pe", "act", "dve", "pool", "sp")

    def __init__(self, nc):
        self.nc = nc
        self.ops = {e: [] for e in self.ENGS}
        self.dma_sems = {}
        self.dma_gen = {}
        self.all_ops = []

    def _dep(self, op, prod):
        if prod is None or prod is op:
            return
        if prod.eng == "pe" and op.eng == "pe" and prod.dma is None and op.dma is None:
            return
        if prod.eng == "pool" and op.eng == "pool" and prod.dma is None and op.dma is None and getattr(op, "chain", False) and getattr(prod, "chain", False):
            return
        if prod not in op.waits:
            op.waits.append(prod)
            prod.needed = True

    def add(self, eng, fn, reads=(), writes=(), dma_key=None, chain=False):
        op = Op(eng, fn)
        op.dma = dma_key
        op.chain = chain
        for b in reads:
            for bb in [b] + b.alias:
                self._dep(op, bb.w)
        for b in writes:
            for bb in [b] + b.alias:
                self._dep(op, bb.w)
                for r in bb.r:
                    self._dep(op, r)
        for b in reads:
            if dma_key is None:
                b.r = [r for r in b.r if not (r.eng == eng and r.dma is None)]
            b.r.append(op)
        for b in writes:
            b.w = op
            b.r = []
        if dma_key is not None:
            gen = self.dma_gen.get(dma_key, 0)
            if (dma_key, gen) in self.dma_sems and self.dma_sems[(dma_key, gen)][1] >= 1500:
                gen += 1
                self.dma_gen[dma_key] = gen
            dma_key = (dma_key, gen)
            op.dma = dma_key
            ent = self.dma_sems.setdefault(dma_key, [None, 0, None])
            self._dep(op, ent[2])
            ent[1] += 1
            ent[2] = op
            op.done = (dma_key, 16 * ent[1])
            op.needed = True
        self.ops[eng].append(op)
        self.all_ops.append(op)
        return op

    def emit(self, final_waits=()):
        nc = self.nc
        import contextlib
        with contextlib.ExitStack() as es:
            for k, ent in self.dma_sems.items():
                ent[0] = es.enter_context(nc.semaphore("d_%s_%d" % k))
            nsem = 0
            for e in self.ENGS:
                cnt = 0
                gen = 0
                cur = es.enter_context(nc.semaphore("s_%s_%d" % (e, gen)))
                for op in self.ops[e]:
                    if op.dma is not None:
                        op.done = (self.dma_sems[op.dma][0], op.done[1])
                    elif op.needed:
                        if cnt >= 30000:
                            gen += 1
                            cnt = 0
                            cur = es.enter_context(nc.semaphore("s_%s_%d" % (e, gen)))
                        cnt += 1
                        op.done = (cur, cnt)
            block = es.enter_context(nc.Block())
            handles = {"pe": block.tensor, "act": block.scalar, "dve": block.vector,
                       "pool": block.gpsimd, "sp": block.sync}

            def make(e):
                def body(eng):
                    waited = {}
                    for op in self.ops[e]:
                        for p in op.waits:
                            sem, val = p.done
                            key = id(sem)
                            if waited.get(key, 0) >= val:
                                continue
                            waited[key] = val
                            eng.wait_ge(sem, val)
                        ins = op.fn(eng)
                        if op.dma is not None:
                            ins.then_inc(op.done[0], 16)
                        elif op.needed:
                            ins.then_inc(op.done[0], 1)
                    if e == "sp":
                        for p in final_waits:
                            sem, val = p.done
                            if waited.get(id(sem), 0) < val:
                                eng.wait_ge(sem, val)
                return body

            for e in self.ENGS:
                handles[e](make(e))


D = 1024
KT = D // 128
NMETA = 16
DFF = 2816
NFC = DFF // 128
TN = 344
EPS = 1e-6
IN_W = 4104


def sub128(n):
    out = []
    s = 0
    while s < n:
        out.append((s, min(128, n - s)))
        s += 128
    return out


class Ctx:
    pass


class Kern:
    def __init__(self, n_seq=2, x_len=2048, depth=2):
        import contextlib
        self.n_seq, self.x_len, self.depth = n_seq, x_len, depth
        self.L = NMETA + x_len
        assert self.L % TN == 0
        self.NT = self.L // TN
        self.nc = bass.Bass("TRN2", target_bir_lowering=False)
        self.S = Sched(self.nc)
        self.es = contextlib.ExitStack()
        self.bufs = {}
        self.final_ops = []
        self.cast_rr = 0
        nc = self.nc
        self.ps = [self.es.enter_context(nc.psum_tensor("ps%d" % i, [128, 512], F32)) for i in range(8)]
        self.B_ps = [Buf("ps%d" % i) for i in range(8)]

    def din(self, name, shape, dt=F32):
        return self.nc.dram_tensor(name, list(shape), dt, kind="ExternalInput").ap()

    def dscratch(self, name, shape, dt):
        return self.nc.dram_tensor(name, list(shape), dt).ap()

    def sb(self, name, shape, dt):
        t = self.es.enter_context(self.nc.sbuf_tensor(name, list(shape), dt))
        return t

    def B(self, name):
        if name not in self.bufs:
            self.bufs[name] = Buf(name)
        return self.bufs[name]

    def dma(self, out, in_, reads, writes, key, q="sp", slow=False):
        if slow:
            fn = lambda e: e.dma_start(out=out, in_=in_, allow_slow_non_contiguous=True)
        else:
            fn = lambda e: e.dma_start(out=out, in_=in_)
        return self.S.add(q, fn, reads=reads, writes=writes, dma_key=key)

    def copy(self, eng, out, in_, reads, writes):
        if eng == "act":
            return self.S.add("act", lambda e: e.activation(out=out, in_=in_, func=AF.Copy), reads=reads, writes=writes)
        return self.S.add(eng, lambda e: e.tensor_copy(out=out, in_=in_), reads=reads, writes=writes)

    def mm(self, out, lhsT, rhs, start, stop, reads, writes):
        return self.S.add("pe", lambda e: e.matmul(out, lhsT, rhs, start=start, stop=stop), reads=reads, writes=writes)

    def declare(self):
        n_seq, x_len, depth, L = self.n_seq, self.x_len, self.depth, self.L
        self.x = self.din("x", [n_seq, x_len, D])
        self.meta = self.din("meta", [NMETA, D])
        self.g_final = self.din("g_final", [D])
        self.ident_d = self.din("ident", [128, 128])
        dd = max(depth, 1)
        self.P = {}
        for nm, shp in [("g_ffn1", [dd, D]), ("w1_gate", [dd, D, DFF]), ("w1_up", [dd, D, DFF]), ("w1_down", [dd, DFF, D]),
                        ("g_ffn2", [dd, D]), ("w2_gate", [dd, D, DFF]), ("w2_up", [dd, D, DFF]), ("w2_down", [dd, DFF, D])]:
            self.P[nm] = self.din(nm, shp)
        self.out = self.nc.dram_tensor("out", [n_seq, x_len, D], F32, kind="ExternalOutput").ap()
        self.hs = self.dscratch("hs", [n_seq, 128, KT, L], F32)
        self.wgu = self.dscratch("wgu", [dd, 2, 11, 128, 2, KT, 256], BF16)
        self.wd = self.dscratch("wd", [dd, 2, 8, 128, NFC, 128], BF16)
        S = self.S
        self.ident = self.sb("ident_sb", [128, 128], F32)
        self.ones_bf = self.sb("ones_bf", [128, 128], BF16)
        self.gvec = self.sb("gvec", [128, 1 + 3 * dd, KT], F32)
        self.dma(self.ident[:], self.ident_d[:], [], [self.B("ident")], "const")
        self.dma(self.gvec[:, 0, :], self.g_final.rearrange("(k p) -> p k", p=128), [], [self.B("gvec")], "const", slow=True)
        for l in range(depth):
            self.dma(self.gvec[:, 1 + 3 * l, :], self.P["g_ffn1"][l].rearrange("(k p) -> p k", p=128), [], [self.B("gvec")], "const", slow=True)
            self.dma(self.gvec[:, 3 + 3 * l, :], self.P["g_ffn2"][l].rearrange("(k p) -> p k", p=128), [], [self.B("gvec")], "const", slow=True)
        S.add("dve", lambda e: e.memset(self.ones_bf[:], 1.0), writes=[self.B("ones")])
        self.sq = self.sb("sq", [128, KT, TN], BF16)
        self.rstd = self.sb("rstd", [128, TN], F32)
        self.wslot = [self.sb("wslot%d" % i, [128, 4096], BF16) for i in range(4)]
        self.wslot_i = 0
        for i in range(4):
            self.S.add("pool", lambda e, i=i: e.memset(self.wslot[i][:, :], 0.0), writes=[self.B("wslot%d" % i)])
        self.ARENA_BYTES = 143360
        self.arena = self.sb("arena", [128, self.ARENA_BYTES // 4], F32)
        self.reg_off = {"pre": 0, "ffn": 0}
        self.NSTG = 6
        self.stg32 = [_carve(self, "pre", "stg32", 2816, F32) for i in range(self.NSTG)]
        self.stg16 = [_carve(self, "pre", "stg16", 2816, BF16) for i in range(self.NSTG)]
        self.stg_i = 0
        self.bank_rr = 0
        self.zi = 0

    def next_wslot(self):
        i = self.wslot_i % 4
        self.wslot_i += 1
        return self.wslot[i], self.B("wslot%d" % i), "wslot%d" % i

    def cast_rows(self, src, ncols, stores, wname, ld_view=None):
        i = self.stg_i % self.NSTG
        self.stg_i += 1
        s32, s16 = self.stg32[i], self.stg16[i]
        b32, b16 = self.B("stg32_%d" % i), self.B("stg16_%d" % i)
        if ld_view is None:
            self.dma(s32[:, 0:ncols], src, [], [b32], "stg32_%d" % i)
        else:
            dv = ld_view(s32)
            n1 = dv.shape[1]
            step = 4
            parts = []
            for pi, c0 in enumerate(range(0, n1, step)):
                bp = self.B("stg32_%d_%d" % (i, pi))
                parts.append(bp)
                if pi == 0:
                    self.dma(dv[:, c0:min(c0 + step, n1), :], src[:, c0:min(c0 + step, n1), :], [], [bp, b32], "stg32_%d_%d" % (i, pi))
                else:
                    self.dma(dv[:, c0:min(c0 + step, n1), :], src[:, c0:min(c0 + step, n1), :], [b32], [bp], "stg32_%d_%d" % (i, pi))
            parts.append(b32)
            b32 = None
        eng = ("dve", "act", "dve")[self.cast_rr % 3]
        self.cast_rr += 1
        if b32 is None:
            self.copy(eng, s16[:, 0:ncols], s32[:, 0:ncols], parts, [b16])
        else:
            self.copy(eng, s16[:, 0:ncols], s32[:, 0:ncols], [b32], [b16])
        for dst, view in stores:
            self.dma(dst, view(s16), [b16], [self.B(wname)], "stg16_%d" % i, q="pool")

    def prepass_ffn(self, l, f):
        wg = self.P["w%d_gate" % (f + 1)][l]
        wu = self.P["w%d_up" % (f + 1)][l]
        wdn = self.P["w%d_down" % (f + 1)][l]
        nm = "W_ffn_%d_%d" % (l, f)
        for blk in range(11):
            for gu, w in enumerate((wg, wu)):
                src = w[:, blk * 256:(blk + 1) * 256].rearrange("(k p) c -> p k c", p=128)
                dst = self.wgu[l, f, blk][:, gu, :, :].rearrange("p k c -> p (k c)")
                self.cast_rows(src, 2048, [(dst, lambda s: s[:, 0:2048])], nm,
                               ld_view=lambda s: s[:, 0:2048].rearrange("p (k c) -> p k c", k=KT))
        for o in range(KT):
            src = wdn[:, o * 128:(o + 1) * 128].rearrange("(c p) m -> p c m", p=128)
            dst = self.wd[l, f, o].rearrange("p c m -> p (c m)")
            self.cast_rows(src, NFC * 128, [(dst, lambda s: s[:, 0:NFC * 128])], nm,
                           ld_view=lambda s: s[:, 0:NFC * 128].rearrange("p (c m) -> p c m", c=NFC))

    def stage_in(self):
        S, L = self.S, self.L
        xin, hT = self.xin, self.hT
        ps, B_ps = self.ps, self.B_ps
        ident = self.ident
        it = 0
        for q in range(self.n_seq):
            for (t0, nt) in sub128(L):
                sl = it % 2
                it += 1
                xt, bx = xin[sl], self.B("xin%d" % sl)
                if t0 == 0:
                    self.dma(xt[0:NMETA, :], self.meta[:, :], [], [bx], "xin%d" % sl)
                    self.dma(xt[NMETA:nt, :], self.x[q, 0:nt - NMETA, :], [], [bx], "xin%d" % sl)
                else:
                    self.dma(xt[0:nt, :], self.x[q, t0 - NMETA:t0 - NMETA + nt, :], [], [bx], "xin%d" % sl)
                ht, bh = hT[sl], self.B("hT%d" % sl)
                for half in range(2):
                    pb, bpb = ps[half], B_ps[half]
                    for j in range(4):
                        kt = half * 4 + j
                        S.add("pe", lambda e, pb=pb, xt=xt, kt=kt, j=j, nt=nt: e.transpose(
                            pb[:, j * 128:j * 128 + nt], xt[0:nt, kt * 128:(kt + 1) * 128], ident[0:nt, 0:nt]),
                            reads=[bx, self.B("ident")], writes=[bpb])
                    self.copy("dve" if half == 0 else "act", ht[:, half * 4:half * 4 + 4, 0:nt],
                              pb[:].rearrange("p (j t) -> p j t", j=4)[:, :, 0:nt], [bpb], [bh])
                self.dma(self.hs[q, :, :, t0:t0 + nt], ht[:, :, 0:nt], [bh], [self.B("hs%d" % q)], "hTst%d" % sl, q="pool")

    def rms_stats(self, h_t, B_h, n, pbi=2):
        S = self.S
        sq, rstd, ones_bf = self.sq, self.rstd, self.ones_bf
        B_sq, B_rstd, B_ones = self.B("sq"), self.B("rstd"), self.B("ones")
        pbank, B_pbank = self.ps[pbi], self.B_ps[pbi]
        S.add("act", lambda e: e.activation(out=sq[:, :, 0:n], in_=h_t[:, :, 0:n], func=AF.Square),
              reads=[B_h], writes=[B_sq])
        for kt in range(KT):
            self.mm(pbank[:, 0:n], ones_bf[:, :], sq[:, kt, 0:n], kt == 0, kt == KT - 1, [B_sq, B_ones], [B_pbank])
        S.add("act", lambda e: e.activation(out=rstd[:, 0:n], in_=pbank[:, 0:n], func=AF.Ln,
                                            scale=1.0 / D, bias=EPS), reads=[B_pbank], writes=[B_rstd])
        S.add("act", lambda e: e.activation(out=rstd[:, 0:n], in_=rstd[:, 0:n], func=AF.Exp, scale=-0.5),
              reads=[B_rstd], writes=[B_rstd])

    def norm_apply(self, h_t, B_h, gi, out_t, B_out, n):
        for kt in range(KT):
            self.S.add("dve", lambda e, kt=kt: e.scalar_tensor_tensor(
                out=out_t[:, kt, 0:n], in0=h_t[:, kt, 0:n], scalar=self.gvec[:, gi, kt:kt + 1], in1=self.rstd[:, 0:n],
                op0=ALU.mult, op1=ALU.mult), reads=[B_h, self.B("rstd"), self.B("gvec")], writes=[B_out])

    def alloc_ffn(self):
        cv = lambda nm, n, dt: _carve(self, "ffn", nm, n, dt)
        self.hF = [cv("hF", KT * TN, F32).rearrange("p (k t) -> p k t", k=KT) for i in range(4)]
        self.nF = [cv("nF", KT * TN, BF16).rearrange("p (k t) -> p k t", k=KT) for i in range(4)]
        self.ffn_set = 0
        self.hid = [cv("hid", NFC * TN, BF16).rearrange("p (k t) -> p k t", k=NFC) for i in range(2)]
        self.sil = [cv("sil", TN, F32) for i in range(3)]
        self.sil_i = 0
        self.yn = cv("yn", KT * TN, F32).rearrange("p (k t) -> p k t", k=KT)
        self.yo = [cv("yo", D, F32) for i in range(2)]
        self.xin = [cv("xin", D, F32) for i in range(2)]
        self.hT = [cv("hT", KT * 128, F32).rearrange("p (k t) -> p k t", k=KT) for i in range(2)]

    def ffn_load_norm(self, q, gi, tiles, st, do_load=True, do_norm=True):
        for j, ti in enumerate(tiles):
            jj = 2 * st + j
            hF, bhF = self.hF[jj], self.B("hF%d" % jj)
            if do_load:
                self.dma(hF[:], self.hs[q, :, :, ti * TN:(ti + 1) * TN], [self.B("hs%d" % q)], [bhF], "hF%d" % jj)
            if do_norm:
                self.rms_stats(hF, bhF, TN)
                self.norm_apply(hF, bhF, gi, self.nF[jj], self.B("nF%d" % jj), TN)

    def stage_ffn(self, q, l, f):
        S, NT = self.S, self.NT
        ps, B_ps = self.ps, self.B_ps
        gi = 1 + 3 * l + (0 if f == 0 else 2)
        wname = "W_ffn_%d_%d" % (l, f)
        tiles_all = list(range(NT))
        groups = [tiles_all[g0:g0 + 2] for g0 in range(0, NT, 2)]
        for gidx, tiles in enumerate(groups):
            st = self.ffn_set % 2
            if gidx == 0:
                self.ffn_load_norm(q, gi, tiles, st)
            self.ffn_set += 1
            mmi = 0
            for blk in range(11):
                wt, bw, wk = self.next_wslot()
                self.dma(wt[:, 0:4096], self.wgu[l, f, blk].rearrange("p g k c -> p (g k c)"),
                         [self.B(wname)], [bw], wk)
                wv = wt[:, 0:4096].rearrange("p (g k c) -> p g k c", g=2, k=KT)
                for j, ti in enumerate(tiles):
                    nF, bn = self.nF[2 * st + j], self.B("nF%d" % (2 * st + j))
                    for cc in range(2):
                        c = blk * 2 + cc
                        gi_, ui_ = (0, 1, 6)[mmi % 3], (2, 3, 7)[mmi % 3]
                        pg, bpg = ps[gi_], B_ps[gi_]
                        pu, bpu = ps[ui_], B_ps[ui_]
                        mmi += 1
                        for kt in range(KT):
                            self.mm(pg[:, 0:TN], wv[:, 0, kt, cc * 128:(cc + 1) * 128], nF[:, kt, :],
                                    kt == 0, kt == KT - 1, [bw, bn], [bpg])
                        for kt in range(KT):
                            self.mm(pu[:, 0:TN], wv[:, 1, kt, cc * 128:(cc + 1) * 128], nF[:, kt, :],
                                    kt == 0, kt == KT - 1, [bw, bn], [bpu])
                        si = self.sil_i % 3
                        self.sil_i += 1
                        sl_t, bsl = self.sil[si], self.B("sil%d" % si)
                        S.add("act", lambda e, sl_t=sl_t, pg=pg: e.activation(out=sl_t[:, :], in_=pg[:, 0:TN], func=AF.Silu),
                              reads=[bpg], writes=[bsl])
                        hid, bhid = self.hid[j], self.B("hid%d" % j)
                        S.add("dve", lambda e, hid=hid, c=c, sl_t=sl_t, pu=pu: e.tensor_tensor(
                            out=hid[:, c, :], in0=sl_t[:, :], in1=pu[:, 0:TN], op=ALU.mult),
                            reads=[bsl, bpu], writes=[bhid])
            if gidx + 1 < len(groups):
                self.ffn_load_norm(q, gi, groups[gidx + 1], 1 - st, do_norm=False)
            for o in range(KT):
                if o == 4 and gidx + 1 < len(groups):
                    self.ffn_load_norm(q, gi, groups[gidx + 1], 1 - st, do_load=False)
                wt, bw, wk = self.next_wslot()
                self.dma(wt[:, 0:NFC * 128], self.wd[l, f, o].rearrange("p c o -> p (c o)"),
                         [self.B(wname)], [bw], wk)
                wv = wt[:, 0:NFC * 128].rearrange("p (c o) -> p c o", c=NFC)
                for j, ti in enumerate(tiles):
                    hF, bhF = self.hF[2 * st + j], self.B("hF%d" % (2 * st + j))
                    hid, bhid = self.hid[j], self.B("hid%d" % j)
                    di_ = (4, 5, 6, 7)[mmi % 4]
                    pd, bpd = ps[di_], B_ps[di_]
                    mmi += 1
                    for c in range(NFC):
                        self.mm(pd[:, 0:TN], wv[:, c, :], hid[:, c, :],
                                c == 0, c == NFC - 1, [bw, bhid], [bpd])
                    S.add("dve", lambda e, hF=hF, o=o, pd=pd: e.scalar_tensor_tensor(
                        out=hF[:, o, :], in0=pd[:, 0:TN], scalar=0.5, in1=hF[:, o, :],
                        op0=ALU.mult, op1=ALU.add), reads=[bpd, bhF], writes=[bhF])
            for j, ti in enumerate(tiles):
                hF, bhF = self.hF[2 * st + j], self.B("hF%d" % (2 * st + j))
                self.dma(self.hs[q, :, :, ti * TN:(ti + 1) * TN], hF[:], [bhF], [self.B("hs%d" % q)], "hFst%d" % (2 * st + j), q="pool")

    def stage_final(self):
        S, NT = self.S, self.NT
        ps, B_ps = self.ps, self.B_ps
        hin = self.hF
        yn = self.yn
        B_yn = self.B("yn")
        yo = self.yo
        it = 0
        oi = 0
        for q in range(self.n_seq):
            for ti in range(NT):
                sl = it % 2
                it += 1
                t0 = ti * TN
                hi_, bhi = hin[sl], self.B("hF%d" % sl)
                self.dma(hi_[:], self.hs[q, :, :, t0:t0 + TN], [self.B("hs%d" % q)], [bhi], "hF%d" % sl)
                self.rms_stats(hi_, bhi, TN)
                self.norm_apply(hi_, bhi, 0, yn, B_yn, TN)
                for (s0, nt) in sub128(TN):
                    lo = max(t0 + s0, NMETA)
                    hi = t0 + s0 + nt
                    if hi <= lo:
                        continue
                    a0 = lo - (t0 + s0)
                    so = oi % 2
                    oi += 1
                    yt, byt = yo[so], self.B("yo%d" % so)
                    for half in range(2):
                        pb, bpb = ps[6 + half], B_ps[6 + half]
                        for j in range(4):
                            kt = half * 4 + j
                            S.add("pe", lambda e, pb=pb, kt=kt, j=j, s0=s0, nt=nt: e.transpose(
                                pb[0:nt, j * 128:(j + 1) * 128], yn[:, kt, s0:s0 + nt], self.ident[:, :]),
                                reads=[B_yn, self.B("ident")], writes=[bpb])
                        self.copy("dve" if half == 0 else "act", yt[0:nt, half * 512:half * 512 + 512], pb[0:nt, :], [bpb], [byt])
                    op = self.dma(self.out[q, lo - NMETA:hi - NMETA, :], yt[a0:nt, :], [byt], [], "yo%d" % so, q="pool")
                    self.final_ops.append(op)

    def finish(self):
        self.S.emit(final_waits=self.final_ops)
        self.es.close()
        return self.nc


NG = 32
NCH = TN // 8
NB = 3
HEADS = 8
MAGIC = 12582912.0
TWO_PI = 6.283185307179586


def _carve(self, region, name, nelems, dt):
    off = self.reg_off[region]
    nbytes = nelems * (4 if dt == F32 else 2)
    nbytes = (nbytes + 31) // 32 * 32
    self.reg_off[region] = off + nbytes
    assert off + nbytes <= self.ARENA_BYTES, (name, off + nbytes)
    a4 = self.arena[:, off // 4:(off + nbytes) // 4]
    v = a4 if dt == F32 else a4.bitcast(dt)
    return v[:, 0:nelems]


def _barrier(self):
    S = self.S
    lasts = [S.ops[e][-1] for e in S.ENGS if S.ops[e]]
    lasts += [ent[2] for ent in S.dma_sems.values() if ent[2] is not None]
    for e in S.ENGS:
        op = S.add(e, lambda eng: eng.nop())
        for p in lasts:
            S._dep(op, p)
            if p.eng == "pe" and e == "pe":
                pass


def _declare_mix(self):
    dd = max(self.depth, 1)
    for nm, shp in [("g_mix", [dd, D]), ("w_in", [dd, D, IN_W]), ("b_gate", [dd, 2 * D]), ("b_f", [dd, HEADS]),
                    ("ssm_a_re", [dd, NG, 64]), ("ssm_a_im", [dd, NG, 64]), ("ssm_log_dt", [dd, NG]),
                    ("ssm_b_re", [dd, NG, 64, 16]), ("ssm_b_im", [dd, NG, 64, 16]),
                    ("ssm_c_re", [dd, NG, 16, 64]), ("ssm_c_im", [dd, NG, 16, 64]), ("ssm_d", [dd, 512]),
                    ("w_glu", [dd, 512, 1024]), ("w_br_a", [dd, 512, 1024]), ("w_br_b", [dd, 512, 1024]),
                    ("w_o", [dd, D, D])]:
        self.P[nm] = self.din(nm, shp)
    self.c_sel = self.din("c_sel", [128, 64 * 128], BF16)
    self.c_mask8 = self.din("c_mask8", [128, 128])
    self.c_maskb = self.din("c_maskb", [128, NB * TN], BF16)
    self.c_et = self.din("c_et", [8, 3 * 8 * 132], BF16)
    self.c_one = self.din("c_one", [1, 256 + TN], BF16)
    self.c_jtab = self.din("c_jtab", [128, 2 * 17])
    self.c_identbf = self.din("c_identbf", [128, 128], BF16)
    self.wu_s = self.dscratch("wu_s", [dd, 128, KT, 512], BF16)
    self.wq_s = self.dscratch("wq_s", [dd, 128, KT, 1024], BF16)
    self.wk_s = self.dscratch("wk_s", [dd, 128, KT, 1024], BF16)
    self.wv_s = self.dscratch("wv_s", [dd, 128, KT, 512], BF16)
    self.wf_s = self.dscratch("wf_s", [dd, 128, KT, 8], BF16)
    self.wmrg_s = self.dscratch("wmrg_s", [dd, 8, 128, 3072], BF16)
    self.wglu_s = self.dscratch("wglu_s", [dd, 2, 128, 4, 512], BF16)
    self.wo_s = self.dscratch("wo_s", [dd, 2, 128, KT, 512], BF16)
    self.s5c_s = self.dscratch("s5c_s", [dd, 128, (3 * 8192 + 256) // 4], F32)
    self.s5_done = set()
    self.sel = self.sb("sel_sb", [128, 64, 128], BF16)
    self.mask8 = self.sb("mask8_sb", [128, 128], F32)
    self.maskb = self.sb("maskb_sb", [128, NB, TN], BF16)
    self.et = self.sb("et_sb", [8, 3, 8, 132], BF16)
    self.one = self.sb("one_sb", [1, 256 + TN], BF16)
    self.jtab = self.sb("jtab_sb", [128, 2, 17], F32)
    self.identbf = self.sb("identbf_sb", [128, 128], BF16)
    self.bgate = self.sb("bgate_sb", [128, dd, 16], F32)
    self.bfneg = self.sb("bfneg_sb", [8, dd], F32)
    bc = self.B("mconst")
    self.dma(self.sel[:].rearrange("p a b -> p (a b)"), self.c_sel[:], [], [bc], "const")
    self.dma(self.mask8[:], self.c_mask8[:], [], [bc], "const")
    self.dma(self.maskb[:].rearrange("p a b -> p (a b)"), self.c_maskb[:], [], [bc], "const")
    self.dma(self.et[:].rearrange("p a b c -> p (a b c)"), self.c_et[:], [], [bc], "const")
    self.dma(self.one[:], self.c_one[:], [], [bc], "const")
    self.dma(self.jtab[:].rearrange("p a b -> p (a b)"), self.c_jtab[:], [], [bc], "const")
    self.dma(self.identbf[:], self.c_identbf[:], [], [bc], "const")
    for l in range(self.depth):
        self.dma(self.gvec[:, 2 + 3 * l, :], self.P["g_mix"][l].rearrange("(k p) -> p k", p=128), [], [self.B("gvec")], "const", slow=True)
        self.dma(self.bgate[:, l, :], self.P["b_gate"][l].rearrange("(k p) -> p k", p=128), [], [bc], "const", slow=True)
        self.dma(self.bfneg[:, l:l + 1], self.P["b_f"][l].rearrange("(h o) -> h o", o=1), [], [bc], "const", slow=True)
    self.S.add("dve", lambda e: e.tensor_scalar(out=self.bfneg[:], in0=self.bfneg[:], scalar1=-1.0, scalar2=None, op0=ALU.mult),
               reads=[bc], writes=[bc])


def _prepass_mix(self, l):
    nm = "W_mix_%d" % l
    w_in = self.P["w_in"][l]
    for kt in range(KT):
        rows = slice(kt * 128, (kt + 1) * 128)
        i = self.stg_i % self.NSTG
        self.stg_i += 1
        s32, s16 = self.stg32[i], self.stg16[i]
        b32, b16 = self.B("stg32_%d" % i), self.B("stg16_%d" % i)
        self.dma(s32[:, 0:2056], w_in[rows, 0:2056], [], [b32], "stg32_%d" % i)
        j = self.stg_i % self.NSTG
        self.stg_i += 1
        s16b, b16b = self.stg16[j], self.B("stg16_%d" % j)
        self.S.add("dve", lambda e, s16b=s16b: e.memset(s16b[:, 0:2048], 0.0), writes=[b16b])
        self.copy("dve", s16[:, 0:512], s32[:, 0:512], [b32], [b16])
        self.copy("act", s16[:, 512:1032], s32[:, 1536:2056], [b32], [b16])
        self.copy("act", s16b[:, 0:1024].rearrange("p (h c) -> p h c", c=128)[:, :, 0:64],
                  s32[:, 512:1024].rearrange("p (h c) -> p h c", c=64), [b32], [b16b])
        self.copy("dve", s16b[:, 1024:2048].rearrange("p (h c) -> p h c", c=128)[:, :, 0:64],
                  s32[:, 1024:1536].rearrange("p (h c) -> p h c", c=64), [b32], [b16b])
        k_ = "stg16_%d" % i
        self.dma(self.wu_s[l, :, kt, :], s16[:, 0:512], [b16], [self.B(nm)], k_, q="pool")
        self.dma(self.wv_s[l, :, kt, :], s16[:, 512:1024], [b16], [self.B(nm)], k_, q="pool")
        self.dma(self.wf_s[l, :, kt, :], s16[:, 1024:1032], [b16], [self.B(nm)], k_, q="pool")
        self.dma(self.wq_s[l, :, kt, :], s16b[:, 0:1024], [b16b], [self.B(nm)], "stg16_%d" % j, q="pool")
        self.dma(self.wk_s[l, :, kt, :], s16b[:, 1024:2048], [b16b], [self.B(nm)], "stg16_%d" % j, q="pool")
        mrg = self.wmrg_s[l]
        stores = []
        for ab in range(2):
            dst = mrg[:, :, ab * 1024 + kt * 128: ab * 1024 + (kt + 1) * 128].rearrange("m p c -> p m c")
            stores.append((dst, (lambda s, ab=ab: s[:, ab * 1024:(ab + 1) * 1024].rearrange("p (m c) -> p m c", c=128))))
        self.cast_rows(w_in[rows, 2056:4104], 2048, stores, nm)
    for kt in range(4):
        dst = self.wglu_s[l][:, :, kt, :].rearrange("b p c -> p b c")
        self.cast_rows(self.P["w_glu"][l][kt * 128:(kt + 1) * 128, :], 1024,
                       [(dst, lambda s: s[:, 0:1024].rearrange("p (b c) -> p b c", c=512))], nm)
    for bi, src in enumerate((self.P["w_br_a"][l], self.P["w_br_b"][l])):
        for kt in range(4):
            off = 2048 + bi * 512 + kt * 128
            dst = self.wmrg_s[l][:, :, off:off + 128].rearrange("m p c -> p m c")
            self.cast_rows(src[kt * 128:(kt + 1) * 128, :], 1024,
                           [(dst, lambda s: s[:, 0:1024].rearrange("p (m c) -> p m c", c=128))], nm)
    for kt in range(KT):
        dst = self.wo_s[l][:, :, kt, :].rearrange("b p c -> p b c")
        self.cast_rows(self.P["w_o"][l][kt * 128:(kt + 1) * 128, :], 1024,
                       [(dst, lambda s: s[:, 0:1024].rearrange("p (b c) -> p b c", c=512))], nm)


def _alloc_mix(self):
    L, NT = self.L, self.NT
    c = lambda reg, nm, n, dt: _carve(self, reg, nm, n, dt)
    self.reg_off["mixP"] = 0
    self.Kc = c("mixP", "Kc", HEADS * L, BF16).rearrange("p (h t) -> p h t", h=HEADS)
    self.Vc = c("mixP", "Vc", NT * NB * HEADS * 65, BF16).rearrange("p (b h d) -> p b h d", h=HEADS, d=65)
    self.s5_off = self.reg_off["mixP"]
    self.W1 = c("mixP", "W1", NG * 128, BF16).rearrange("p (g m) -> p g m", g=NG)
    self.W2 = c("mixP", "W2", 16 * 2 * 128, BF16).rearrange("p (g r m) -> p g r m", g=16, r=2)
    self.W3 = c("mixP", "W3", NG * 2 * 64, BF16).rearrange("p (g r m) -> p g r m", g=NG, r=2)
    self.A8 = c("mixP", "A8", 2 * 2 * 16, F32).rearrange("p (a r g) -> p a r g", a=2, r=2)
    self.Z = [c("mixP", "Z%d" % i, 3 * 16, F32).rearrange("p (r g) -> p r g", r=3) for i in range(2)]
    self.Gk = [c("mixP", "Gk%d" % i, TN, F32) for i in range(2)]
    self.wf = c("mixP", "wf", KT * 8, BF16).rearrange("p (k c) -> p k c", k=KT)
    self.ones8 = c("mixP", "ones8", TN, F32)
    self.dsk = c("mixP", "dsk", NG, F32)
    self.reg_off["mixT"] = self.reg_off["mixP"]
    self.reg_off["mixS"] = self.reg_off["mixP"]
    self.hM = c("mixT", "hM", KT * TN, F32).rearrange("p (k t) -> p k t", k=KT)
    self.nM = c("mixT", "nM", KT * TN, BF16).rearrange("p (k t) -> p k t", k=KT)
    self.Qa = c("mixT", "Qa", KT * TN, BF16).rearrange("p (k t) -> p k t", k=KT)
    self.uT = c("mixT", "uT", 4 * TN, BF16).rearrange("p (k t) -> p k t", k=4)
    self.U = c("mixT", "U", NG * NCH, BF16).rearrange("p (g c) -> p g c", g=NG)
    self.Ssb = c("mixT", "Ssb", 2 * 16 * NCH, F32).rearrange("p (g r c) -> p g r c", r=2, g=16)
    self.Ssb_gr = self.Ssb.rearrange("p g r c -> p (g r) c")
    self.yaT = self.U.rearrange("p g c -> p (g c)").rearrange("p (k t) -> p k t", k=4)
    self.Hbf = c("mixT", "Hbf", 2 * 16 * NCH, BF16).rearrange("p (r g c) -> p r g c", r=2, g=16)
    self.Ybf = c("mixT", "Ybf", NG * NCH, BF16).rearrange("p (g c) -> p g c", g=NG)
    self.ybtok = c("mixT", "ybtok", NB * 512, F32).rearrange("p (b f) -> p b f", b=NB)
    self.ybT = c("mixT", "ybT", 4 * TN, BF16).rearrange("p (k t) -> p k t", k=4)
    self.Pt = [c("mixT", "Pt%d" % i, TN, BF16) for i in range(5)]
    self.gat = [c("mixT", "gat%d" % i, TN, F32) for i in range(2)]
    self.t12 = [c("mixT", "t12_%d" % i, TN, F32) for i in range(2)]
    self.fl = c("mixT", "fl", TN, F32)
    self.gsp = c("mixT", "gsp", 3 * TN, BF16).rearrange("p (j t) -> p j t", j=3)
    self.gr = c("mixT", "gr", TN, F32)
    self.rec = c("mixT", "rec", 8, F32)
    self.m12 = [c("mixT", "m12_%d" % i, 2 * 16, F32).rearrange("p (r g) -> p r g", r=2) for i in range(2)]
    self.ytmp = [self.Ssb.rearrange("p g r c -> p (g r c)")[:, i * TN:(i + 1) * TN] for i in range(4)]
    self.s_lr = c("mixS", "lr", 16, F32)
    self.s_li = c("mixS", "li", 16, F32)
    self.s_dt = c("mixS", "dt", 16, F32)
    self.s_lrd = c("mixS", "lrd", 16, F32)
    self.s_lid = c("mixS", "lid", 16, F32)
    self.s_t = [c("mixS", "st%d" % i, 16 * 2 * 17, F32).rearrange("p (g a j) -> p g a j", g=16, a=2) for i in range(4)]
    self.s_Ere = c("mixS", "Ere", 16 * 2 * 17, F32).rearrange("p (g a j) -> p g a j", g=16, a=2)
    self.s_Eim = c("mixS", "Eim", 16 * 2 * 17, F32).rearrange("p (g a j) -> p g a j", g=16, a=2)
    self.s_sm = [c("mixS", "sm%d" % i, 16, F32) for i in range(6)]
    self.s_b = [c("mixS", "b%d" % i, 16 * 16, F32).rearrange("p (g c) -> p g c", g=16) for i in range(2)]
    self.s_Bb = [c("mixS", "Bb%d" % i, 16 * 16, F32).rearrange("p (g c) -> p g c", g=16) for i in range(2)]
    self.s_cn = [c("mixS", "cn%d" % i, 128, F32) for i in range(2)]
    self.s_c = [c("mixS", "c%d" % i, 16 * 16, F32).rearrange("p (g c) -> p g c", g=16) for i in range(2)]
    self.s_q = [c("mixS", "q%d" % i, 4 * 128, F32).rearrange("p (g t c) -> p g t c", g=4, t=8) for i in range(8)]
    self.s_w1t = c("mixS", "w1t", 128, F32)


def _mix_setup(self, l):
    S = self.S
    P = self.P
    bs = self.B("s5setup")
    bt = self.B("s5tab")
    V, Sc, G = "dve", "act", "pool"
    tt = lambda o, a, b, op, eng="dve": S.add(eng, lambda e: e.tensor_tensor(out=o, in0=a, in1=b, op=op), reads=[bs, self.B("mconst")], writes=[bs])
    ts = lambda o, a, s1, s2, op0, op1=None: S.add("dve", (lambda e: e.tensor_scalar(out=o, in0=a, scalar1=s1, scalar2=s2, op0=op0, op1=op1)) if op1 is not None else
                                                   (lambda e: e.tensor_scalar(out=o, in0=a, scalar1=s1, scalar2=None, op0=op0)), reads=[bs], writes=[bs])
    act = lambda o, a, f, **kw: S.add("act", lambda e: e.activation(out=o, in_=a, func=f, **kw), reads=[bs], writes=[bs])
    for hg in range(2):
        pr = slice(64 * hg, 64 * hg + 64)
        gs = slice(16 * hg, 16 * hg + 16)
        self.dma(self.s_lr[pr, :], P["ssm_a_re"][l][gs, :].rearrange("g p -> p g"), [], [bs], "s5ld", slow=True)
        self.dma(self.s_li[pr, :], P["ssm_a_im"][l][gs, :].rearrange("g p -> p g"), [], [bs], "s5ld", slow=True)
        self.dma(self.s_dt[pr, :], P["ssm_log_dt"][l][gs].partition_broadcast(64), [], [bs], "s5ld", slow=True)
        self.dma(self.s_b[0][pr, :, :], P["ssm_b_re"][l][gs].rearrange("g p c -> p g c"), [], [bs], "s5ld")
        self.dma(self.s_b[1][pr, :, :], P["ssm_b_im"][l][gs].rearrange("g p c -> p g c"), [], [bs], "s5ld")
    for t in range(8):
        self.dma(self.dsk[16 * t:16 * t + 16, :], P["ssm_d"][l].rearrange("(g c) -> c g", c=16), [], [bt], "s5ld", slow=True)
    for ri, nm in enumerate(("ssm_c_re", "ssm_c_im")):
        for half8 in range(2):
            cn = self.s_cn[ri]
            for hg in range(2):
                g0 = 16 * hg + 8 * half8
                self.dma(cn[:, 64 * hg:64 * hg + 64], P[nm][l][g0:g0 + 8].rearrange("g c p -> (g c) p"), [], [bs], "s5ld")
            pb, bpb = self.ps[5], self.B_ps[5]
            S.add("pe", lambda e, pb=pb, cn=cn: e.transpose(pb[:, 0:128], cn[:, :], self.ident[:, :]),
                  reads=[bs, self.B("ident")], writes=[bpb])
            self.copy("dve", self.s_c[ri][:, 8 * half8:8 * half8 + 8, :], pb[:, 0:128].rearrange("p (g c) -> p g c", g=8), [bpb], [bs])
    act(self.s_dt[:, :], self.s_dt[:, :], AF.Exp)
    tt(self.s_lrd[:, :], self.s_lr[:, :], self.s_dt[:, :], ALU.mult)
    tt(self.s_lid[:, :], self.s_li[:, :], self.s_dt[:, :], ALU.mult)
    jt = self.jtab[:, :, :].unsqueeze(1).broadcast_to([128, 16, 2, 17])
    bc3 = lambda a: a.unsqueeze(2).unsqueeze(3).broadcast_to([128, 16, 2, 17])
    t0, t1, t2, t3 = self.s_t
    tt(t0[:], bc3(self.s_lrd[:, :]), jt, ALU.mult)
    act(t0[:], t0[:], AF.Exp)
    tt(t1[:], bc3(self.s_lid[:, :]), jt, ALU.mult)
    for (dst, shift) in ((self.s_Eim, 0.0), (self.s_Ere, 1.5707963267948966)):
        if shift != 0.0:
            ts(t2[:], t1[:], shift, None, ALU.add)
            src = t2
        else:
            src = t1
        ts(t3[:], src[:], 1.0 / TWO_PI, MAGIC, ALU.mult, ALU.add)
        ts(t3[:], t3[:], -MAGIC, None, ALU.add)
        S.add("dve", lambda e, src=src: e.scalar_tensor_tensor(out=t3[:], in0=t3[:], scalar=-TWO_PI, in1=src[:], op0=ALU.mult, op1=ALU.add),
              reads=[bs], writes=[bs])
        ts(t3[:], t3[:], 3.14159, -3.14159, ALU.min, ALU.max)
        act(dst[:], t3[:], AF.Sin)
        tt(dst[:], dst[:], t0[:], ALU.mult)
    Ere, Eim = self.s_Ere, self.s_Eim
    sm = self.s_sm
    ts(sm[0][:, :], Ere[:, :, 0, 9], -1.0, None, ALU.add)
    tt(sm[1][:, :], self.s_lr[:, :], self.s_lr[:, :], ALU.mult)
    tt(sm[2][:, :], self.s_li[:, :], self.s_li[:, :], ALU.mult)
    tt(sm[1][:, :], sm[1][:, :], sm[2][:, :], ALU.add)
    S.add("dve", lambda e: e.reciprocal(out=sm[1][:, :], in_=sm[1][:, :]), reads=[bs], writes=[bs])
    tt(sm[2][:, :], sm[0][:, :], self.s_lr[:, :], ALU.mult)
    tt(sm[3][:, :], Eim[:, :, 0, 9], self.s_li[:, :], ALU.mult)
    tt(sm[2][:, :], sm[2][:, :], sm[3][:, :], ALU.add)
    tt(sm[2][:, :], sm[2][:, :], sm[1][:, :], ALU.mult)
    tt(sm[3][:, :], Eim[:, :, 0, 9], self.s_lr[:, :], ALU.mult)
    tt(sm[4][:, :], sm[0][:, :], self.s_li[:, :], ALU.mult)
    tt(sm[3][:, :], sm[3][:, :], sm[4][:, :], ALU.subtract)
    tt(sm[3][:, :], sm[3][:, :], sm[1][:, :], ALU.mult)
    bcc = lambda a: a.unsqueeze(2).broadcast_to([128, 16, 16])
    bre, bim = self.s_b
    Bbr, Bbi = self.s_Bb
    q = self.s_q
    tt(q[0].rearrange("p g t c -> p (g t c)")[:, 0:256].rearrange("p (g c) -> p g c", g=16), bcc(sm[2][:, :]), bre[:], ALU.mult)
    tmpA = q[0].rearrange("p g t c -> p (g t c)")[:, 0:256].rearrange("p (g c) -> p g c", g=16)
    tmpB = q[0].rearrange("p g t c -> p (g t c)")[:, 256:512].rearrange("p (g c) -> p g c", g=16)
    tt(tmpB, bcc(sm[3][:, :]), bim[:], ALU.mult)
    tt(Bbr[:], tmpA, tmpB, ALU.subtract)
    tt(tmpA, bcc(sm[2][:, :]), bim[:], ALU.mult)
    tt(tmpB, bcc(sm[3][:, :]), bre[:], ALU.mult)
    tt(Bbi[:], tmpA, tmpB, ALU.add)
    S.add("dve", lambda e: e.tensor_copy(out=self.A8[:, 0, 0, :], in_=Ere[:, :, 0, 16]), reads=[bs], writes=[bt])
    S.add("dve", lambda e: e.tensor_copy(out=self.A8[:, 0, 1, :], in_=Ere[:, :, 0, 16]), reads=[bs], writes=[bt])
    S.add("dve", lambda e: e.tensor_scalar(out=self.A8[:, 1, 0, :], in0=Eim[:, :, 0, 16], scalar1=-1.0, scalar2=None, op0=ALU.mult), reads=[bs], writes=[bt])
    S.add("dve", lambda e: e.tensor_copy(out=self.A8[:, 1, 1, :], in_=Eim[:, :, 0, 16]), reads=[bs], writes=[bt])
    cre, cim = self.s_c
    for qq in range(4):
        g4 = slice(4 * qq, 4 * qq + 4)
        shp = [128, 4, 8, 16]
        bE = lambda E, a, j0: E[:, g4, a, j0:j0 + 8].unsqueeze(3).broadcast_to(shp)
        bX = lambda X: X[:, g4, :].unsqueeze(2).broadcast_to(shp)
        W3r, W3i, CNr, CNi, CAr, CAi, ta, tb = q
        tt(ta[:], bE(Ere, 1, 1), bX(Bbr), ALU.mult); tt(tb[:], bE(Eim, 1, 1), bX(Bbi), ALU.mult); tt(W3r[:], ta[:], tb[:], ALU.subtract)
        tt(ta[:], bE(Ere, 1, 1), bX(Bbi), ALU.mult); tt(tb[:], bE(Eim, 1, 1), bX(Bbr), ALU.mult); tt(W3i[:], ta[:], tb[:], ALU.add)
        for (Xr, Xi, j0) in ((CNr, CNi, 1), (CAr, CAi, 9)):
            tt(ta[:], bE(Ere, 0, j0), bX(cre), ALU.mult); tt(tb[:], bE(Eim, 0, j0), bX(cim), ALU.mult); tt(Xr[:], ta[:], tb[:], ALU.subtract)
            tt(ta[:], bE(Eim, 0, j0), bX(cre), ALU.mult); tt(tb[:], bE(Ere, 0, j0), bX(cim), ALU.mult)
            S.add("dve", lambda e, Xi=Xi: e.scalar_tensor_tensor(out=Xi[:], in0=ta[:], scalar=-1.0, in1=tb[:], op0=ALU.mult, op1=ALU.subtract),
                  reads=[bs], writes=[bs])
        for hg in range(2):
            pr = slice(64 * hg, 64 * hg + 64)
            for gq in range(4):
                gp = 4 * qq + gq
                g = 16 * hg + gp
                f2 = lambda X: X[pr, gq, :, :].rearrange("p t c -> p (t c)")
                self.copy("act", self.W2[pr, gp, 0, :], f2(CAr), [bs], [bt])
                self.copy("act", self.W2[pr, gp, 1, :], f2(CAi), [bs], [bt])
                pb, bpb = self.ps[5 + g % 2], self.B_ps[5 + g % 2]
                self.mm(pb[:, 0:128], f2(W3r), f2(CNr), True, False, [bs], [bpb])
                self.mm(pb[:, 0:128], f2(W3i), f2(CNi), False, True, [bs], [bpb])
                S.add("dve", lambda e, pb=pb: e.tensor_tensor(out=self.s_w1t[:, :], in0=pb[:, 0:128], in1=self.mask8[:, :], op=ALU.mult),
                      reads=[bpb, self.B("mconst")], writes=[self.B("w1t")])
                S.add("dve", lambda e, g=g: e.scalar_tensor_tensor(out=self.W1[:, g, :], in0=self.ident[:, :], scalar=self.dsk[:, g:g + 1],
                                                                     in1=self.s_w1t[:, :], op0=ALU.mult, op1=ALU.add),
                      reads=[self.B("w1t"), bt, self.B("ident")], writes=[bt])
                pw, bpw = self.ps[7], self.B_ps[7]
                S.add("pe", lambda e, pw=pw, a=f2(W3r), pr=pr: e.transpose(pw[:, 0:64], a, self.ident[pr, pr]), reads=[bs, self.B("ident")], writes=[bpw])
                S.add("pe", lambda e, pw=pw, a=f2(W3i), pr=pr: e.transpose(pw[:, 64:128], a, self.ident[pr, pr]), reads=[bs, self.B("ident")], writes=[bpw])
                self.copy("act", self.W3[:, g, :, :], pw[:, 0:128].rearrange("p (r m) -> p r m", r=2), [bpw], [bt])


Kern.declare_mix = _declare_mix
Kern.prepass_mix = _prepass_mix
Kern.alloc_mix = _alloc_mix
Kern.mix_setup = _mix_setup
Kern.barrier = _barrier


def _load_w(self, src_ap, n, wname, full=False):
    wt, bw, wk = self.next_wslot()
    self.dma(wt[:, 0:n], src_ap, [self.B(wname)], [bw], wk)
    return (wt[:, :] if full else wt[:, 0:n]), bw


def _nextb(self, pool=(5, 6, 7, 0, 1, 3, 4)):
    i = pool[self.bank_rr % len(pool)]
    self.bank_rr += 1
    return self.ps[i], self.B_ps[i]


def _mix_tile(self, q, l, ti):
    S = self.S
    L, NT = self.L, self.NT
    wname = "W_mix_%d" % l
    t0 = ti * TN
    first = (ti == 0)
    bh, bn, bqa, but, bU = self.B("hM"), self.B("nM"), self.B("Qa"), self.B("uT"), self.B("U")
    bKc, bVc, bt = self.B("Kc"), self.B("Vc"), self.B("s5tab")
    bmc = self.B("mconst")
    hM, nM, Qa, uT, U = self.hM, self.nM, self.Qa, self.uT, self.U
    evr = [0]

    def evac(out, in_, reads, writes, scale=None):
        evr[0] += 1
        if evr[0] % 2 == 0:
            if scale is None:
                return S.add("act", lambda e: e.activation(out=out, in_=in_, func=AF.Copy), reads=reads, writes=writes)
            return S.add("act", lambda e: e.activation(out=out, in_=in_, func=AF.Copy, scale=scale), reads=reads, writes=writes)
        if scale is None:
            return S.add("dve", lambda e: e.tensor_copy(out=out, in_=in_), reads=reads, writes=writes)
        return S.add("dve", lambda e: e.tensor_scalar(out=out, in0=in_, scalar1=scale, scalar2=None, op0=ALU.mult), reads=reads, writes=writes)

    pre_q = [_load_w(self, self.wq_s[l][:, :, hh * 256:(hh + 1) * 256], KT * 256, wname, full=True) for hh in range(2)]
    self.dma(hM[:], self.hs[q, :, :, t0:t0 + TN], [self.B("hs%d" % q)], [bh], "hM")
    self.rms_stats(hM, bh, TN)
    self.norm_apply(hM, bh, 2 + 3 * l, nM, bn, TN)

    pf, bpf = _nextb(self)
    for kt in range(KT):
        self.mm(pf[0:8, 0:TN], self.wf[:, kt, :], nM[:, kt, :], kt == 0, kt == KT - 1, [bt, bn], [bpf])
    bfl, bG = self.B("fl"), self.B("Gk")
    fl, gsp, gr = self.fl, self.gsp, self.gr
    S.add("act", lambda e: e.activation(out=fl[0:8, :], in_=pf[0:8, 0:TN], func=AF.Exp, scale=-1.0, bias=self.bfneg[:, l:l + 1]),
          reads=[bpf, bmc], writes=[bfl])
    S.add("act", lambda e: e.activation(out=fl[0:8, :], in_=fl[0:8, :], func=AF.Ln, bias=1.0), reads=[bfl], writes=[bfl])
    Gc, Gp = self.Gk[ti % 2], self.Gk[(ti + 1) % 2]
    if first:
        S.add("dve", lambda e: e.tensor_tensor_scan(out=Gc[0:8, :], data0=self.ones8[0:8, :], data1=fl[0:8, :], initial=0.0,
                                                    op0=ALU.mult, op1=ALU.add), reads=[bfl, bt], writes=[bG])
    else:
        S.add("dve", lambda e: e.tensor_tensor_scan(out=Gc[0:8, :], data0=self.ones8[0:8, :], data1=fl[0:8, :],
                                                    initial=Gp[0:8, TN - 1:TN], op0=ALU.mult, op1=ALU.add), reads=[bfl, bG, bt], writes=[bG])
    bgs = self.B("gsp")
    S.add("dve", lambda e: e.tensor_copy(out=gsp[0:8, 0, :], in_=Gc[0:8, :]), reads=[bG], writes=[bgs])
    S.add("dve", lambda e: e.tensor_tensor(out=gr[0:8, :], in0=Gc[0:8, :], in1=gsp[0:8, 0, :], op=ALU.subtract), reads=[bG, bgs], writes=[bfl])
    S.add("dve", lambda e: e.tensor_copy(out=gsp[0:8, 1, :], in_=gr[0:8, :]), reads=[bfl], writes=[bgs])
    S.add("dve", lambda e: e.tensor_tensor(out=gr[0:8, :], in0=gr[0:8, :], in1=gsp[0:8, 1, :], op=ALU.subtract), reads=[bfl, bgs], writes=[bfl])
    S.add("dve", lambda e: e.tensor_copy(out=gsp[0:8, 2, :], in_=gr[0:8, :]), reads=[bfl], writes=[bgs])

    one = self.one
    for which in range(2):
        wsrc = (self.wq_s if which == 0 else self.wk_s)[l]
        E = self.et[:, :, :, 4:132] if which == 0 else self.et[:, :, :, 0:128]
        onerow = one[0:1, 0:128] if which == 0 else one[0:1, 128:256]
        for hh in range(4):
            if which == 0 and hh < 2:
                wv_, bw = pre_q[hh]
            else:
                wv_, bw = _load_w(self, wsrc[:, :, hh * 256:(hh + 1) * 256], KT * 256, wname, full=True)
            for h4 in range(2):
                h = hh * 2 + h4
                pb, bpb = _nextb(self)
                for kt in range(KT):
                    c0 = kt * 256 + h4 * 128
                    self.mm(pb[:, 0:TN], wv_[:, c0:c0 + 128], nM[:, kt, :], kt == 0, False, [bw, bn], [bpb])
                for j in range(3):
                    self.mm(pb[:, 0:TN], E[0:8, j, h, :], gsp[0:8, j, :], False, False, [bmc, bgs], [bpb])
                self.mm(pb[:, 0:TN], onerow, one[0:1, 256:256 + TN], False, True, [bmc], [bpb])
                if which == 0:
                    evac(Qa[0:71, h, :], pb[0:71, 0:TN], [bpb], [bqa], scale=0.125)
                else:
                    evac(self.Kc[0:71, h, t0:t0 + TN], pb[0:71, 0:TN], [bpb], [bKc])
    wv_, bw = _load_w(self, self.wv_s[l].rearrange("p k c -> p (k c)"), KT * 512, wname)
    wv_ = wv_.rearrange("p (k c) -> p k c", k=KT)
    for jb, (s0, nt) in enumerate(sub128(TN)):
        pb, bpb = _nextb(self)
        for kt in range(KT):
            self.mm(pb[0:nt, 0:512], nM[:, kt, s0:s0 + nt], wv_[:, kt, :], kt == 0, kt == KT - 1, [bw, bn], [bpb])
        evac(self.Vc[0:nt, ti * NB + jb, :, 0:64], pb[0:nt, 0:512].rearrange("p (h d) -> p h d", h=HEADS), [bpb], [bVc])
    wv_, bw = _load_w(self, self.wu_s[l].rearrange("p k c -> p (k c)"), KT * 512, wname)
    wv_ = wv_.rearrange("p (k c) -> p k c", k=KT)
    for j in range(4):
        pb, bpb = _nextb(self)
        for kt in range(KT):
            self.mm(pb[:, 0:TN], wv_[:, kt, j * 128:(j + 1) * 128], nM[:, kt, :], kt == 0, kt == KT - 1, [bw, bn], [bpb])
        evac(uT[:, j, :], pb[:, 0:TN], [bpb], [but])

    grp_banks = [(0, 11), (11, 22), (22, 32)]
    for bi, (ga, gb) in enumerate(grp_banks):
        pb, bpb = self.ps[5 + bi], self.B_ps[5 + bi]
        for g in range(ga, gb):
            j, gl = g // 8, g % 8
            for s in range(8):
                self.mm(pb[:, (g - ga) * NCH:(g - ga + 1) * NCH], self.sel[:, gl * 8 + s, :], uT[:, j, s:TN:8],
                        s == 0, s == 7, [bmc, but], [bpb])
        evac(U[:, ga:gb, :], pb[:, 0:(gb - ga) * NCH].rearrange("p (g c) -> p g c", c=NCH), [bpb], [bU])
    bS, bH = self.B("Ssb"), self.B("Hbf")
    Ssb, Hbf = self.Ssb, self.Hbf
    blk_banks = [(0, 11), (11, 22), (22, 32)]
    for bi, (ba, bb) in enumerate(blk_banks):
        pb, bpb = self.ps[5 + bi], self.B_ps[5 + bi]
        for hg in range(2):
            pr = slice(64 * hg, 64 * hg + 64)
            for blk in range(ba, bb):
                gp, ri = blk // 2, blk % 2
                g = 16 * hg + gp
                self.mm(pb[pr, (blk - ba) * NCH:(blk - ba + 1) * NCH], self.W3[:, g, ri, :], U[:, g, :], True, True, [bt, bU], [bpb])
        evac(self.Ssb_gr[:, ba:bb, :], pb[:, 0:(bb - ba) * NCH].rearrange("p (b c) -> p b c", c=NCH), [bpb], [bS])
    bZ = self.B("Z")
    A8 = self.A8
    if first:
        S.add("pool", lambda e: e.memset(self.Z[0][:], 0.0), writes=[bZ])
    zi = self.zi
    for c in range(NCH):
        Zc, Zn = self.Z[zi % 2], self.Z[(zi + 1) % 2]
        zi += 1
        m1, m2 = self.m12
        bm = self.B("m12")
        S.add("pool", lambda e, Zc=Zc, c=c: e.tensor_copy(out=Hbf[:, :, :, c], in_=Zc[:, 0:2, :]), reads=[bZ], writes=[bH], chain=True)
        S.add("pool", lambda e, Zc=Zc: e.tensor_tensor(out=m1[:], in0=A8[:, 0, :, :], in1=Zc[:, 0:2, :], op=ALU.mult), reads=[bZ, bt], writes=[bm], chain=True)
        S.add("pool", lambda e, Zc=Zc: e.tensor_tensor(out=m2[:], in0=A8[:, 1, :, :], in1=Zc[:, 1:3, :], op=ALU.mult), reads=[bZ, bt], writes=[bm], chain=True)
        S.add("pool", lambda e: e.tensor_tensor(out=m1[:], in0=m1[:], in1=m2[:], op=ALU.add), reads=[bm], writes=[bm], chain=True)
        S.add("pool", lambda e, Zn=Zn, c=c: e.tensor_tensor(out=Zn[:, 0:2, :], in0=m1[:], in1=Ssb[:, :, :, c].rearrange("p g r -> p r g"), op=ALU.add), reads=[bm, bS], writes=[bZ], chain=True)
        S.add("pool", lambda e, Zn=Zn: e.tensor_copy(out=Zn[:, 2, :], in_=Zn[:, 0, :]), reads=[bZ], writes=[bZ], chain=True)
    self.zi = zi
    ybtok, ybT = self.ybtok, self.ybT
    bybt, bybT = self.B("ybtok"), self.B("ybT")
    qsubs = sub128(TN)
    nkb = (ti + 1) * NB
    pti = 0
    for h in range(HEADS):
        ob0 = 2 if h % 2 == 0 else 5
        Ob = [(self.ps[ob0 + i], self.B_ps[ob0 + i]) for i in range(3)]
        for kb in range(nkb):
            kti, kj = kb // NB, kb % NB
            ks, nk = kti * TN + kj * 128, (128 if kj < 2 else TN - 256)
            diag = (kti == ti)
            pS, bpS = self.ps[kb % 2], self.B_ps[kb % 2]
            mk = nk
            self.mm(pS[0:mk, 0:TN], self.Kc[0:71, h, ks:ks + mk], Qa[0:71, h, :], True, not diag, [bKc, bqa], [bpS])
            if diag:
                self.mm(pS[0:mk, 0:TN], self.identbf[0:nk, 0:mk], self.maskb[0:nk, kj, :], False, True, [bmc], [bpS])
            Pt, bPt = self.Pt[pti % 5], self.B("Pt%d" % (pti % 5))
            pti += 1
            S.add("act", lambda e, Pt=Pt, pS=pS, nk=nk: e.activation(out=Pt[0:nk, :], in_=pS[0:nk, 0:TN], func=AF.Exp), reads=[bpS], writes=[bPt])
            for sq_, (qs, nq) in enumerate(qsubs):
                if diag and sq_ < kj:
                    continue
                lastkb = nkb - 1 if True else 0
                is_last = diag and kj == sq_
                pO, bpO = Ob[sq_]
                self.mm(pO[0:nq, 0:65], Pt[0:nk, qs:qs + nq], self.Vc[0:nk, kb, h, :], kb == 0, is_last, [bPt, bVc], [bpO])
        for sq_, (qs, nq) in enumerate(qsubs):
            pO, bpO = Ob[sq_]
            brec = self.B("rec")
            S.add("dve", lambda e, pO=pO, nq=nq: e.reciprocal(out=self.rec[0:nq, 0:1], in_=pO[0:nq, 64:65]), reads=[bpO], writes=[brec])
            S.add("dve", lambda e, pO=pO, nq=nq, sq_=sq_, h=h: e.tensor_scalar(out=ybtok[0:nq, sq_, h * 64:(h + 1) * 64], in0=pO[0:nq, 0:64],
                                                                             scalar1=self.rec[0:nq, 0:1], scalar2=None, op0=ALU.mult),
                  reads=[bpO, brec], writes=[bybt])
    for sq_, (qs, nq) in enumerate(qsubs):
        pb, bpb = _nextb(self)
        for kt in range(4):
            S.add("pe", lambda e, pb=pb, kt=kt, nq=nq, sq_=sq_: e.transpose(pb[:, kt * 128:kt * 128 + nq], ybtok[0:nq, sq_, kt * 128:(kt + 1) * 128],
                                                                      self.ident[0:nq, 0:nq]), reads=[bybt, self.B("ident")], writes=[bpb])
        evac(ybT[:, :, qs:qs + nq], pb[:, 0:512].rearrange("p (k t) -> p k t", k=4)[:, :, 0:nq], [bpb], [bybT])

    bY = self.B("Ybf")
    Ybf = self.Ybf
    for bi, (ga, gb) in enumerate(grp_banks):
        pb, bpb = self.ps[5 + bi], self.B_ps[5 + bi]
        for g in range(ga, gb):
            hg, gp = g // 16, g % 16
            pr = slice(64 * hg, 64 * hg + 64)
            o = pb[:, (g - ga) * NCH:(g - ga + 1) * NCH]
            self.mm(o, self.W1[:, g, :], U[:, g, :], True, False, [bt, bU], [bpb])
            self.mm(o, self.W2[pr, gp, 0, :], Hbf[pr, 0, gp, :], False, False, [bt, bH], [bpb])
            self.mm(o, self.W2[pr, gp, 1, :], Hbf[pr, 1, gp, :], False, True, [bt, bH], [bpb])
        evac(Ybf[:, ga:gb, :], pb[:, 0:(gb - ga) * NCH].rearrange("p (g c) -> p g c", c=NCH), [bpb], [bY])
    y0, y1, y2, y3 = self.ytmp
    byt = self.B("Ssb")
    gT = uT
    for j in range(4):
        pb, bpb = _nextb(self)
        for t in range(8):
            for gl in range(8):
                self.mm(pb[:, t * NCH:(t + 1) * NCH], self.sel[:, t * 8 + gl, :], Ybf[:, 8 * j + gl, :], gl == 0, gl == 7, [bmc, bY], [bpb])
        S.add("dve", lambda e, pb=pb: e.tensor_copy(out=y0.rearrange("p (c t) -> p t c", t=8), in_=pb[:, 0:TN].rearrange("p (t c) -> p t c", t=8)),
              reads=[bpb], writes=[byt])
        S.add("act", lambda e: e.activation(out=y1, in_=y0, func=AF.Square), reads=[byt], writes=[byt])
        S.add("dve", lambda e: e.tensor_scalar(out=y1, in0=y1, scalar1=0.044715, scalar2=1.0, op0=ALU.mult, op1=ALU.add), reads=[byt], writes=[byt])
        S.add("dve", lambda e: e.tensor_tensor(out=y1, in0=y1, in1=y0, op=ALU.mult), reads=[byt], writes=[byt])
        S.add("act", lambda e: e.activation(out=y2, in_=y1, func=AF.Sigmoid, scale=1.5957691216057308), reads=[byt], writes=[byt])
        S.add("dve", lambda e, j=j: e.tensor_tensor(out=gT[:, j, :], in0=y0, in1=y2, op=ALU.mult), reads=[byt], writes=[but])
    yaT = self.yaT
    w1_, bw1 = _load_w(self, self.wglu_s[l, 0].rearrange("p k c -> p (k c)"), 4 * 512, wname)
    w2_, bw2 = _load_w(self, self.wglu_s[l, 1].rearrange("p k c -> p (k c)"), 4 * 512, wname)
    w1_ = w1_.rearrange("p (k c) -> p k c", k=4)
    w2_ = w2_.rearrange("p (k c) -> p k c", k=4)
    for m in range(4):
        pa, bpa = _nextb(self)
        pb2, bpb2 = _nextb(self)
        for kt in range(4):
            self.mm(pa[:, 0:TN], w1_[:, kt, m * 128:(m + 1) * 128], gT[:, kt, :], kt == 0, kt == 3, [bw1, but], [bpa])
        for kt in range(4):
            self.mm(pb2[:, 0:TN], w2_[:, kt, m * 128:(m + 1) * 128], gT[:, kt, :], kt == 0, kt == 3, [bw2, but], [bpb2])
        S.add("act", lambda e, pb2=pb2: e.activation(out=y3, in_=pb2[:, 0:TN], func=AF.Sigmoid), reads=[bpb2], writes=[byt])
        S.add("dve", lambda e, pa=pa, m=m: e.tensor_tensor(out=yaT[:, m, :], in0=pa[:, 0:TN], in1=y3, op=ALU.mult), reads=[bpa, byt], writes=[bU])

    mg = Qa
    pool5 = (5, 6, 7, 0, 1, 2, 3, 4)
    for m in range(8):
        wm_, bwm = _load_w(self, self.wmrg_s[l, m], 3072, wname)
        wga_ = wm_[:, 0:1024].rearrange("p (k c) -> p k c", k=KT)
        wgb_ = wm_[:, 1024:2048].rearrange("p (k c) -> p k c", k=KT)
        wa_ = wm_[:, 2048:2560].rearrange("p (k c) -> p k c", k=4)
        wb_ = wm_[:, 2560:3072].rearrange("p (k c) -> p k c", k=4)
        bwga = bwgb = bwa = bwb = bwm
        if True:
            cs = slice(0, 128)
            pga, bpga = _nextb(self, pool5)
            for kt in range(KT):
                self.mm(pga[:, 0:TN], wga_[:, kt, cs], nM[:, kt, :], kt == 0, kt == KT - 1, [bwga, bn], [bpga])
            pA, bpA = _nextb(self, pool5)
            for kt in range(4):
                self.mm(pA[:, 0:TN], wa_[:, kt, cs], yaT[:, kt, :], kt == 0, kt == 3, [bwa, bU], [bpA])
            pgb, bpgb = _nextb(self, pool5)
            for kt in range(KT):
                self.mm(pgb[:, 0:TN], wgb_[:, kt, cs], nM[:, kt, :], kt == 0, kt == KT - 1, [bwgb, bn], [bpgb])
            pB, bpB = _nextb(self, pool5)
            for kt in range(4):
                self.mm(pB[:, 0:TN], wb_[:, kt, cs], ybT[:, kt, :], kt == 0, kt == 3, [bwb, bybT], [bpB])
            g0, g1 = self.gat
            bg0, bg1 = self.B("gat0"), self.B("gat1")
            t1, t2 = self.t12
            b1, b2 = self.B("t12_0"), self.B("t12_1")
            S.add("act", lambda e, pga=pga, m=m: e.activation(out=g0, in_=pga[:, 0:TN], func=AF.Sigmoid, bias=self.bgate[:, l, m:m + 1]),
                  reads=[bpga, bmc], writes=[bg0])
            S.add("act", lambda e, pgb=pgb, m=m: e.activation(out=g1, in_=pgb[:, 0:TN], func=AF.Sigmoid, bias=self.bgate[:, l, 8 + m:9 + m]),
                  reads=[bpgb, bmc], writes=[bg1])
            S.add("dve", lambda e, pA=pA: e.tensor_tensor(out=t1, in0=g0, in1=pA[:, 0:TN], op=ALU.mult), reads=[bg0, bpA], writes=[b1])
            S.add("dve", lambda e, pB=pB: e.tensor_tensor(out=t2, in0=g1, in1=pB[:, 0:TN], op=ALU.mult), reads=[bg1, bpB], writes=[b2])
            S.add("dve", lambda e, m=m: e.tensor_tensor(out=mg[:, m, :], in0=t1, in1=t2, op=ALU.add), reads=[b1, b2], writes=[bqa])
    for half in range(2):
        wo_, bwo = _load_w(self, self.wo_s[l, half].rearrange("p k c -> p (k c)"), KT * 512, wname)
        wo_ = wo_.rearrange("p (k c) -> p k c", k=KT)
        for mm_ in range(4):
            o = half * 4 + mm_
            po, bpo = _nextb(self, pool5)
            for kt in range(KT):
                self.mm(po[:, 0:TN], wo_[:, kt, mm_ * 128:(mm_ + 1) * 128], mg[:, kt, :], kt == 0, kt == KT - 1, [bwo, bqa], [bpo])
            S.add("dve", lambda e, po=po, o=o: e.tensor_tensor(out=hM[:, o, :], in0=hM[:, o, :], in1=po[:, 0:TN], op=ALU.add),
                  reads=[bpo, bh], writes=[bh])
    self.dma(self.hs[q, :, :, t0:t0 + TN], hM[:], [bh], [self.B("hs%d" % q)], "hMst", q="pool")


def _mix_begin(self, l):
    S = self.S
    bt = self.B("s5tab")
    o4 = self.s5_off // 4
    tabs = self.arena[:, o4:o4 + (3 * 8192 + 256) // 4]
    if l not in self.s5_done:
        self.mix_setup(l)
        self.s5_done.add(l)
        self.dma(self.s5c_s[l], tabs, [bt], [self.B("s5c%d" % l)], "s5c")
    else:
        self.dma(tabs, self.s5c_s[l], [self.B("s5c%d" % l)], [bt], "s5c")
    S.add("pool", lambda e: e.memset(self.ones8[:, :], 1.0), writes=[bt])
    S.add("pool", lambda e: e.memset(self.Vc[:, :, :, 64:65], 1.0), writes=[self.B("Vc")])
    self.dma(self.wf[:].rearrange("p k c -> p (k c)"), self.wf_s[l].rearrange("p k c -> p (k c)"), [self.B("W_mix_%d" % l)], [bt], "wfld")
    self.zi = 0


Kern.mix_begin = _mix_begin
Kern.mix_tile = _mix_tile


def build(n_seq=2, x_len=2048, depth=2, mix=True, ffn=True):
    k = Kern(n_seq, x_len, depth)
    k.declare()
    if mix:
        k.declare_mix()
    k.alloc_ffn()
    if mix:
        k.alloc_mix()
    for l in range(depth):
        if ffn:
            for f in range(2):
                k.prepass_ffn(l, f)
        if mix:
            k.prepass_mix(l)
    k.barrier()
    k.stage_in()
    for q in range(n_seq):
        for l in range(depth):
            if ffn:
                k.stage_ffn(q, l, 0)
            if mix:
                k.barrier()
                k.mix_begin(l)
                k.barrier()
                for ti in range(k.NT):
                    k.mix_tile(q, l, ti)
                k.barrier()
            if ffn:
                k.stage_ffn(q, l, 1)
    k.stage_final()
    return k.finish()


def host_consts():
    bf = ml_dtypes.bfloat16
    c = {}
    c["ident"] = np.eye(128, dtype=np.float32)
    sel = np.zeros((128, 64, 128), np.float32)
    for a in range(8):
        for b in range(8):
            for i in range(16):
                sel[16 * a + i, a * 8 + b, 16 * b + i] = 1.0
    c["c_sel"] = sel.reshape(128, 64 * 128).astype(bf)
    m8 = np.zeros((128, 128), np.float32)
    for s in range(8):
        for t in range(s, 8):
            m8[16 * s:16 * s + 16, 16 * t:16 * t + 16] = 1.0
    c["c_mask8"] = m8
    mb = np.zeros((128, NB, TN), np.float32)
    r = np.arange(128)[:, None]
    ql = np.arange(TN)[None, :]
    for j in range(NB):
        mb[:, j, :] = np.where(ql - r - 128 * j >= 0, 0.0, -30000.0)
    c["c_maskb"] = mb.reshape(128, NB * TN).astype(bf)
    et = np.zeros((8, 3, 8, 132), np.float32)
    for h in range(8):
        for j in range(3):
            et[h, j, h, 68 + j] = 1.0
    c["c_et"] = et.reshape(8, -1).astype(bf)
    one = np.zeros((1, 256 + TN), np.float32)
    one[0, 68:71] = 8.0
    one[0, 128 + 64:128 + 67] = -8.0
    one[0, 256:] = 1.0
    c["c_one"] = one.astype(bf)
    jt = np.zeros((128, 2, 17), np.float32)
    jt[:, 0, :] = np.arange(17) - 8
    jt[:, 1, :] = 8 - np.arange(17)
    c["c_jtab"] = jt.reshape(128, 34)
    c["c_identbf"] = np.eye(128, dtype=np.float32).astype(bf)
    return c


PARAM_NAMES = ["g_ffn1", "w1_gate", "w1_up", "w1_down", "g_mix", "w_in", "b_gate", "b_f",
               "ssm_a_re", "ssm_a_im", "ssm_log_dt", "ssm_b_re", "ssm_b_im", "ssm_c_re", "ssm_c_im",
               "ssm_d", "w_glu", "w_br_a", "w_br_b", "w_o", "g_ffn2", "w2_gate", "w2_up", "w2_down"]

_NC_CACHE = {}


def kernel(**inputs):
    n_cores = 8
    x = np.ascontiguousarray(np.asarray(inputs["x"], dtype=np.float32))
    bsz, x_len, _ = x.shape
    n_seq = bsz // n_cores
    key = (n_seq, x_len)
    if key not in _NC_CACHE:
        _NC_CACHE[key] = build(n_seq=n_seq, x_len=x_len, depth=2)
    nc = _NC_CACHE[key]
    base = host_consts()
    base["meta"] = np.ascontiguousarray(np.asarray(inputs["meta"], np.float32))
    base["g_final"] = np.ascontiguousarray(np.asarray(inputs["g_final"], np.float32))
    for nm in PARAM_NAMES:
        base[nm] = np.ascontiguousarray(np.asarray(inputs[nm], np.float32))
    in_maps = []
    for c in range(n_cores):
        m = dict(base)
        m["x"] = x[c * n_seq:(c + 1) * n_seq]
        in_maps.append(m)
    res = run_bass_kernel_spmd(nc, in_maps, core_ids=list(range(n_cores)))
    return np.concatenate([np.asarray(r["out"], np.float32) for r in res.results], axis=0)
```

```python
import numpy as np
import ml_dtypes
import concourse.bass as bass
import concourse.mybir as mybir
from concourse.bass_utils import run_bass_kernel_spmd

F32 = mybir.dt.float32
BF16 = mybir.dt.bfloat16
AF = mybir.ActivationFunctionType
ALU = mybir.AluOpType


class Buf:
    __slots__ = ("name", "w", "r", "alias")

    def __init__(self, name):
        self.name = name
        self.w = None
        self.r = []
        self.alias = []


class Op:
    __slots__ = ("eng", "fn", "waits", "needed", "done", "dma", "idx", "chain")

    def __init__(self, eng, fn):
        self.eng = eng
        self.fn = fn
        self.waits = []
        self.needed = False
        self.done = None
        self.dma = None
        self.chain = False


class Sched:
    ENGS = ("pe", "act", "dve", "pool", "sp")

    def __init__(self, nc):
        self.nc = nc
        self.ops = {e: [] for e in self.ENGS}
        self.dma_sems = {}
        self.dma_gen = {}
        self.all_ops = []

    def _dep(self, op, prod):
        if prod is None or prod is op:
            return
        if prod.eng == "pe" and op.eng == "pe" and prod.dma is None and op.dma is None:
            return
        if prod.eng == "pool" and op.eng == "pool" and prod.dma is None and op.dma is None and getattr(op, "chain", False) and getattr(prod, "chain", False):
            return
        if prod not in op.waits:
            op.waits.append(prod)
            prod.needed = True

    def add(self, eng, fn, reads=(), writes=(), dma_key=None, chain=False):
        op = Op(eng, fn)
        op.dma = dma_key
        op.chain = chain
        for b in reads:
            for bb in [b] + b.alias:
                self._dep(op, bb.w)
        for b in writes:
            for bb in [b] + b.alias:
                self._dep(op, bb.w)
                for r in bb.r:
                    self._dep(op, r)
        for b in reads:
            if dma_key is None:
                b.r = [r for r in b.r if not (r.eng == eng and r.dma is None)]
            b.r.append(op)
        for b in writes:
            b.w = op
            b.r = []
        if dma_key is not None:
            gen = self.dma_gen.get(dma_key, 0)
            if (dma_key, gen) in self.dma_sems and self.dma_sems[(dma_key, gen)][1] >= 1500:
                gen += 1
                self.dma_gen[dma_key] = gen
            dma_key = (dma_key, gen)
            op.dma = dma_key
            ent = self.dma_sems.setdefault(dma_key, [None, 0, None])
            self._dep(op, ent[2])
            ent[1] += 1
            ent[2] = op
            op.done = (dma_key, 16 * ent[1])
            op.needed = True
        self.ops[eng].append(op)
        self.all_ops.append(op)
        return op

    def emit(self, final_waits=()):
        nc = self.nc
        import contextlib
        with contextlib.ExitStack() as es:
            for k, ent in self.dma_sems.items():
                ent[0] = es.enter_context(nc.semaphore("d_%s_%d" % k))
            nsem = 0
            for e in self.ENGS:
                cnt = 0
                gen = 0
                cur = es.enter_context(nc.semaphore("s_%s_%d" % (e, gen)))
                for op in self.ops[e]:
                    if op.dma is not None:
                        op.done = (self.dma_sems[op.dma][0], op.done[1])
                    elif op.needed:
                        if cnt >= 30000:
                            gen += 1
                            cnt = 0
                            cur = es.enter_context(nc.semaphore("s_%s_%d" % (e, gen)))
                        cnt += 1
                        op.done = (cur, cnt)
            block = es.enter_context(nc.Block())
            handles = {"pe": block.tensor, "act": block.scalar, "dve": block.vector,
                       "pool": block.gpsimd, "sp": block.sync}

            def make(e):
                def body(eng):
                    waited = {}
                    for op in self.ops[e]:
                        for p in op.waits:
                            sem, val = p.done
                            key = id(sem)
                            if waited.get(key, 0) >= val:
                                continue
                            waited[key] = val
                            eng.wait_ge(sem, val)
                        ins = op.fn(eng)
                        if op.dma is not None:
                            ins.then_inc(op.done[0], 16)
                        elif op.needed:
                            ins.then_inc(op.done[0], 1)
                    if e == "sp":
                        for p in final_waits:
                            sem, val = p.done
                            if waited.get(id(sem), 0) < val:
                                eng.wait_ge(sem, val)
                return body

            for e in self.ENGS:
                handles[e](make(e))


D = 1024
KT = D // 128
NMETA = 16
DFF = 2816
NFC = DFF // 128
TN = 344
EPS = 1e-6
IN_W = 4104


def sub128(n):
    out = []
    s = 0
    while s < n:
        out.append((s, min(128, n - s)))
        s += 128
    return out


class Ctx:
    pass


class Kern:
    def __init__(self, n_seq=2, x_len=2048, depth=2):
        import contextlib
        self.n_seq, self.x_len, self.depth = n_seq, x_len, depth
        self.L = NMETA + x_len
        assert self.L % TN == 0
        self.NT = self.L // TN
        self.nc = bass.Bass("TRN2", target_bir_lowering=False)
        self.S = Sched(self.nc)
        self.es = contextlib.ExitStack()
        self.bufs = {}
        self.final_ops = []
        self.cast_rr = 0
        nc = self.nc
        self.ps = [self.es.enter_context(nc.psum_tensor("ps%d" % i, [128, 512], F32)) for i in range(8)]
        self.B_ps = [Buf("ps%d" % i) for i in range(8)]

    def din(self, name, shape, dt=F32):
        return self.nc.dram_tensor(name, list(shape), dt, kind="ExternalInput").ap()

    def dscratch(self, name, shape, dt):
        return self.nc.dram_tensor(name, list(shape), dt).ap()

    def sb(self, name, shape, dt):
        t = self.es.enter_context(self.nc.sbuf_tensor(name, list(shape), dt))
        return t

    def B(self, name):
        if name not in self.bufs:
            self.bufs[name] = Buf(name)
        return self.bufs[name]

    def dma(self, out, in_, reads, writes, key, q="sp", slow=False):
        if slow:
            fn = lambda e: e.dma_start(out=out, in_=in_, allow_slow_non_contiguous=True)
        else:
            fn = lambda e: e.dma_start(out=out, in_=in_)
        return self.S.add(q, fn, reads=reads, writes=writes, dma_key=key)

    def copy(self, eng, out, in_, reads, writes):
        if eng == "act":
            return self.S.add("act", lambda e: e.activation(out=out, in_=in_, func=AF.Copy), reads=reads, writes=writes)
        return self.S.add(eng, lambda e: e.tensor_copy(out=out, in_=in_), reads=reads, writes=writes)

    def mm(self, out, lhsT, rhs, start, stop, reads, writes):
        return self.S.add("pe", lambda e: e.matmul(out, lhsT, rhs, start=start, stop=stop), reads=reads, writes=writes)

    def declare(self):
        n_seq, x_len, depth, L = self.n_seq, self.x_len, self.depth, self.L
        self.x = self.din("x", [n_seq, x_len, D])
        self.meta = self.din("meta", [NMETA, D])
        self.g_final = self.din("g_final", [D])
        self.ident_d = self.din("ident", [128, 128])
        dd = max(depth, 1)
        self.P = {}
        for nm, shp in [("g_ffn1", [dd, D]), ("w1_gate", [dd, D, DFF]), ("w1_up", [dd, D, DFF]), ("w1_down", [dd, DFF, D]),
                        ("g_ffn2", [dd, D]), ("w2_gate", [dd, D, DFF]), ("w2_up", [dd, D, DFF]), ("w2_down", [dd, DFF, D])]:
            self.P[nm] = self.din(nm, shp)
        self.out = self.nc.dram_tensor("out", [n_seq, x_len, D], F32, kind="ExternalOutput").ap()
        self.hs = self.dscratch("hs", [n_seq, 128, KT, L], F32)
        self.wgu = self.dscratch("wgu", [dd, 2, 11, 128, 2, KT, 256], BF16)
        self.wd = self.dscratch("wd", [dd, 2, 8, 128, NFC, 128], BF16)
        S = self.S
        self.ident = self.sb("ident_sb", [128, 128], F32)
        self.ones_bf = self.sb("ones_bf", [128, 128], BF16)
        self.gvec = self.sb("gvec", [128, 1 + 3 * dd, KT], F32)
        self.dma(self.ident[:], self.ident_d[:], [], [self.B("ident")], "const")
        self.dma(self.gvec[:, 0, :], self.g_final.rearrange("(k p) -> p k", p=128), [], [self.B("gvec")], "const", slow=True)
        for l in range(depth):
            self.dma(self.gvec[:, 1 + 3 * l, :], self.P["g_ffn1"][l].rearrange("(k p) -> p k", p=128), [], [self.B("gvec")], "const", slow=True)
            self.dma(self.gvec[:, 3 + 3 * l, :], self.P["g_ffn2"][l].rearrange("(k p) -> p k", p=128), [], [self.B("gvec")], "const", slow=True)
        S.add("dve", lambda e: e.memset(self.ones_bf[:], 1.0), writes=[self.B("ones")])
        self.sq = self.sb("sq", [128, KT, TN], BF16)
        self.rstd = self.sb("rstd", [128, TN], F32)
        self.wslot = [self.sb("wslot%d" % i, [128, 4096], BF16) for i in range(4)]
        self.wslot_i = 0
        for i in range(4):
            self.S.add("pool", lambda e, i=i: e.memset(self.wslot[i][:, :], 0.0), writes=[self.B("wslot%d" % i)])
        self.ARENA_BYTES = 143360
        self.arena = self.sb("arena", [128, self.ARENA_BYTES // 4], F32)
        self.reg_off = {"pre": 0, "ffn": 0}
        self.NSTG = 6
        self.stg32 = [_carve(self, "pre", "stg32", 2816, F32) for i in range(self.NSTG)]
        self.stg16 = [_carve(self, "pre", "stg16", 2816, BF16) for i in range(self.NSTG)]
        self.stg_i = 0
        self.bank_rr = 0
        self.zi = 0

    def next_wslot(self):
        i = self.wslot_i % 4
        self.wslot_i += 1
        return self.wslot[i], self.B("wslot%d" % i), "wslot%d" % i

    def cast_rows(self, src, ncols, stores, wname, ld_view=None):
        i = self.stg_i % self.NSTG
        self.stg_i += 1
        s32, s16 = self.stg32[i], self.stg16[i]
        b32, b16 = self.B("stg32_%d" % i), self.B("stg16_%d" % i)
        if ld_view is None:
            self.dma(s32[:, 0:ncols], src, [], [b32], "stg32_%d" % i)
        else:
            dv = ld_view(s32)
            n1 = dv.shape[1]
            step = 4
            parts = []
            for pi, c0 in enumerate(range(0, n1, step)):
                bp = self.B("stg32_%d_%d" % (i, pi))
                parts.append(bp)
                if pi == 0:
                    self.dma(dv[:, c0:min(c0 + step, n1), :], src[:, c0:min(c0 + step, n1), :], [], [bp, b32], "stg32_%d_%d" % (i, pi))
                else:
                    self.dma(dv[:, c0:min(c0 + step, n1), :], src[:, c0:min(c0 + step, n1), :], [b32], [bp], "stg32_%d_%d" % (i, pi))
            parts.append(b32)
            b32 = None
        eng = ("dve", "act", "dve")[self.cast_rr % 3]
        self.cast_rr += 1
        if b32 is None:
            self.copy(eng, s16[:, 0:ncols], s32[:, 0:ncols], parts, [b16])
        else:
            self.copy(eng, s16[:, 0:ncols], s32[:, 0:ncols], [b32], [b16])
        for dst, view in stores:
            self.dma(dst, view(s16), [b16], [self.B(wname)], "stg16_%d" % i, q="pool")

    def prepass_ffn(self, l, f):
        wg = self.P["w%d_gate" % (f + 1)][l]
        wu = self.P["w%d_up" % (f + 1)][l]
        wdn = self.P["w%d_down" % (f + 1)][l]
        nm = "W_ffn_%d_%d" % (l, f)
        for blk in range(11):
            for gu, w in enumerate((wg, wu)):
                src = w[:, blk * 256:(blk + 1) * 256].rearrange("(k p) c -> p k c", p=128)
                dst = self.wgu[l, f, blk][:, gu, :, :].rearrange("p k c -> p (k c)")
                self.cast_rows(src, 2048, [(dst, lambda s: s[:, 0:2048])], nm,
                               ld_view=lambda s: s[:, 0:2048].rearrange("p (k c) -> p k c", k=KT))
        for o in range(KT):
            src = wdn[:, o * 128:(o + 1) * 128].rearrange("(c p) m -> p c m", p=128)
            dst = self.wd[l, f, o].rearrange("p c m -> p (c m)")
            self.cast_rows(src, NFC * 128, [(dst, lambda s: s[:, 0:NFC * 128])], nm,
                           ld_view=lambda s: s[:, 0:NFC * 128].rearrange("p (c m) -> p c m", c=NFC))

    def stage_in(self):
        S, L = self.S, self.L
        xin, hT = self.xin, self.hT
        ps, B_ps = self.ps, self.B_ps
        ident = self.ident
        it = 0
        for q in range(self.n_seq):
            for (t0, nt) in sub128(L):
                sl = it % 2
                it += 1
                xt, bx = xin[sl], self.B("xin%d" % sl)
                if t0 == 0:
                    self.dma(xt[0:NMETA, :], self.meta[:, :], [], [bx], "xin%d" % sl)
                    self.dma(xt[NMETA:nt, :], self.x[q, 0:nt - NMETA, :], [], [bx], "xin%d" % sl)
                else:
                    self.dma(xt[0:nt, :], self.x[q, t0 - NMETA:t0 - NMETA + nt, :], [], [bx], "xin%d" % sl)
                ht, bh = hT[sl], self.B("hT%d" % sl)
                for half in range(2):
                    pb, bpb = ps[half], B_ps[half]
                    for j in range(4):
                        kt = half * 4 + j
                        S.add("pe", lambda e, pb=pb, xt=xt, kt=kt, j=j, nt=nt: e.transpose(
                            pb[:, j * 128:j * 128 + nt], xt[0:nt, kt * 128:(kt + 1) * 128], ident[0:nt, 0:nt]),
                            reads=[bx, self.B("ident")], writes=[bpb])
                    self.copy("dve" if half == 0 else "act", ht[:, half * 4:half * 4 + 4, 0:nt],
                              pb[:].rearrange("p (j t) -> p j t", j=4)[:, :, 0:nt], [bpb], [bh])
                self.dma(self.hs[q, :, :, t0:t0 + nt], ht[:, :, 0:nt], [bh], [self.B("hs%d" % q)], "hTst%d" % sl, q="pool")

    def rms_stats(self, h_t, B_h, n, pbi=2):
        S = self.S
        sq, rstd, ones_bf = self.sq, self.rstd, self.ones_bf
        B_sq, B_rstd, B_ones = self.B("sq"), self.B("rstd"), self.B("ones")
        pbank, B_pbank = self.ps[pbi], self.B_ps[pbi]
        S.add("act", lambda e: e.activation(out=sq[:, :, 0:n], in_=h_t[:, :, 0:n], func=AF.Square),
              reads=[B_h], writes=[B_sq])
        for kt in range(KT):
            self.mm(pbank[:, 0:n], ones_bf[:, :], sq[:, kt, 0:n], kt == 0, kt == KT - 1, [B_sq, B_ones], [B_pbank])
        S.add("act", lambda e: e.activation(out=rstd[:, 0:n], in_=pbank[:, 0:n], func=AF.Ln,
                                            scale=1.0 / D, bias=EPS), reads=[B_pbank], writes=[B_rstd])
        S.add("act", lambda e: e.activation(out=rstd[:, 0:n], in_=rstd[:, 0:n], func=AF.Exp, scale=-0.5),
              reads=[B_rstd], writes=[B_rstd])

    def norm_apply(self, h_t, B_h, gi, out_t, B_out, n):
        for kt in range(KT):
            self.S.add("dve", lambda e, kt=kt: e.scalar_tensor_tensor(
                out=out_t[:, kt, 0:n], in0=h_t[:, kt, 0:n], scalar=self.gvec[:, gi, kt:kt + 1], in1=self.rstd[:, 0:n],
                op0=ALU.mult, op1=ALU.mult), reads=[B_h, self.B("rstd"), self.B("gvec")], writes=[B_out])

    def alloc_ffn(self):
        cv = lambda nm, n, dt: _carve(self, "ffn", nm, n, dt)
        self.hF = [cv("hF", KT * TN, F32).rearrange("p (k t) -> p k t", k=KT) for i in range(4)]
        self.nF = [cv("nF", KT * TN, BF16).rearrange("p (k t) -> p k t", k=KT) for i in range(4)]
        self.ffn_set = 0
        self.hid = [cv("hid", NFC * TN, BF16).rearrange("p (k t) -> p k t", k=NFC) for i in range(2)]
        self.sil = [cv("sil", TN, F32) for i in range(3)]
        self.sil_i = 0
        self.yn = cv("yn", KT * TN, F32).rearrange("p (k t) -> p k t", k=KT)
        self.yo = [cv("yo", D, F32) for i in range(2)]
        self.xin = [cv("xin", D, F32) for i in range(2)]
        self.hT = [cv("hT", KT * 128, F32).rearrange("p (k t) -> p k t", k=KT) for i in range(2)]

    def ffn_load_norm(self, q, gi, tiles, st, do_load=True, do_norm=True):
        for j, ti in enumerate(tiles):
            jj = 2 * st + j
            hF, bhF = self.hF[jj], self.B("hF%d" % jj)
            if do_load:
                self.dma(hF[:], self.hs[q, :, :, ti * TN:(ti + 1) * TN], [self.B("hs%d" % q)], [bhF], "hF%d" % jj)
            if do_norm:
                self.rms_stats(hF, bhF, TN)
                self.norm_apply(hF, bhF, gi, self.nF[jj], self.B("nF%d" % jj), TN)

    def stage_ffn(self, q, l, f):
        S, NT = self.S, self.NT
        ps, B_ps = self.ps, self.B_ps
        gi = 1 + 3 * l + (0 if f == 0 else 2)
        wname = "W_ffn_%d_%d" % (l, f)
        tiles_all = list(range(NT))
        groups = [tiles_all[g0:g0 + 2] for g0 in range(0, NT, 2)]
        for gidx, tiles in enumerate(groups):
            st = self.ffn_set % 2
            if gidx == 0:
                self.ffn_load_norm(q, gi, tiles, st)
            self.ffn_set += 1
            mmi = 0
            for blk in range(11):
                wt, bw, wk = self.next_wslot()
                self.dma(wt[:, 0:4096], self.wgu[l, f, blk].rearrange("p g k c -> p (g k c)"),
                         [self.B(wname)], [bw], wk)
                wv = wt[:, 0:4096].rearrange("p (g k c) -> p g k c", g=2, k=KT)
                for j, ti in enumerate(tiles):
                    nF, bn = self.nF[2 * st + j], self.B("nF%d" % (2 * st + j))
                    for cc in range(2):
                        c = blk * 2 + cc
                        gi_, ui_ = (0, 1, 6)[mmi % 3], (2, 3, 7)[mmi % 3]
                        pg, bpg = ps[gi_], B_ps[gi_]
                        pu, bpu = ps[ui_], B_ps[ui_]
                        mmi += 1
                        for kt in range(KT):
                            self.mm(pg[:, 0:TN], wv[:, 0, kt, cc * 128:(cc + 1) * 128], nF[:, kt, :],
                                    kt == 0, kt == KT - 1, [bw, bn], [bpg])
                        for kt in range(KT):
                            self.mm(pu[:, 0:TN], wv[:, 1, kt, cc * 128:(cc + 1) * 128], nF[:, kt, :],
                                    kt == 0, kt == KT - 1, [bw, bn], [bpu])
                        si = self.sil_i % 3
                        self.sil_i += 1
                        sl_t, bsl = self.sil[si], self.B("sil%d" % si)
                        S.add("act", lambda e, sl_t=sl_t, pg=pg: e.activation(out=sl_t[:, :], in_=pg[:, 0:TN], func=AF.Silu),
                              reads=[bpg], writes=[bsl])
                        hid, bhid = self.hid[j], self.B("hid%d" % j)
                        S.add("dve", lambda e, hid=hid, c=c, sl_t=sl_t, pu=pu: e.tensor_tensor(
                            out=hid[:, c, :], in0=sl_t[:, :], in1=pu[:, 0:TN], op=ALU.mult),
                            reads=[bsl, bpu], writes=[bhid])
            if gidx + 1 < len(groups):
                self.ffn_load_norm(q, gi, groups[gidx + 1], 1 - st, do_norm=False)
            for o in range(KT):
                if o == 4 and gidx + 1 < len(groups):
                    self.ffn_load_norm(q, gi, groups[gidx + 1], 1 - st, do_load=False)
                wt, bw, wk = self.next_wslot()
                self.dma(wt[:, 0:NFC * 128], self.wd[l, f, o].rearrange("p c o -> p (c o)"),
                         [self.B(wname)], [bw], wk)
                wv = wt[:, 0:NFC * 128].rearrange("p (c o) -> p c o", c=NFC)
                for j, ti in enumerate(tiles):
                    hF, bhF = self.hF[2 * st + j], self.B("hF%d" % (2 * st + j))
                    hid, bhid = self.hid[j], self.B("hid%d" % j)
                    di_ = (4, 5, 6, 7)[mmi % 4]
                    pd, bpd = ps[di_], B_ps[di_]
                    mmi += 1
                    for c in range(NFC):
                        self.mm(pd[:, 0:TN], wv[:, c, :], hid[:, c, :],
                                c == 0, c == NFC - 1, [bw, bhid], [bpd])
                    S.add("dve", lambda e, hF=hF, o=o, pd=pd: e.scalar_tensor_tensor(
                        out=hF[:, o, :], in0=pd[:, 0:TN], scalar=0.5, in1=hF[:, o, :],
                        op0=ALU.mult, op1=ALU.add), reads=[bpd, bhF], writes=[bhF])
            for j, ti in enumerate(tiles):
                hF, bhF = self.hF[2 * st + j], self.B("hF%d" % (2 * st + j))
                self.dma(self.hs[q, :, :, ti * TN:(ti + 1) * TN], hF[:], [bhF], [self.B("hs%d" % q)], "hFst%d" % (2 * st + j), q="pool")

    def stage_final(self):
        S, NT = self.S, self.NT
        ps, B_ps = self.ps, self.B_ps
        hin = self.hF
        yn = self.yn
        B_yn = self.B("yn")
        yo = self.yo
        it = 0
        oi = 0
        for q in range(self.n_seq):
            for ti in range(NT):
                sl = it % 2
                it += 1
                t0 = ti * TN
                hi_, bhi = hin[sl], self.B("hF%d" % sl)
                self.dma(hi_[:], self.hs[q, :, :, t0:t0 + TN], [self.B("hs%d" % q)], [bhi], "hF%d" % sl)
                self.rms_stats(hi_, bhi, TN)
                self.norm_apply(hi_, bhi, 0, yn, B_yn, TN)
                for (s0, nt) in sub128(TN):
                    lo = max(t0 + s0, NMETA)
                    hi = t0 + s0 + nt
                    if hi <= lo:
                        continue
                    a0 = lo - (t0 + s0)
                    so = oi % 2
                    oi += 1
                    yt, byt = yo[so], self.B("yo%d" % so)
                    for half in range(2):
                        pb, bpb = ps[6 + half], B_ps[6 + half]
                        for j in range(4):
                            kt = half * 4 + j
                            S.add("pe", lambda e, pb=pb, kt=kt, j=j, s0=s0, nt=nt: e.transpose(
                                pb[0:nt, j * 128:(j + 1) * 128], yn[:, kt, s0:s0 + nt], self.ident[:, :]),
                                reads=[B_yn, self.B("ident")], writes=[bpb])
                        self.copy("dve" if half == 0 else "act", yt[0:nt, half * 512:half * 512 + 512], pb[0:nt, :], [bpb], [byt])
                    op = self.dma(self.out[q, lo - NMETA:hi - NMETA, :], yt[a0:nt, :], [byt], [], "yo%d" % so, q="pool")
                    self.final_ops.append(op)

    def finish(self):
        self.S.emit(final_waits=self.final_ops)
        self.es.close()
        return self.nc


NG = 32
NCH = TN // 8
NB = 3
HEADS = 8
MAGIC = 12582912.0
TWO_PI = 6.283185307179586


def _carve(self, region, name, nelems, dt):
    off = self.reg_off[region]
    nbytes = nelems * (4 if dt == F32 else 2)
    nbytes = (nbytes + 31) // 32 * 32
    self.reg_off[region] = off + nbytes
    assert off + nbytes <= self.ARENA_BYTES, (name, off + nbytes)
    a4 = self.arena[:, off // 4:(off + nbytes) // 4]
    v = a4 if dt == F32 else a4.bitcast(dt)
    return v[:, 0:nelems]


def _barrier(self):
    S = self.S
    lasts = [S.ops[e][-1] for e in S.ENGS if S.ops[e]]
    lasts += [ent[2] for ent in S.dma_sems.values() if ent[2] is not None]
    for e in S.ENGS:
        op = S.add(e, lambda eng: eng.nop())
        for p in lasts:
            S._dep(op, p)
            if p.eng == "pe" and e == "pe":
                pass


def _declare_mix(self):
    dd = max(self.depth, 1)
    for nm, shp in [("g_mix", [dd, D]), ("w_in", [dd, D, IN_W]), ("b_gate", [dd, 2 * D]), ("b_f", [dd, HEADS]),
                    ("ssm_a_re", [dd, NG, 64]), ("ssm_a_im", [dd, NG, 64]), ("ssm_log_dt", [dd, NG]),
                    ("ssm_b_re", [dd, NG, 64, 16]), ("ssm_b_im", [dd, NG, 64, 16]),
                    ("ssm_c_re", [dd, NG, 16, 64]), ("ssm_c_im", [dd, NG, 16, 64]), ("ssm_d", [dd, 512]),
                    ("w_glu", [dd, 512, 1024]), ("w_br_a", [dd, 512, 1024]), ("w_br_b", [dd, 512, 1024]),
                    ("w_o", [dd, D, D])]:
        self.P[nm] = self.din(nm, shp)
    self.c_sel = self.din("c_sel", [128, 64 * 128], BF16)
    self.c_mask8 = self.din("c_mask8", [128, 128])
    self.c_maskb = self.din("c_maskb", [128, NB * TN], BF16)
    self.c_et = self.din("c_et", [8, 3 * 8 * 132], BF16)
    self.c_one = self.din("c_one", [1, 256 + TN], BF16)
    self.c_jtab = self.din("c_jtab", [128, 2 * 17])
    self.c_identbf = self.din("c_identbf", [128, 128], BF16)
    self.wu_s = self.dscratch("wu_s", [dd, 128, KT, 512], BF16)
    self.wq_s = self.dscratch("wq_s", [dd, 128, KT, 1024], BF16)
    self.wk_s = self.dscratch("wk_s", [dd, 128, KT, 1024], BF16)
    self.wv_s = self.dscratch("wv_s", [dd, 128, KT, 512], BF16)
    self.wf_s = self.dscratch("wf_s", [dd, 128, KT, 8], BF16)
    self.wmrg_s = self.dscratch("wmrg_s", [dd, 8, 128, 3072], BF16)
    self.wglu_s = self.dscratch("wglu_s", [dd, 2, 128, 4, 512], BF16)
    self.wo_s = self.dscratch("wo_s", [dd, 2, 128, KT, 512], BF16)
    self.s5c_s = self.dscratch("s5c_s", [dd, 128, (3 * 8192 + 256) // 4], F32)
    self.s5_done = set()
    self.sel = self.sb("sel_sb", [128, 64, 128], BF16)
    self.mask8 = self.sb("mask8_sb", [128, 128], F32)
    self.maskb = self.sb("maskb_sb", [128, NB, TN], BF16)
    self.et = self.sb("et_sb", [8, 3, 8, 132], BF16)
    self.one = self.sb("one_sb", [1, 256 + TN], BF16)
    self.jtab = self.sb("jtab_sb", [128, 2, 17], F32)
    self.identbf = self.sb("identbf_sb", [128, 128], BF16)
    self.bgate = self.sb("bgate_sb", [128, dd, 16], F32)
    self.bfneg = self.sb("bfneg_sb", [8, dd], F32)
    bc = self.B("mconst")
    self.dma(self.sel[:].rearrange("p a b -> p (a b)"), self.c_sel[:], [], [bc], "const")
    self.dma(self.mask8[:], self.c_mask8[:], [], [bc], "const")
    self.dma(self.maskb[:].rearrange("p a b -> p (a b)"), self.c_maskb[:], [], [bc], "const")
    self.dma(self.et[:].rearrange("p a b c -> p (a b c)"), self.c_et[:], [], [bc], "const")
    self.dma(self.one[:], self.c_one[:], [], [bc], "const")
    self.dma(self.jtab[:].rearrange("p a b -> p (a b)"), self.c_jtab[:], [], [bc], "const")
    self.dma(self.identbf[:], self.c_identbf[:], [], [bc], "const")
    for l in range(self.depth):
        self.dma(self.gvec[:, 2 + 3 * l, :], self.P["g_mix"][l].rearrange("(k p) -> p k", p=128), [], [self.B("gvec")], "const", slow=True)
        self.dma(self.bgate[:, l, :], self.P["b_gate"][l].rearrange("(k p) -> p k", p=128), [], [bc], "const", slow=True)
        self.dma(self.bfneg[:, l:l + 1], self.P["b_f"][l].rearrange("(h o) -> h o", o=1), [], [bc], "const", slow=True)
    self.S.add("dve", lambda e: e.tensor_scalar(out=self.bfneg[:], in0=self.bfneg[:], scalar1=-1.0, scalar2=None, op0=ALU.mult),
               reads=[bc], writes=[bc])


def _prepass_mix(self, l):
    nm = "W_mix_%d" % l
    w_in = self.P["w_in"][l]
    for kt in range(KT):
        rows = slice(kt * 128, (kt + 1) * 128)
        i = self.stg_i % self.NSTG
        self.stg_i += 1
        s32, s16 = self.stg32[i], self.stg16[i]
        b32, b16 = self.B("stg32_%d" % i), self.B("stg16_%d" % i)
        self.dma(s32[:, 0:2056], w_in[rows, 0:2056], [], [b32], "stg32_%d" % i)
        j = self.stg_i % self.NSTG
        self.stg_i += 1
        s16b, b16b = self.stg16[j], self.B("stg16_%d" % j)
        self.S.add("dve", lambda e, s16b=s16b: e.memset(s16b[:, 0:2048], 0.0), writes=[b16b])
        self.copy("dve", s16[:, 0:512], s32[:, 0:512], [b32], [b16])
        self.copy("act", s16[:, 512:1032], s32[:, 1536:2056], [b32], [b16])
        self.copy("act", s16b[:, 0:1024].rearrange("p (h c) -> p h c", c=128)[:, :, 0:64],
                  s32[:, 512:1024].rearrange("p (h c) -> p h c", c=64), [b32], [b16b])
        self.copy("dve", s16b[:, 1024:2048].rearrange("p (h c) -> p h c", c=128)[:, :, 0:64],
                  s32[:, 1024:1536].rearrange("p (h c) -> p h c", c=64), [b32], [b16b])
        k_ = "stg16_%d" % i
        self.dma(self.wu_s[l, :, kt, :], s16[:, 0:512], [b16], [self.B(nm)], k_, q="pool")
        self.dma(self.wv_s[l, :, kt, :], s16[:, 512:1024], [b16], [self.B(nm)], k_, q="pool")
        self.dma(self.wf_s[l, :, kt, :], s16[:, 1024:1032], [b16], [self.B(nm)], k_, q="pool")
        self.dma(self.wq_s[l, :, kt, :], s16b[:, 0:1024], [b16b], [self.B(nm)], "stg16_%d" % j, q="pool")
        self.dma(self.wk_s[l, :, kt, :], s16b[:, 1024:2048], [b16b], [self.B(nm)], "stg16_%d" % j, q="pool")
        mrg = self.wmrg_s[l]
        stores = []
        for ab in range(2):
            dst = mrg[:, :, ab * 1024 + kt * 128: ab * 1024 + (kt + 1) * 128].rearrange("m p c -> p m c")
            stores.append((dst, (lambda s, ab=ab: s[:, ab * 1024:(ab + 1) * 1024].rearrange("p (m c) -> p m c", c=128))))
        self.cast_rows(w_in[rows, 2056:4104], 2048, stores, nm)
    for kt in range(4):
        dst = self.wglu_s[l][:, :, kt, :].rearrange("b p c -> p b c")
        self.cast_rows(self.P["w_glu"][l][kt * 128:(kt + 1) * 128, :], 1024,
                       [(dst, lambda s: s[:, 0:1024].rearrange("p (b c) -> p b c", c=512))], nm)
    for bi, src in enumerate((self.P["w_br_a"][l], self.P["w_br_b"][l])):
        for kt in range(4):
            off = 2048 + bi * 512 + kt * 128
            dst = self.wmrg_s[l][:, :, off:off + 128].rearrange("m p c -> p m c")
            self.cast_rows(src[kt * 128:(kt + 1) * 128, :], 1024,
                           [(dst, lambda s: s[:, 0:1024].rearrange("p (m c) -> p m c", c=128))], nm)
    for kt in range(KT):
        dst = self.wo_s[l][:, :, kt, :].rearrange("b p c -> p b c")
        self.cast_rows(self.P["w_o"][l][kt * 128:(kt + 1) * 128, :], 1024,
                       [(dst, lambda s: s[:, 0:1024].rearrange("p (b c) -> p b c", c=512))], nm)


def _alloc_mix(self):
    L, NT = self.L, self.NT
    c = lambda reg, nm, n, dt: _carve(self, reg, nm, n, dt)
    self.reg_off["mixP"] = 0
    self.Kc = c("mixP", "Kc", HEADS * L, BF16).rearrange("p (h t) -> p h t", h=HEADS)
    self.Vc = c("mixP", "Vc", NT * NB * HEADS * 65, BF16).rearrange("p (b h d) -> p b h d", h=HEADS, d=65)
    self.s5_off = self.reg_off["mixP"]
    self.W1 = c("mixP", "W1", NG * 128, BF16).rearrange("p (g m) -> p g m", g=NG)
    self.W2 = c("mixP", "W2", 16 * 2 * 128, BF16).rearrange("p (g r m) -> p g r m", g=16, r=2)
    self.W3 = c("mixP", "W3", NG * 2 * 64, BF16).rearrange("p (g r m) -> p g r m", g=NG, r=2)
    self.A8 = c("mixP", "A8", 2 * 2 * 16, F32).rearrange("p (a r g) -> p a r g", a=2, r=2)
    self.Z = [c("mixP", "Z%d" % i, 3 * 16, F32).rearrange("p (r g) -> p r g", r=3) for i in range(2)]
    self.Gk = [c("mixP", "Gk%d" % i, TN, F32) for i in range(2)]
    self.wf = c("mixP", "wf", KT * 8, BF16).rearrange("p (k c) -> p k c", k=KT)
    self.ones8 = c("mixP", "ones8", TN, F32)
    self.dsk = c("mixP", "dsk", NG, F32)
    self.reg_off["mixT"] = self.reg_off["mixP"]
    self.reg_off["mixS"] = self.reg_off["mixP"]
    self.hM = c("mixT", "hM", KT * TN, F32).rearrange("p (k t) -> p k t", k=KT)
    self.nM = c("mixT", "nM", KT * TN, BF16).rearrange("p (k t) -> p k t", k=KT)
    self.Qa = c("mixT", "Qa", KT * TN, BF16).rearrange("p (k t) -> p k t", k=KT)
    self.uT = c("mixT", "uT", 4 * TN, BF16).rearrange("p (k t) -> p k t", k=4)
    self.U = c("mixT", "U", NG * NCH, BF16).rearrange("p (g c) -> p g c", g=NG)
    self.Ssb = c("mixT", "Ssb", 2 * 16 * NCH, F32).rearrange("p (g r c) -> p g r c", r=2, g=16)
    self.Ssb_gr = self.Ssb.rearrange("p g r c -> p (g r) c")
    self.yaT = self.U.rearrange("p g c -> p (g c)").rearrange("p (k t) -> p k t", k=4)
    self.Hbf = c("mixT", "Hbf", 2 * 16 * NCH, BF16).rearrange("p (r g c) -> p r g c", r=2, g=16)
    self.Ybf = c("mixT", "Ybf", NG * NCH, BF16).rearrange("p (g c) -> p g c", g=NG)
    self.ybtok = c("mixT", "ybtok", NB * 512, F32).rearrange("p (b f) -> p b f", b=NB)
    self.ybT = c("mixT", "ybT", 4 * TN, BF16).rearrange("p (k t) -> p k t", k=4)
    self.Pt = [c("mixT", "Pt%d" % i, TN, BF16) for i in range(5)]
    self.gat = [c("mixT", "gat%d" % i, TN, F32) for i in range(2)]
    self.t12 = [c("mixT", "t12_%d" % i, TN, F32) for i in range(2)]
    self.fl = c("mixT", "fl", TN, F32)
    self.gsp = c("mixT", "gsp", 3 * TN, BF16).rearrange("p (j t) -> p j t", j=3)
    self.gr = c("mixT", "gr", TN, F32)
    self.rec = c("mixT", "rec", 8, F32)
    self.m12 = [c("mixT", "m12_%d" % i, 2 * 16, F32).rearrange("p (r g) -> p r g", r=2) for i in range(2)]
    self.ytmp = [self.Ssb.rearrange("p g r c -> p (g r c)")[:, i * TN:(i + 1) * TN] for i in range(4)]
    self.s_lr = c("mixS", "lr", 16, F32)
    self.s_li = c("mixS", "li", 16, F32)
    self.s_dt = c("mixS", "dt", 16, F32)
    self.s_lrd = c("mixS", "lrd", 16, F32)
    self.s_lid = c("mixS", "lid", 16, F32)
    self.s_t = [c("mixS", "st%d" % i, 16 * 2 * 17, F32).rearrange("p (g a j) -> p g a j", g=16, a=2) for i in range(4)]
    self.s_Ere = c("mixS", "Ere", 16 * 2 * 17, F32).rearrange("p (g a j) -> p g a j", g=16, a=2)
    self.s_Eim = c("mixS", "Eim", 16 * 2 * 17, F32).rearrange("p (g a j) -> p g a j", g=16, a=2)
    self.s_sm = [c("mixS", "sm%d" % i, 16, F32) for i in range(6)]
    self.s_b = [c("mixS", "b%d" % i, 16 * 16, F32).rearrange("p (g c) -> p g c", g=16) for i in range(2)]
    self.s_Bb = [c("mixS", "Bb%d" % i, 16 * 16, F32).rearrange("p (g c) -> p g c", g=16) for i in range(2)]
    self.s_cn = [c("mixS", "cn%d" % i, 128, F32) for i in range(2)]
    self.s_c = [c("mixS", "c%d" % i, 16 * 16, F32).rearrange("p (g c) -> p g c", g=16) for i in range(2)]
    self.s_q = [c("mixS", "q%d" % i, 4 * 128, F32).rearrange("p (g t c) -> p g t c", g=4, t=8) for i in range(8)]
    self.s_w1t = c("mixS", "w1t", 128, F32)


def _mix_setup(self, l):
    S = self.S
    P = self.P
    bs = self.B("s5setup")
    bt = self.B("s5tab")
    V, Sc, G = "dve", "act", "pool"
    tt = lambda o, a, b, op, eng="dve": S.add(eng, lambda e: e.tensor_tensor(out=o, in0=a, in1=b, op=op), reads=[bs, self.B("mconst")], writes=[bs])
    ts = lambda o, a, s1, s2, op0, op1=None: S.add("dve", (lambda e: e.tensor_scalar(out=o, in0=a, scalar1=s1, scalar2=s2, op0=op0, op1=op1)) if op1 is not None else
                                                   (lambda e: e.tensor_scalar(out=o, in0=a, scalar1=s1, scalar2=None, op0=op0)), reads=[bs], writes=[bs])
    act = lambda o, a, f, **kw: S.add("act", lambda e: e.activation(out=o, in_=a, func=f, **kw), reads=[bs], writes=[bs])
    for hg in range(2):
        pr = slice(64 * hg, 64 * hg + 64)
        gs = slice(16 * hg, 16 * hg + 16)
        self.dma(self.s_lr[pr, :], P["ssm_a_re"][l][gs, :].rearrange("g p -> p g"), [], [bs], "s5ld", slow=True)
        self.dma(self.s_li[pr, :], P["ssm_a_im"][l][gs, :].rearrange("g p -> p g"), [], [bs], "s5ld", slow=True)
        self.dma(self.s_dt[pr, :], P["ssm_log_dt"][l][gs].partition_broadcast(64), [], [bs], "s5ld", slow=True)
        self.dma(self.s_b[0][pr, :, :], P["ssm_b_re"][l][gs].rearrange("g p c -> p g c"), [], [bs], "s5ld")
        self.dma(self.s_b[1][pr, :, :], P["ssm_b_im"][l][gs].rearrange("g p c -> p g c"), [], [bs], "s5ld")
    for t in range(8):
        self.dma(self.dsk[16 * t:16 * t + 16, :], P["ssm_d"][l].rearrange("(g c) -> c g", c=16), [], [bt], "s5ld", slow=True)
    for ri, nm in enumerate(("ssm_c_re", "ssm_c_im")):
        for half8 in range(2):
            cn = self.s_cn[ri]
            for hg in range(2):
                g0 = 16 * hg + 8 * half8
                self.dma(cn[:, 64 * hg:64 * hg + 64], P[nm][l][g0:g0 + 8].rearrange("g c p -> (g c) p"), [], [bs], "s5ld")
            pb, bpb = self.ps[5], self.B_ps[5]
            S.add("pe", lambda e, pb=pb, cn=cn: e.transpose(pb[:, 0:128], cn[:, :], self.ident[:, :]),
                  reads=[bs, self.B("ident")], writes=[bpb])
            self.copy("dve", self.s_c[ri][:, 8 * half8:8 * half8 + 8, :], pb[:, 0:128].rearrange("p (g c) -> p g c", g=8), [bpb], [bs])
    act(self.s_dt[:, :], self.s_dt[:, :], AF.Exp)
    tt(self.s_lrd[:, :], self.s_lr[:, :], self.s_dt[:, :], ALU.mult)
    tt(self.s_lid[:, :], self.s_li[:, :], self.s_dt[:, :], ALU.mult)
    jt = self.jtab[:, :, :].unsqueeze(1).broadcast_to([128, 16, 2, 17])
    bc3 = lambda a: a.unsqueeze(2).unsqueeze(3).broadcast_to([128, 16, 2, 17])
    t0, t1, t2, t3 = self.s_t
    tt(t0[:], bc3(self.s_lrd[:, :]), jt, ALU.mult)
    act(t0[:], t0[:], AF.Exp)
    tt(t1[:], bc3(self.s_lid[:, :]), jt, ALU.mult)
    for (dst, shift) in ((self.s_Eim, 0.0), (self.s_Ere, 1.5707963267948966)):
        if shift != 0.0:
            ts(t2[:], t1[:], shift, None, ALU.add)
            src = t2
        else:
            src = t1
        ts(t3[:], src[:], 1.0 / TWO_PI, MAGIC, ALU.mult, ALU.add)
        ts(t3[:], t3[:], -MAGIC, None, ALU.add)
        S.add("dve", lambda e, src=src: e.scalar_tensor_tensor(out=t3[:], in0=t3[:], scalar=-TWO_PI, in1=src[:], op0=ALU.mult, op1=ALU.add),
              reads=[bs], writes=[bs])
        ts(t3[:], t3[:], 3.14159, -3.14159, ALU.min, ALU.max)
        act(dst[:], t3[:], AF.Sin)
        tt(dst[:], dst[:], t0[:], ALU.mult)
    Ere, Eim = self.s_Ere, self.s_Eim
    sm = self.s_sm
    ts(sm[0][:, :], Ere[:, :, 0, 9], -1.0, None, ALU.add)
    tt(sm[1][:, :], self.s_lr[:, :], self.s_lr[:, :], ALU.mult)
    tt(sm[2][:, :], self.s_li[:, :], self.s_li[:, :], ALU.mult)
    tt(sm[1][:, :], sm[1][:, :], sm[2][:, :], ALU.add)
    S.add("dve", lambda e: e.reciprocal(out=sm[1][:, :], in_=sm[1][:, :]), reads=[bs], writes=[bs])
    tt(sm[2][:, :], sm[0][:, :], self.s_lr[:, :], ALU.mult)
    tt(sm[3][:, :], Eim[:, :, 0, 9], self.s_li[:, :], ALU.mult)
    tt(sm[2][:, :], sm[2][:, :], sm[3][:, :], ALU.add)
    tt(sm[2][:, :], sm[2][:, :], sm[1][:, :], ALU.mult)
    tt(sm[3][:, :], Eim[:, :, 0, 9], self.s_lr[:, :], ALU.mult)
    tt(sm[4][:, :], sm[0][:, :], self.s_li[:, :], ALU.mult)
    tt(sm[3][:, :], sm[3][:, :], sm[4][:, :], ALU.subtract)
    tt(sm[3][:, :], sm[3][:, :], sm[1][:, :], ALU.mult)
    bcc = lambda a: a.unsqueeze(2).broadcast_to([128, 16, 16])
    bre, bim = self.s_b
    Bbr, Bbi = self.s_Bb
    q = self.s_q
    tt(q[0].rearrange("p g t c -> p (g t c)")[:, 0:256].rearrange("p (g c) -> p g c", g=16), bcc(sm[2][:, :]), bre[:], ALU.mult)
    tmpA = q[0].rearrange("p g t c -> p (g t c)")[:, 0:256].rearrange("p (g c) -> p g c", g=16)
    tmpB = q[0].rearrange("p g t c -> p (g t c)")[:, 256:512].rearrange("p (g c) -> p g c", g=16)
    tt(tmpB, bcc(sm[3][:, :]), bim[:], ALU.mult)
    tt(Bbr[:], tmpA, tmpB, ALU.subtract)
    tt(tmpA, bcc(sm[2][:, :]), bim[:], ALU.mult)
    tt(tmpB, bcc(sm[3][:, :]), bre[:], ALU.mult)
    tt(Bbi[:], tmpA, tmpB, ALU.add)
    S.add("dve", lambda e: e.tensor_copy(out=self.A8[:, 0, 0, :], in_=Ere[:, :, 0, 16]), reads=[bs], writes=[bt])
    S.add("dve", lambda e: e.tensor_copy(out=self.A8[:, 0, 1, :], in_=Ere[:, :, 0, 16]), reads=[bs], writes=[bt])
    S.add("dve", lambda e: e.tensor_scalar(out=self.A8[:, 1, 0, :], in0=Eim[:, :, 0, 16], scalar1=-1.0, scalar2=None, op0=ALU.mult), reads=[bs], writes=[bt])
    S.add("dve", lambda e: e.tensor_copy(out=self.A8[:, 1, 1, :], in_=Eim[:, :, 0, 16]), reads=[bs], writes=[bt])
    cre, cim = self.s_c
    for qq in range(4):
        g4 = slice(4 * qq, 4 * qq + 4)
        shp = [128, 4, 8, 16]
        bE = lambda E, a, j0: E[:, g4, a, j0:j0 + 8].unsqueeze(3).broadcast_to(shp)
        bX = lambda X: X[:, g4, :].unsqueeze(2).broadcast_to(shp)
        W3r, W3i, CNr, CNi, CAr, CAi, ta, tb = q
        tt(ta[:], bE(Ere, 1, 1), bX(Bbr), ALU.mult); tt(tb[:], bE(Eim, 1, 1), bX(Bbi), ALU.mult); tt(W3r[:], ta[:], tb[:], ALU.subtract)
        tt(ta[:], bE(Ere, 1, 1), bX(Bbi), ALU.mult); tt(tb[:], bE(Eim, 1, 1), bX(Bbr), ALU.mult); tt(W3i[:], ta[:], tb[:], ALU.add)
        for (Xr, Xi, j0) in ((CNr, CNi, 1), (CAr, CAi, 9)):
            tt(ta[:], bE(Ere, 0, j0), bX(cre), ALU.mult); tt(tb[:], bE(Eim, 0, j0), bX(cim), ALU.mult); tt(Xr[:], ta[:], tb[:], ALU.subtract)
            tt(ta[:], bE(Eim, 0, j0), bX(cre), ALU.mult); tt(tb[:], bE(Ere, 0, j0), bX(cim), ALU.mult)
            S.add("dve", lambda e, Xi=Xi: e.scalar_tensor_tensor(out=Xi[:], in0=ta[:], scalar=-1.0, in1=tb[:], op0=ALU.mult, op1=ALU.subtract),
                  reads=[bs], writes=[bs])
        for hg in range(2):
            pr = slice(64 * hg, 64 * hg + 64)
            for gq in range(4):
                gp = 4 * qq + gq
                g = 16 * hg + gp
                f2 = lambda X: X[pr, gq, :, :].rearrange("p t c -> p (t c)")
                self.copy("act", self.W2[pr, gp, 0, :], f2(CAr), [bs], [bt])
                self.copy("act", self.W2[pr, gp, 1, :], f2(CAi), [bs], [bt])
                pb, bpb = self.ps[5 + g % 2], self.B_ps[5 + g % 2]
                self.mm(pb[:, 0:128], f2(W3r), f2(CNr), True, False, [bs], [bpb])
                self.mm(pb[:, 0:128], f2(W3i), f2(CNi), False, True, [bs], [bpb])
                S.add("dve", lambda e, pb=pb: e.tensor_tensor(out=self.s_w1t[:, :], in0=pb[:, 0:128], in1=self.mask8[:, :], op=ALU.mult),
                      reads=[bpb, self.B("mconst")], writes=[self.B("w1t")])
                S.add("dve", lambda e, g=g: e.scalar_tensor_tensor(out=self.W1[:, g, :], in0=self.ident[:, :], scalar=self.dsk[:, g:g + 1],
                                                                     in1=self.s_w1t[:, :], op0=ALU.mult, op1=ALU.add),
                      reads=[self.B("w1t"), bt, self.B("ident")], writes=[bt])
                pw, bpw = self.ps[7], self.B_ps[7]
                S.add("pe", lambda e, pw=pw, a=f2(W3r), pr=pr: e.transpose(pw[:, 0:64], a, self.ident[pr, pr]), reads=[bs, self.B("ident")], writes=[bpw])
                S.add("pe", lambda e, pw=pw, a=f2(W3i), pr=pr: e.transpose(pw[:, 64:128], a, self.ident[pr, pr]), reads=[bs, self.B("ident")], writes=[bpw])
                self.copy("act", self.W3[:, g, :, :], pw[:, 0:128].rearrange("p (r m) -> p r m", r=2), [bpw], [bt])


Kern.declare_mix = _declare_mix
Kern.prepass_mix = _prepass_mix
Kern.alloc_mix = _alloc_mix
Kern.mix_setup = _mix_setup
Kern.barrier = _barrier


def _load_w(self, src_ap, n, wname, full=False):
    wt, bw, wk = self.next_wslot()
    self.dma(wt[:, 0:n], src_ap, [self.B(wname)], [bw], wk)
    return (wt[:, :] if full else wt[:, 0:n]), bw


def _nextb(self, pool=(5, 6, 7, 0, 1, 3, 4)):
    i = pool[self.bank_rr % len(pool)]
    self.bank_rr += 1
    return self.ps[i], self.B_ps[i]


def _mix_tile(self, q, l, ti):
    S = self.S
    L, NT = self.L, self.NT
    wname = "W_mix_%d" % l
    t0 = ti * TN
    first = (ti == 0)
    bh, bn, bqa, but, bU = self.B("hM"), self.B("nM"), self.B("Qa"), self.B("uT"), self.B("U")
    bKc, bVc, bt = self.B("Kc"), self.B("Vc"), self.B("s5tab")
    bmc = self.B("mconst")
    hM, nM, Qa, uT, U = self.hM, self.nM, self.Qa, self.uT, self.U
    evr = [0]

    def evac(out, in_, reads, writes, scale=None):
        evr[0] += 1
        if evr[0] % 2 == 0:
            if scale is None:
                return S.add("act", lambda e: e.activation(out=out, in_=in_, func=AF.Copy), reads=reads, writes=writes)
            return S.add("act", lambda e: e.activation(out=out, in_=in_, func=AF.Copy, scale=scale), reads=reads, writes=writes)
        if scale is None:
            return S.add("dve", lambda e: e.tensor_copy(out=out, in_=in_), reads=reads, writes=writes)
        return S.add("dve", lambda e: e.tensor_scalar(out=out, in0=in_, scalar1=scale, scalar2=None, op0=ALU.mult), reads=reads, writes=writes)

    pre_q = [_load_w(self, self.wq_s[l][:, :, hh * 256:(hh + 1) * 256], KT * 256, wname, full=True) for hh in range(2)]
    self.dma(hM[:], self.hs[q, :, :, t0:t0 + TN], [self.B("hs%d" % q)], [bh], "hM")
    self.rms_stats(hM, bh, TN)
    self.norm_apply(hM, bh, 2 + 3 * l, nM, bn, TN)

    pf, bpf = _nextb(self)
    for kt in range(KT):
        self.mm(pf[0:8, 0:TN], self.wf[:, kt, :], nM[:, kt, :], kt == 0, kt == KT - 1, [bt, bn], [bpf])
    bfl, bG = self.B("fl"), self.B("Gk")
    fl, gsp, gr = self.fl, self.gsp, self.gr
    S.add("act", lambda e: e.activation(out=fl[0:8, :], in_=pf[0:8, 0:TN], func=AF.Exp, scale=-1.0, bias=self.bfneg[:, l:l + 1]),
          reads=[bpf, bmc], writes=[bfl])
    S.add("act", lambda e: e.activation(out=fl[0:8, :], in_=fl[0:8, :], func=AF.Ln, bias=1.0), reads=[bfl], writes=[bfl])
    Gc, Gp = self.Gk[ti % 2], self.Gk[(ti + 1) % 2]
    if first:
        S.add("dve", lambda e: e.tensor_tensor_scan(out=Gc[0:8, :], data0=self.ones8[0:8, :], data1=fl[0:8, :], initial=0.0,
                                                    op0=ALU.mult, op1=ALU.add), reads=[bfl, bt], writes=[bG])
    else:
        S.add("dve", lambda e: e.tensor_tensor_scan(out=Gc[0:8, :], data0=self.ones8[0:8, :], data1=fl[0:8, :],
                                                    initial=Gp[0:8, TN - 1:TN], op0=ALU.mult, op1=ALU.add), reads=[bfl, bG, bt], writes=[bG])
    bgs = self.B("gsp")
    S.add("dve", lambda e: e.tensor_copy(out=gsp[0:8, 0, :], in_=Gc[0:8, :]), reads=[bG], writes=[bgs])
    S.add("dve", lambda e: e.tensor_tensor(out=gr[0:8, :], in0=Gc[0:8, :], in1=gsp[0:8, 0, :], op=ALU.subtract), reads=[bG, bgs], writes=[bfl])
    S.add("dve", lambda e: e.tensor_copy(out=gsp[0:8, 1, :], in_=gr[0:8, :]), reads=[bfl], writes=[bgs])
    S.add("dve", lambda e: e.tensor_tensor(out=gr[0:8, :], in0=gr[0:8, :], in1=gsp[0:8, 1, :], op=ALU.subtract), reads=[bfl, bgs], writes=[bfl])
    S.add("dve", lambda e: e.tensor_copy(out=gsp[0:8, 2, :], in_=gr[0:8, :]), reads=[bfl], writes=[bgs])

    one = self.one
    for which in range(2):
        wsrc = (self.wq_s if which == 0 else self.wk_s)[l]
        E = self.et[:, :, :, 4:132] if which == 0 else self.et[:, :, :, 0:128]
        onerow = one[0:1, 0:128] if which == 0 else one[0:1, 128:256]
        for hh in range(4):
            if which == 0 and hh < 2:
                wv_, bw = pre_q[hh]
            else:
                wv_, bw = _load_w(self, wsrc[:, :, hh * 256:(hh + 1) * 256], KT * 256, wname, full=True)
            for h4 in range(2):
                h = hh * 2 + h4
                pb, bpb = _nextb(self)
                for kt in range(KT):
                    c0 = kt * 256 + h4 * 128
                    self.mm(pb[:, 0:TN], wv_[:, c0:c0 + 128], nM[:, kt, :], kt == 0, False, [bw, bn], [bpb])
                for j in range(3):
                    self.mm(pb[:, 0:TN], E[0:8, j, h, :], gsp[0:8, j, :], False, False, [bmc, bgs], [bpb])
                self.mm(pb[:, 0:TN], onerow, one[0:1, 256:256 + TN], False, True, [bmc], [bpb])
                if which == 0:
                    evac(Qa[0:71, h, :], pb[0:71, 0:TN], [bpb], [bqa], scale=0.125)
                else:
                    evac(self.Kc[0:71, h, t0:t0 + TN], pb[0:71, 0:TN], [bpb], [bKc])
    wv_, bw = _load_w(self, self.wv_s[l].rearrange("p k c -> p (k c)"), KT * 512, wname)
    wv_ = wv_.rearrange("p (k c) -> p k c", k=KT)
    for jb, (s0, nt) in enumerate(sub128(TN)):
        pb, bpb = _nextb(self)
        for kt in range(KT):
            self.mm(pb[0:nt, 0:512], nM[:, kt, s0:s0 + nt], wv_[:, kt, :], kt == 0, kt == KT - 1, [bw, bn], [bpb])
        evac(self.Vc[0:nt, ti * NB + jb, :, 0:64], pb[0:nt, 0:512].rearrange("p (h d) -> p h d", h=HEADS), [bpb], [bVc])
    wv_, bw = _load_w(self, self.wu_s[l].rearrange("p k c -> p (k c)"), KT * 512, wname)
    wv_ = wv_.rearrange("p (k c) -> p k c", k=KT)
    for j in range(4):
        pb, bpb = _nextb(self)
        for kt in range(KT):
            self.mm(pb[:, 0:TN], wv_[:, kt, j * 128:(j + 1) * 128], nM[:, kt, :], kt == 0, kt == KT - 1, [bw, bn], [bpb])
        evac(uT[:, j, :], pb[:, 0:TN], [bpb], [but])

    grp_banks = [(0, 11), (11, 22), (22, 32)]
    for bi, (ga, gb) in enumerate(grp_banks):
        pb, bpb = self.ps[5 + bi], self.B_ps[5 + bi]
        for g in range(ga, gb):
            j, gl = g // 8, g % 8
            for s in range(8):
                self.mm(pb[:, (g - ga) * NCH:(g - ga + 1) * NCH], self.sel[:, gl * 8 + s, :], uT[:, j, s:TN:8],
                        s == 0, s == 7, [bmc, but], [bpb])
        evac(U[:, ga:gb, :], pb[:, 0:(gb - ga) * NCH].rearrange("p (g c) -> p g c", c=NCH), [bpb], [bU])
    bS, bH = self.B("Ssb"), self.B("Hbf")
    Ssb, Hbf = self.Ssb, self.Hbf
    blk_banks = [(0, 11), (11, 22), (22, 32)]
    for bi, (ba, bb) in enumerate(blk_banks):
        pb, bpb = self.ps[5 + bi], self.B_ps[5 + bi]
        for hg in range(2):
            pr = slice(64 * hg, 64 * hg + 64)
            for blk in range(ba, bb):
                gp, ri = blk // 2, blk % 2
                g = 16 * hg + gp
                self.mm(pb[pr, (blk - ba) * NCH:(blk - ba + 1) * NCH], self.W3[:, g, ri, :], U[:, g, :], True, True, [bt, bU], [bpb])
        evac(self.Ssb_gr[:, ba:bb, :], pb[:, 0:(bb - ba) * NCH].rearrange("p (b c) -> p b c", c=NCH), [bpb], [bS])
    bZ = self.B("Z")
    A8 = self.A8
    if first:
        S.add("pool", lambda e: e.memset(self.Z[0][:], 0.0), writes=[bZ])
    zi = self.zi
    for c in range(NCH):
        Zc, Zn = self.Z[zi % 2], self.Z[(zi + 1) % 2]
        zi += 1
        m1, m2 = self.m12
        bm = self.B("m12")
        S.add("pool", lambda e, Zc=Zc, c=c: e.tensor_copy(out=Hbf[:, :, :, c], in_=Zc[:, 0:2, :]), reads=[bZ], writes=[bH], chain=True)
        S.add("pool", lambda e, Zc=Zc: e.tensor_tensor(out=m1[:], in0=A8[:, 0, :, :], in1=Zc[:, 0:2, :], op=ALU.mult), reads=[bZ, bt], writes=[bm], chain=True)
        S.add("pool", lambda e, Zc=Zc: e.tensor_tensor(out=m2[:], in0=A8[:, 1, :, :], in1=Zc[:, 1:3, :], op=ALU.mult), reads=[bZ, bt], writes=[bm], chain=True)
        S.add("pool", lambda e: e.tensor_tensor(out=m1[:], in0=m1[:], in1=m2[:], op=ALU.add), reads=[bm], writes=[bm], chain=True)
        S.add("pool", lambda e, Zn=Zn, c=c: e.tensor_tensor(out=Zn[:, 0:2, :], in0=m1[:], in1=Ssb[:, :, :, c].rearrange("p g r -> p r g"), op=ALU.add), reads=[bm, bS], writes=[bZ], chain=True)
        S.add("pool", lambda e, Zn=Zn: e.tensor_copy(out=Zn[:, 2, :], in_=Zn[:, 0, :]), reads=[bZ], writes=[bZ], chain=True)
    self.zi = zi
    ybtok, ybT = self.ybtok, self.ybT
    bybt, bybT = self.B("ybtok"), self.B("ybT")
    qsubs = sub128(TN)
    nkb = (ti + 1) * NB
    pti = 0
    for h in range(HEADS):
        ob0 = 2 if h % 2 == 0 else 5
        Ob = [(self.ps[ob0 + i], self.B_ps[ob0 + i]) for i in range(3)]
        for kb in range(nkb):
            kti, kj = kb // NB, kb % NB
            ks, nk = kti * TN + kj * 128, (128 if kj < 2 else TN - 256)
            diag = (kti == ti)
            pS, bpS = self.ps[kb % 2], self.B_ps[kb % 2]
            mk = nk
            self.mm(pS[0:mk, 0:TN], self.Kc[0:71, h, ks:ks + mk], Qa[0:71, h, :], True, not diag, [bKc, bqa], [bpS])
            if diag:
                self.mm(pS[0:mk, 0:TN], self.identbf[0:nk, 0:mk], self.maskb[0:nk, kj, :], False, True, [bmc], [bpS])
            Pt, bPt = self.Pt[pti % 5], self.B("Pt%d" % (pti % 5))
            pti += 1
            S.add("act", lambda e, Pt=Pt, pS=pS, nk=nk: e.activation(out=Pt[0:nk, :], in_=pS[0:nk, 0:TN], func=AF.Exp), reads=[bpS], writes=[bPt])
            for sq_, (qs, nq) in enumerate(qsubs):
                if diag and sq_ < kj:
                    continue
                lastkb = nkb - 1 if True else 0
                is_last = diag and kj == sq_
                pO, bpO = Ob[sq_]
                self.mm(pO[0:nq, 0:65], Pt[0:nk, qs:qs + nq], self.Vc[0:nk, kb, h, :], kb == 0, is_last, [bPt, bVc], [bpO])
        for sq_, (qs, nq) in enumerate(qsubs):
            pO, bpO = Ob[sq_]
            brec = self.B("rec")
            S.add("dve", lambda e, pO=pO, nq=nq: e.reciprocal(out=self.rec[0:nq, 0:1], in_=pO[0:nq, 64:65]), reads=[bpO], writes=[brec])
            S.add("dve", lambda e, pO=pO, nq=nq, sq_=sq_, h=h: e.tensor_scalar(out=ybtok[0:nq, sq_, h * 64:(h + 1) * 64], in0=pO[0:nq, 0:64],
                                                                             scalar1=self.rec[0:nq, 0:1], scalar2=None, op0=ALU.mult),
                  reads=[bpO, brec], writes=[bybt])
    for sq_, (qs, nq) in enumerate(qsubs):
        pb, bpb = _nextb(self)
        for kt in range(4):
            S.add("pe", lambda e, pb=pb, kt=kt, nq=nq, sq_=sq_: e.transpose(pb[:, kt * 128:kt * 128 + nq], ybtok[0:nq, sq_, kt * 128:(kt + 1) * 128],
                                                                      self.ident[0:nq, 0:nq]), reads=[bybt, self.B("ident")], writes=[bpb])
        evac(ybT[:, :, qs:qs + nq], pb[:, 0:512].rearrange("p (k t) -> p k t", k=4)[:, :, 0:nq], [bpb], [bybT])

    bY = self.B("Ybf")
    Ybf = self.Ybf
    for bi, (ga, gb) in enumerate(grp_banks):
        pb, bpb = self.ps[5 + bi], self.B_ps[5 + bi]
        for g in range(ga, gb):
            hg, gp = g // 16, g % 16
            pr = slice(64 * hg, 64 * hg + 64)
            o = pb[:, (g - ga) * NCH:(g - ga + 1) * NCH]
            self.mm(o, self.W1[:, g, :], U[:, g, :], True, False, [bt, bU], [bpb])
            self.mm(o, self.W2[pr, gp, 0, :], Hbf[pr, 0, gp, :], False, False, [bt, bH], [bpb])
            self.mm(o, self.W2[pr, gp, 1, :], Hbf[pr, 1, gp, :], False, True, [bt, bH], [bpb])
        evac(Ybf[:, ga:gb, :], pb[:, 0:(gb - ga) * NCH].rearrange("p (g c) -> p g c", c=NCH), [bpb], [bY])
    y0, y1, y2, y3 = self.ytmp
    byt = self.B("Ssb")
    gT = uT
    for j in range(4):
        pb, bpb = _nextb(self)
        for t in range(8):
            for gl in range(8):
                self.mm(pb[:, t * NCH:(t + 1) * NCH], self.sel[:, t * 8 + gl, :], Ybf[:, 8 * j + gl, :], gl == 0, gl == 7, [bmc, bY], [bpb])
        S.add("dve", lambda e, pb=pb: e.tensor_copy(out=y0.rearrange("p (c t) -> p t c", t=8), in_=pb[:, 0:TN].rearrange("p (t c) -> p t c", t=8)),
              reads=[bpb], writes=[byt])
        S.add("act", lambda e: e.activation(out=y1, in_=y0, func=AF.Square), reads=[byt], writes=[byt])
        S.add("dve", lambda e: e.tensor_scalar(out=y1, in0=y1, scalar1=0.044715, scalar2=1.0, op0=ALU.mult, op1=ALU.add), reads=[byt], writes=[byt])
        S.add("dve", lambda e: e.tensor_tensor(out=y1, in0=y1, in1=y0, op=ALU.mult), reads=[byt], writes=[byt])
        S.add("act", lambda e: e.activation(out=y2, in_=y1, func=AF.Sigmoid, scale=1.5957691216057308), reads=[byt], writes=[byt])
        S.add("dve", lambda e, j=j: e.tensor_tensor(out=gT[:, j, :], in0=y0, in1=y2, op=ALU.mult), reads=[byt], writes=[but])
    yaT = self.yaT
    w1_, bw1 = _load_w(self, self.wglu_s[l, 0].rearrange("p k c -> p (k c)"), 4 * 512, wname)
    w2_, bw2 = _load_w(self, self.wglu_s[l, 1].rearrange("p k c -> p (k c)"), 4 * 512, wname)
    w1_ = w1_.rearrange("p (k c) -> p k c", k=4)
    w2_ = w2_.rearrange("p (k c) -> p k c", k=4)
    for m in range(4):
        pa, bpa = _nextb(self)
        pb2, bpb2 = _nextb(self)
        for kt in range(4):
            self.mm(pa[:, 0:TN], w1_[:, kt, m * 128:(m + 1) * 128], gT[:, kt, :], kt == 0, kt == 3, [bw1, but], [bpa])
        for kt in range(4):
            self.mm(pb2[:, 0:TN], w2_[:, kt, m * 128:(m + 1) * 128], gT[:, kt, :], kt == 0, kt == 3, [bw2, but], [bpb2])
        S.add("act", lambda e, pb2=pb2: e.activation(out=y3, in_=pb2[:, 0:TN], func=AF.Sigmoid), reads=[bpb2], writes=[byt])
        S.add("dve", lambda e, pa=pa, m=m: e.tensor_tensor(out=yaT[:, m, :], in0=pa[:, 0:TN], in1=y3, op=ALU.mult), reads=[bpa, byt], writes=[bU])

    mg = Qa
    pool5 = (5, 6, 7, 0, 1, 2, 3, 4)
    for m in range(8):
        wm_, bwm = _load_w(self, self.wmrg_s[l, m], 3072, wname)
        wga_ = wm_[:, 0:1024].rearrange("p (k c) -> p k c", k=KT)
        wgb_ = wm_[:, 1024:2048].rearrange("p (k c) -> p k c", k=KT)
        wa_ = wm_[:, 2048:2560].rearrange("p (k c) -> p k c", k=4)
        wb_ = wm_[:, 2560:3072].rearrange("p (k c) -> p k c", k=4)
        bwga = bwgb = bwa = bwb = bwm
        if True:
            cs = slice(0, 128)
            pga, bpga = _nextb(self, pool5)
            for kt in range(KT):
                self.mm(pga[:, 0:TN], wga_[:, kt, cs], nM[:, kt, :], kt == 0, kt == KT - 1, [bwga, bn], [bpga])
            pA, bpA = _nextb(self, pool5)
            for kt in range(4):
                self.mm(pA[:, 0:TN], wa_[:, kt, cs], yaT[:, kt, :], kt == 0, kt == 3, [bwa, bU], [bpA])
            pgb, bpgb = _nextb(self, pool5)
            for kt in range(KT):
                self.mm(pgb[:, 0:TN], wgb_[:, kt, cs], nM[:, kt, :], kt == 0, kt == KT - 1, [bwgb, bn], [bpgb])
            pB, bpB = _nextb(self, pool5)
            for kt in range(4):
                self.mm(pB[:, 0:TN], wb_[:, kt, cs], ybT[:, kt, :], kt == 0, kt == 3, [bwb, bybT], [bpB])
            g0, g1 = self.gat
            bg0, bg1 = self.B("gat0"), self.B("gat1")
            t1, t2 = self.t12
            b1, b2 = self.B("t12_0"), self.B("t12_1")
            S.add("act", lambda e, pga=pga, m=m: e.activation(out=g0, in_=pga[:, 0:TN], func=AF.Sigmoid, bias=self.bgate[:, l, m:m + 1]),
                  reads=[bpga, bmc], writes=[bg0])
            S.add("act", lambda e, pgb=pgb, m=m: e.activation(out=g1, in_=pgb[:, 0:TN], func=AF.Sigmoid, bias=self.bgate[:, l, 8 + m:9 + m]),
                  reads=[bpgb, bmc], writes=[bg1])
            S.add("dve", lambda e, pA=pA: e.tensor_tensor(out=t1, in0=g0, in1=pA[:, 0:TN], op=ALU.mult), reads=[bg0, bpA], writes=[b1])
            S.add("dve", lambda e, pB=pB: e.tensor_tensor(out=t2, in0=g1, in1=pB[:, 0:TN], op=ALU.mult), reads=[bg1, bpB], writes=[b2])
            S.add("dve", lambda e, m=m: e.tensor_tensor(out=mg[:, m, :], in0=t1, in1=t2, op=ALU.add), reads=[b1, b2], writes=[bqa])
    for half in range(2):
        wo_, bwo = _load_w(self, self.wo_s[l, half].rearrange("p k c -> p (k c)"), KT * 512, wname)
        wo_ = wo_.rearrange("p (k c) -> p k c", k=KT)
        for mm_ in range(4):
            o = half * 4 + mm_
            po, bpo = _nextb(self, pool5)
            for kt in range(KT):
                self.mm(po[:, 0:TN], wo_[:, kt, mm_ * 128:(mm_ + 1) * 128], mg[:, kt, :], kt == 0, kt == KT - 1, [bwo, bqa], [bpo])
            S.add("dve", lambda e, po=po, o=o: e.tensor_tensor(out=hM[:, o, :], in0=hM[:, o, :], in1=po[:, 0:TN], op=ALU.add),
                  reads=[bpo, bh], writes=[bh])
    self.dma(self.hs[q, :, :, t0:t0 + TN], hM[:], [bh], [self.B("hs%d" % q)], "hMst", q="pool")


def _mix_begin(self, l):
    S = self.S
    bt = self.B("s5tab")
    o4 = self.s5_off // 4
    tabs = self.arena[:, o4:o4 + (3 * 8192 + 256) // 4]
    self.ran_setup = l not in self.s5_done
    if self.ran_setup:
        self.mix_setup(l)
        self.s5_done.add(l)
        self.dma(self.s5c_s[l], tabs, [bt], [self.B("s5c%d" % l)], "s5c")
    else:
        self.dma(tabs, self.s5c_s[l], [self.B("s5c%d" % l)], [bt], "s5c")
    S.add("pool", lambda e: e.memset(self.ones8[:, :], 1.0), writes=[bt])
    S.add("pool", lambda e: e.memset(self.Vc[:, :, :, 64:65], 1.0), writes=[self.B("Vc")])
    self.dma(self.wf[:].rearrange("p k c -> p (k c)"), self.wf_s[l].rearrange("p k c -> p (k c)"), [self.B("W_mix_%d" % l)], [bt], "wfld")
    self.zi = 0


Kern.mix_begin = _mix_begin
Kern.mix_tile = _mix_tile


def build(n_seq=2, x_len=2048, depth=2, mix=True, ffn=True):
    k = Kern(n_seq, x_len, depth)
    k.declare()
    if mix:
        k.declare_mix()
    k.alloc_ffn()
    if mix:
        k.alloc_mix()
    for l in range(depth):
        if ffn:
            for f in range(2):
                k.prepass_ffn(l, f)
        if mix:
            k.prepass_mix(l)
    k.barrier()
    k.stage_in()
    for q in range(n_seq):
        for l in range(depth):
            if ffn:
                k.stage_ffn(q, l, 0)
            if mix:
                k.barrier()
                k.mix_begin(l)
                if k.ran_setup:
                    k.barrier()
                for ti in range(k.NT):
                    k.mix_tile(q, l, ti)
                k.barrier()
            if ffn:
                k.stage_ffn(q, l, 1)
    k.stage_final()
    return k.finish()


def host_consts():
    bf = ml_dtypes.bfloat16
    c = {}
    c["ident"] = np.eye(128, dtype=np.float32)
    sel = np.zeros((128, 64, 128), np.float32)
    for a in range(8):
        for b in range(8):
            for i in range(16):
                sel[16 * a + i, a * 8 + b, 16 * b + i] = 1.0
    c["c_sel"] = sel.reshape(128, 64 * 128).astype(bf)
    m8 = np.zeros((128, 128), np.float32)
    for s in range(8):
        for t in range(s, 8):
            m8[16 * s:16 * s + 16, 16 * t:16 * t + 16] = 1.0
    c["c_mask8"] = m8
    mb = np.zeros((128, NB, TN), np.float32)
    r = np.arange(128)[:, None]
    ql = np.arange(TN)[None, :]
    for j in range(NB):
        mb[:, j, :] = np.where(ql - r - 128 * j >= 0, 0.0, -30000.0)
    c["c_maskb"] = mb.reshape(128, NB * TN).astype(bf)
    et = np.zeros((8, 3, 8, 132), np.float32)
    for h in range(8):
        for j in range(3):
            et[h, j, h, 68 + j] = 1.0
    c["c_et"] = et.reshape(8, -1).astype(bf)
    one = np.zeros((1, 256 + TN), np.float32)
    one[0, 68:71] = 8.0
    one[0, 128 + 64:128 + 67] = -8.0
    one[0, 256:] = 1.0
    c["c_one"] = one.astype(bf)
    jt = np.zeros((128, 2, 17), np.float32)
    jt[:, 0, :] = np.arange(17) - 8
    jt[:, 1, :] = 8 - np.arange(17)
    c["c_jtab"] = jt.reshape(128, 34)
    c["c_identbf"] = np.eye(128, dtype=np.float32).astype(bf)
    return c


PARAM_NAMES = ["g_ffn1", "w1_gate", "w1_up", "w1_down", "g_mix", "w_in", "b_gate", "b_f",
               "ssm_a_re", "ssm_a_im", "ssm_log_dt", "ssm_b_re", "ssm_b_im", "ssm_c_re", "ssm_c_im",
               "ssm_d", "w_glu", "w_br_a", "w_br_b", "w_o", "g_ffn2", "w2_gate", "w2_up", "w2_down"]

_NC_CACHE = {}


def kernel(**inputs):
    n_cores = 8
    x = np.ascontiguousarray(np.asarray(inputs["x"], dtype=np.float32))
    bsz, x_len, _ = x.shape
    n_seq = bsz // n_cores
    key = (n_seq, x_len)
    if key not in _NC_CACHE:
        _NC_CACHE[key] = build(n_seq=n_seq, x_len=x_len, depth=2)
    nc = _NC_CACHE[key]
    base = host_consts()
    base["meta"] = np.ascontiguousarray(np.asarray(inputs["meta"], np.float32))
    base["g_final"] = np.ascontiguousarray(np.asarray(inputs["g_final"], np.float32))
    for nm in PARAM_NAMES:
        base[nm] = np.ascontiguousarray(np.asarray(inputs[nm], np.float32))
    in_maps = []
    for c in range(n_cores):
        m = dict(base)
        m["x"] = x[c * n_seq:(c + 1) * n_seq]
        in_maps.append(m)
    res = run_bass_kernel_spmd(nc, in_maps, core_ids=list(range(n_cores)))
    return np.concatenate([np.asarray(r["out"], np.float32) for r in res.results], axis=0)
```
